# Optimizing a Trainium2 kernel written in Bass

```python
import math
import jax, jax.numpy as jnp
from jax import lax
import numpy as np

D_MODEL = 2048
BATCH = 2
SEQ = 8192
DEPTH = 4

BLOCK = 128
RET_HEADS = 4
RET_QK_DIM = 256
RET_V_DIM = 512
SG_GROUPS = 4
SG_GROUP_DIM = 256
ATT_HEADS = 8
ATT_HEAD_DIM = 128
DILATION_CONFIGS = ((128, 1), (512, 4), (2048, 16))
REL_BUCKETS = 32
REL_MAX_DIST = 2048
D_FF = 5632
ROPE_BASE = 10000.0
EPS = 1e-6
NEG_INF = -1e30

RET_QK_W = RET_HEADS * RET_QK_DIM
RET_V_W = RET_HEADS * RET_V_DIM
SG_W = SG_GROUPS * SG_GROUP_DIM
ATT_W = ATT_HEADS * ATT_HEAD_DIM
N_BRANCH = 3
IN_SPLITS = (RET_QK_W, RET_QK_W, RET_V_W, RET_V_W, SG_W, SG_W, ATT_W, ATT_W, ATT_W, N_BRANCH * D_MODEL)
D_IN = sum(IN_SPLITS)

kernel_name = 'hybrid_retention_sgu_dilated_attn_macaron'

f32 = jnp.float32


def _rmsnorm(x, g):
    xf = x.astype(f32)
    y = xf * lax.rsqrt(jnp.mean(xf * xf, axis=-1, keepdims=True) + EPS) * g.astype(f32)
    return y.astype(x.dtype)


def _layernorm(x, g, b):
    xf = x.astype(f32)
    mu = jnp.mean(xf, axis=-1, keepdims=True)
    var = jnp.mean(jnp.square(xf - mu), axis=-1, keepdims=True)
    y = (xf - mu) * lax.rsqrt(var + EPS) * g.astype(f32) + b.astype(f32)
    return y.astype(x.dtype)


def _swiglu(h, w_gate, w_up, w_down):
    return (jax.nn.silu(h @ w_gate) * (h @ w_up)) @ w_down


def _rotary(t, positions):
    dh = t.shape[-1]
    inv = ROPE_BASE ** (-jnp.arange(0, dh, 2, dtype=f32) / dh)
    ang = positions.astype(f32)[:, None] * inv[None, :]
    cos, sin = jnp.cos(ang), jnp.sin(ang)
    t1, t2 = t[..., : dh // 2], t[..., dh // 2:]
    return jnp.concatenate([t1 * cos - t2 * sin, t1 * sin + t2 * cos], axis=-1)


def _retention(q, k, v):
    B, H, S, dk = q.shape
    dv = v.shape[-1]
    C = BLOCK
    nc = S // C
    log_g = jnp.log1p(-(2.0 ** (-5.0 - jnp.arange(H, dtype=f32))))
    idx = jnp.arange(C, dtype=f32)
    rel = idx[:, None] - idx[None, :]
    inner = jnp.where(rel[None] >= 0, jnp.exp(jnp.maximum(rel, 0.0)[None] * log_g[:, None, None]), 0.0)
    xi = jnp.exp((idx + 1.0)[None, :] * log_g[:, None])
    zeta = jnp.exp((C - 1.0 - idx)[None, :] * log_g[:, None])
    g_chunk = jnp.exp(C * log_g)
    k = k * (dk ** -0.5)

    def chunks(t):
        return jnp.moveaxis(t.reshape(B, H, nc, C, t.shape[-1]), 2, 0)

    def step(R, qkv):
        qc, kc, vc = qkv
        a = jnp.einsum('bhid,bhjd->bhij', qc, kc) * inner
        o = jnp.einsum('bhij,bhje->bhie', a, vc) + jnp.einsum('bhid,bhde->bhie', qc, R) * xi[..., None]
        R = R * g_chunk[:, None, None] + jnp.einsum('bhjd,bhje->bhde', kc * zeta[..., None], vc)
        return R, o

    R0 = jnp.zeros((B, H, dk, dv), f32)
    _, o = lax.scan(step, R0, (chunks(q), chunks(k), chunks(v)))
    return jnp.moveaxis(o, 0, 2).reshape(B, H, S, dv)


def _t5_bucket(dist):
    max_exact = REL_BUCKETS // 2
    d_f = jnp.maximum(dist, 1).astype(f32)
    large = max_exact + (jnp.log(d_f / max_exact) / math.log(REL_MAX_DIST / max_exact)
                         * (REL_BUCKETS - max_exact)).astype(jnp.int32)
    large = jnp.minimum(large, REL_BUCKETS - 1)
    return jnp.where(dist < max_exact, dist, large)


def _dilated_branch(q, k, v, bias_tab, window, dilation):
    B, H, S, hd = q.shape
    n_back = window // dilation
    blk = n_back
    span = dilation * blk
    L = -(-S // span) * span
    nb = L // span

    def to_stream(t):
        t = jnp.pad(t, ((0, 0), (0, 0), (0, L - S), (0, 0))).reshape(B, H, L // dilation, dilation, hd)
        return jnp.swapaxes(t, 2, 3).reshape(B, H, dilation, nb, blk, hd)

    def with_prev(t):
        prev = jnp.pad(t[:, :, :, :-1], ((0, 0), (0, 0), (0, 0), (1, 0), (0, 0), (0, 0)))
        return jnp.concatenate([prev, t], axis=4)

    def from_stream(t):
        t = t.reshape(B, H, dilation, L // dilation, t.shape[-1])
        return jnp.swapaxes(t, 2, 3).reshape(B, H, L, t.shape[-1])[:, :, :S]

    qs = to_stream(q)
    kc = with_prev(to_stream(k))
    vc = with_prev(to_stream(v))
    i = jnp.arange(blk)[:, None]
    j = jnp.arange(2 * blk)[None, :]
    steps = blk + i - j
    band = (steps >= 0) & (steps <= n_back)
    not_before_start = (jnp.arange(nb)[:, None, None] > 0) | (j[None] >= blk)
    mask = band[None] & not_before_start
    bucket = _t5_bucket(dilation * jnp.maximum(steps, 0))
    bias = jnp.transpose(bias_tab[bucket], (2, 0, 1))

    logits = jnp.einsum('bhrnqd,bhrnkd->bhrnqk', qs, kc) * (hd ** -0.5) + bias[:, None, None]
    logits = jnp.where(mask, logits, NEG_INF)
    m = jnp.max(logits, axis=-1, keepdims=True)
    p = jnp.exp(logits - m)
    den = jnp.sum(p, axis=-1, keepdims=True)
    o = jnp.einsum('bhrnqk,bhrnkd->bhrnqd', p, vc) / den
    lse = m + jnp.log(den)
    return from_stream(o), from_stream(lse)


def _mixer(h, w_in, b_gate, sg_ln_g, sg_ln_b, sg_w, sg_b, rel_bias,
           w_proj_ret, w_proj_sg, w_proj_att, w_out):
    B, S, _ = h.shape
    z = h @ w_in
    offs = [int(o) for o in np.cumsum(IN_SPLITS)[:-1]]
    rq, rk, rv, rg, su, sv, aq, ak, av, gl = jnp.split(z, offs, axis=-1)

    def heads(t, n):
        return jnp.swapaxes(t.reshape(B, S, n, -1), 1, 2).astype(f32)

    def merge_heads(t):
        return jnp.swapaxes(t, 1, 2).reshape(B, S, -1).astype(h.dtype)

    pos = jnp.arange(S)
    ret = _retention(_rotary(heads(rq, RET_HEADS), pos), _rotary(heads(rk, RET_HEADS), pos), heads(rv, RET_HEADS))
    ret = ret * lax.rsqrt(jnp.mean(ret * ret, axis=-1, keepdims=True) + EPS)
    y_ret = (jax.nn.silu(rg) * merge_heads(ret)) @ w_proj_ret

    su = jax.nn.gelu(su)
    sv = _layernorm(jax.nn.gelu(sv), sg_ln_g, sg_ln_b)
    vch = sv.reshape(B, S // BLOCK, BLOCK, SG_GROUPS, SG_GROUP_DIM)
    w_causal = sg_w * jnp.tril(jnp.ones((BLOCK, BLOCK), sg_w.dtype))
    s_mix = jnp.einsum('gts,bnsgc->bntgc', w_causal, vch) + sg_b.T[:, :, None]
    y_sg = (su * s_mix.reshape(B, S, SG_W)) @ w_proj_sg

    qa, ka, va = heads(aq, ATT_HEADS), heads(ak, ATT_HEADS), heads(av, ATT_HEADS)
    bias_tab = rel_bias.astype(f32)
    outs, lses = [], []
    for window, dil in DILATION_CONFIGS:
        o, l = _dilated_branch(qa, ka, va, bias_tab, window, dil)
        outs.append(o)
        lses.append(l)
    wts = jax.nn.softmax(jnp.stack(lses), axis=0)
    att = jnp.sum(wts * jnp.stack(outs), axis=0)
    y_att = merge_heads(att) @ w_proj_att

    gates = jax.nn.sigmoid((gl + b_gate).astype(f32)).astype(h.dtype).reshape(B, S, N_BRANCH, D_MODEL)
    merged = gates[:, :, 0] * y_ret + gates[:, :, 1] * y_sg + gates[:, :, 2] * y_att
    return merged @ w_out


def setup_inputs(seed: int = 0) -> dict:
    key = jax.random.key(seed)
    ks = jax.random.split(key, 24)

    def nrm(k, shape, scale):
        return jax.random.normal(k, shape, f32) * scale

    def gain(k, shape):
        return 1.0 + 0.02 * jax.random.normal(k, shape, f32)

    return {
        'x': nrm(ks[0], (BATCH, SEQ, D_MODEL), 1.0),
        'ffn1_norm': gain(ks[1], (DEPTH, D_MODEL)),
        'ffn1_w_gate': nrm(ks[2], (DEPTH, D_MODEL, D_FF), D_MODEL ** -0.5),
        'ffn1_w_up': nrm(ks[3], (DEPTH, D_MODEL, D_FF), D_MODEL ** -0.5),
        'ffn1_w_down': nrm(ks[4], (DEPTH, D_FF, D_MODEL), D_FF ** -0.5),
        'mix_norm': gain(ks[5], (DEPTH, D_MODEL)),
        'w_in': nrm(ks[6], (DEPTH, D_MODEL, D_IN), D_MODEL ** -0.5),
        'b_gate': nrm(ks[7], (DEPTH, N_BRANCH * D_MODEL), 0.02),
        'sg_ln_g': gain(ks[8], (DEPTH, SG_W)),
        'sg_ln_b': nrm(ks[9], (DEPTH, SG_W), 0.02),
        'sg_w': nrm(ks[10], (DEPTH, SG_GROUPS, BLOCK, BLOCK), BLOCK ** -0.5),
        'sg_b': nrm(ks[11], (DEPTH, SG_GROUPS, BLOCK), 0.02),
        'rel_bias': nrm(ks[12], (REL_BUCKETS, ATT_HEADS), 0.5),
        'w_proj_ret': nrm(ks[13], (DEPTH, RET_V_W, D_MODEL), RET_V_W ** -0.5),
        'w_proj_sg': nrm(ks[14], (DEPTH, SG_W, D_MODEL), SG_W ** -0.5),
        'w_proj_att': nrm(ks[15], (DEPTH, ATT_W, D_MODEL), ATT_W ** -0.5),
        'w_out': nrm(ks[16], (DEPTH, D_MODEL, D_MODEL), D_MODEL ** -0.5),
        'ffn2_norm': gain(ks[17], (DEPTH, D_MODEL)),
        'ffn2_w_gate': nrm(ks[18], (DEPTH, D_MODEL, D_FF), D_MODEL ** -0.5),
        'ffn2_w_up': nrm(ks[19], (DEPTH, D_MODEL, D_FF), D_MODEL ** -0.5),
        'ffn2_w_down': nrm(ks[20], (DEPTH, D_FF, D_MODEL), D_FF ** -0.5),
        'final_norm': gain(ks[21], (D_MODEL,)),
    }


def reference(x, ffn1_norm, ffn1_w_gate, ffn1_w_up, ffn1_w_down, mix_norm, w_in, b_gate,
              sg_ln_g, sg_ln_b, sg_w, sg_b, rel_bias, w_proj_ret, w_proj_sg, w_proj_att, w_out,
              ffn2_norm, ffn2_w_gate, ffn2_w_up, ffn2_w_down, final_norm):
    for l in range(DEPTH):
        x = x + 0.5 * _swiglu(_rmsnorm(x, ffn1_norm[l]), ffn1_w_gate[l], ffn1_w_up[l], ffn1_w_down[l])
        x = x + _mixer(_rmsnorm(x, mix_norm[l]), w_in[l], b_gate[l], sg_ln_g[l], sg_ln_b[l],
                       sg_w[l], sg_b[l], rel_bias, w_proj_ret[l], w_proj_sg[l], w_proj_att[l], w_out[l])
        x = x + 0.5 * _swiglu(_rmsnorm(x, ffn2_norm[l]), ffn2_w_gate[l], ffn2_w_up[l], ffn2_w_down[l])
    return _rmsnorm(x, final_norm)
```

```python
import math
import numpy as np
from contextlib import ExitStack
import concourse.bass as bass
import concourse.mybir as mybir
from concourse.bass_utils import run_bass_kernel_spmd

F32 = mybir.dt.float32
BF16 = mybir.dt.bfloat16
AF = mybir.ActivationFunctionType
ALU = mybir.AluOpType

D = 2048
DFF = 5632
DIN = 17408
TT = 512
EPS = 1e-6
NDS = 6
LDEPTH = 4
SEQ = 8192


class Sched:
    def __init__(s, nc, es):
        s.nc = nc
        s.E = {'pe': nc.tensor, 'act': nc.scalar, 'dve': nc.vector, 'pool': nc.gpsimd, 'sp': nc.sync}
        s.sem = {}
        s.cnt = {}
        for e in s.E:
            s.sem[e] = es.enter_context(nc.semaphore('s_' + e))
            s.cnt[e] = 0
        s.waited = {e: {} for e in s.E}
        s.hist = {}
        s.dq = {}
        s.dqi = {}
        for q in ('sp', 'act', 'pool'):
            s.dq[q] = [[es.enter_context(nc.semaphore('d_%s%d' % (q, i))), 0] for i in range(NDS)]
            s.dqi[q] = 0

    def _wait(s, e, toks):
        need = {}
        for t in toks:
            if t is None:
                continue
            sem, val = t
            k = id(sem)
            if e == 'pe' and sem is s.sem['pe']:
                continue
            if s.waited[e].get(k, 0) >= val:
                continue
            if k not in need or need[k][1] < val:
                need[k] = (sem, val)
        for k, (sem, val) in need.items():
            s.E[e].wait_ge(sem, val)
            s.waited[e][k] = val

    def _deps(s, reads, writes):
        toks = []
        for r in reads:
            h = s.hist.get(r)
            if h:
                toks.append(h[0])
        for w in writes:
            h = s.hist.get(w)
            if h:
                toks.append(h[0])
                toks.extend(h[1].values())
        return toks

    def _record(s, key, tok, reads, writes):
        for r in reads:
            h = s.hist.setdefault(r, [None, {}])
            h[1][key] = tok
        for w in writes:
            s.hist[w] = [tok, {}]

    def op(s, e, fn, reads=(), writes=(), inc=True):
        s._wait(e, s._deps(reads, writes))
        ins = fn()
        if inc:
            s.cnt[e] += 1
            ins.then_inc(s.sem[e], 1)
            tok = (s.sem[e], s.cnt[e])
        else:
            tok = (s.sem[e], s.cnt[e] + 1)
        s._record(e, tok, reads, writes)
        return tok

    def dma(s, q, out, in_, reads=(), writes=()):
        slot = s.dq[q][s.dqi[q] % NDS]
        s.dqi[q] += 1
        toks = s._deps(reads, writes)
        if slot[1] > 0:
            toks.append((slot[0], slot[1]))
        s._wait(q, toks)
        ins = s.E[q].dma_start(out=out, in_=in_)
        slot[1] += 16
        ins.then_inc(slot[0], 16)
        tok = (slot[0], slot[1])
        s._record(id(slot[0]), tok, reads, writes)
        return tok

    def barrier(s):
        toks = [(s.sem[e], s.cnt[e]) for e in s.E if s.cnt[e] > 0]
        for q in s.dq:
            for sl in s.dq[q]:
                if sl[1] > 0:
                    toks.append((sl[0], sl[1]))
        for e in s.E:
            s._wait(e, toks)
        s.hist.clear()


class Pool:
    def __init__(s, tiles, name):
        s.tiles = tiles
        s.name = name
        s.i = 0

    def get(s):
        k = s.i % len(s.tiles)
        s.i += 1
        return s.tiles[k], (s.name, k)


def build(T, L, dbg=False):
    NT = T // TT
    NCH = T // 128
    nc = bass.Bass("TRN2", target_bir_lowering=False)
    es = ExitStack()
    KIN = "ExternalInput"
    KSC = "ExternalOutput" if dbg else "Internal"

    def din(name, shape, dt=F32):
        return nc.dram_tensor(name, list(shape), dt, kind=KIN).ap()

    def dsc(name, shape, dt):
        return nc.dram_tensor(name, list(shape), dt, kind=KSC).ap()

    x_in = din("x", [T, D])
    W = {}
    wspec = [("wg1", D, DFF), ("wu1", D, DFF), ("wd1", DFF, D), ("win", D, DIN), ("wpr", 2048, D),
             ("wps", 1024, D), ("wpa", 1024, D), ("wo", D, D), ("wg2", D, DFF), ("wu2", D, DFF), ("wd2", DFF, D)]
    WB = {}
    for nm, K, N in wspec:
        W[nm] = din(nm, [L, K, N])
        cb = 128 if nm in ("wd1", "wd2") else 256
        WB[nm] = ([nc.dram_tensor("%sb%d" % (nm, l_), [N // cb, 128, K // 128, cb], BF16, kind="Internal").ap() for l_ in range(L)], K // 128, cb)
    NCOLS = (3 * L + 1) * 16 + L * 48
    cols_in = din("cols", [128, NCOLS])
    lng_in = din("lng", [L, 1024])
    lnb_in = din("lnb", [L, 1024])
    sgwT_in = din("sgwT", [L, 4, 128, 128])
    sgb_in = din("sgb", [L, 4, 128])
    bm_in = din("bm", [3, 8, 2, 128, 128])
    rotq_in = din("rotq", [4, 2, 128, T])
    rotk_in = din("rotk", [4, 2, 128, T])
    cmask_in = din("cmask", [128, 128])
    ident_in = din("ident", [128, 128])
    y_out = nc.dram_tensor("y", [T, D], F32, kind="ExternalOutput").ap()

    xT = dsc("xT", [D, T], F32)
    qrT = dsc("qrT", [1024, T], BF16)
    krT = dsc("krT", [1024, T], BF16)
    rvT = dsc("rvT", [2048, T], BF16)
    rgT = dsc("rgT", [2048, T], BF16)
    suT = dsc("suT", [1024, T], BF16)
    svT = dsc("svT", [1024, T], F32)
    aqT = dsc("aqT", [1024, T], BF16)
    akT = dsc("akT", [1024, T], BF16)
    avT = dsc("avT", [1024, T], BF16)
    gT = dsc("gT", [6144, T], BF16)
    retoT = dsc("retoT", [2048, T], BF16)
    sgoT = dsc("sgoT", [1024, T], BF16)
    attoT = dsc("attoT", [1024, T], BF16)

    S = Sched(nc, es)
    E = S.E

    uid = [0]

    def sb(name, shape, dt, stack=None):
        uid[0] += 1
        return (stack or es).enter_context(nc.sbuf_tensor("sb%d_%s" % (uid[0], name), list(shape), dt))

    PSB = [es.enter_context(nc.psum_tensor("psb%d" % i, [128, 512], F32)) for i in range(8)]
    ps_pool = Pool(PSB, "psb")

    cols = sb("cols", [128, NCOLS], F32)
    ident_f = sb("identf", [128, 128], F32)
    ident_b = sb("identb", [128, 128], BF16)
    ones_b = sb("onesb", [128, 128], BF16)
    cmask = sb("cmaskf", [128, 128], F32)
    epsc = sb("epsc", [128, 1], F32)
    S.dma('sp', cols[:], cols_in, writes=['cols'])
    S.dma('sp', ident_f[:], ident_in, writes=['identf'])
    S.dma('sp', cmask[:], cmask_in, writes=['cmask'])
    S.op('pool', lambda: E['pool'].memset(ones_b[:], 1.0), writes=['onesb'])
    S.op('pool', lambda: E['pool'].memset(epsc[:], EPS), writes=['epsc'])
    S.op('dve', lambda: E['dve'].tensor_copy(out=ident_b[:], in_=ident_f[:]), reads=['identf'], writes=['identb'])

    for l in range(L):
        for nm, K, N in wspec:
            wb, KC, cb = WB[nm]
            for c0 in range(N // cb):
                src = W[nm][l][:, c0 * cb:(c0 + 1) * cb].rearrange("(kc p) c -> p kc c", p=128)
                S.dma('pool', wb[l][c0], src)
    S.barrier()

    def colv(idx):
        return cols[:, idx:idx + 1]

    def c_norm(kind, l):
        return (kind * L + l) * 16

    C_FINAL = 3 * L * 16
    C_BG = (3 * L + 1) * 16

    xTv = xT.rearrange("(m p) t -> p m t", p=128)

    def gemm_blocks(st, wname, l, nblocks, rhs_fn, KC, epi, wpool, b0=0):
        wb, KCw, cb = WB[wname]
        assert KCw == KC
        nsub = cb // 128
        pend = []
        PF = 2
        blocks = list(range(b0, b0 + nblocks))
        loaded = {}

        def load(bi):
            wt, wk = wpool.get()
            S.dma('sp', wt[:, 0:KC, 0:cb], wb[l][bi], writes=[wk])
            loaded[bi] = (wt, wk)
        for bi in blocks[:PF]:
            load(bi)
        for ii, bi in enumerate(blocks):
            if ii + PF < len(blocks):
                load(blocks[ii + PF])
            wt, wk = loaded.pop(bi)
            pss = []
            for sub in range(nsub):
                ps, pk = ps_pool.get()
                for kc in range(KC):
                    rap, rk = rhs_fn(kc)
                    S.op('pe', lambda: E['pe'].matmul(ps[:], lhsT=wt[:, kc, sub * 128:(sub + 1) * 128], rhs=rap,
                                                      start=(kc == 0), stop=(kc == KC - 1)),
                         reads=[wk, rk], writes=[pk], inc=(kc == KC - 1))
                pss.append((ps, pk))
            epi(bi, pss)

    def rms_norm(st, t, gcol0, out_fn):
        ps, pk = ps_pool.get()
        for m in range(16):
            xs, xk = st['xs'].get()
            S.dma('sp', xs[:], xTv[:, m, t * TT:(t + 1) * TT], writes=[xk])
            sq, sk = st['sq'].get()
            S.op('act', lambda: E['act'].activation(out=sq[:], in_=xs[:], func=AF.Square), reads=[xk], writes=[sk])
            S.op('pe', lambda: E['pe'].matmul(ps[:], lhsT=ones_b[:], rhs=sq[:], start=(m == 0), stop=(m == 15)),
                 reads=['onesb', sk], writes=[pk], inc=True)
        rs = st['rstd']
        S.op('act', lambda: E['act'].activation(out=rs[:], in_=ps[:], func=AF.Sqrt, bias=epsc[:], scale=1.0 / D),
             reads=[pk, 'epsc'], writes=['rstd'])
        S.op('dve', lambda: E['dve'].reciprocal(out=rs[:], in_=rs[:]), reads=['rstd'], writes=['rstd'])
        for m in range(16):
            xs, xk = st['xs'].get()
            S.dma('sp', xs[:], xTv[:, m, t * TT:(t + 1) * TT], writes=[xk])
            out_fn(m, xs, xk, rs)

    def norm_to_hT(st, t, gcol0):
        hT = st['hT']

        def o(m, xs, xk, rs):
            S.op('dve', lambda: E['dve'].scalar_tensor_tensor(out=hT[:, m, :], in0=xs[:], scalar=colv(gcol0 + m),
                                                              in1=rs[:], op0=ALU.mult, op1=ALU.mult),
                 reads=[xk, 'rstd', 'cols'], writes=[('hT', m)])
        rms_norm(st, t, gcol0, o)

    def resid_epi(st, t, scale):
        def epi(bi, pss):
            for sub, (ps, pk) in enumerate(pss):
                m = bi * len(pss) + sub
                xs, xk = st['xs'].get()
                S.dma('sp', xs[:], xTv[:, m, t * TT:(t + 1) * TT], writes=[xk])
                S.op('dve', lambda: E['dve'].scalar_tensor_tensor(out=xs[:], in0=ps[:], scalar=scale, in1=xs[:],
                                                                  op0=ALU.mult, op1=ALU.add),
                     reads=[pk, xk], writes=[xk])
                S.dma('sp', xTv[:, m, t * TT:(t + 1) * TT], xs[:], reads=[xk])
        return epi

    def ffn(st, l, t, which):
        wg, wu, wd = ("wg1", "wu1", "wd1") if which == 1 else ("wg2", "wu2", "wd2")
        norm_to_hT(st, t, c_norm(0 if which == 1 else 2, l))
        hT = st['hT']
        aT = st['aT']
        wbg, _, _ = WB[wg]
        wbu, _, _ = WB[wu]
        PF = 1
        nb = DFF // 256

        def loadgu(bi):
            wt, wk = st['wA'].get()
            S.dma('sp', wt[:], wbg[l][bi], writes=[wk])
            wt2, wk2 = st['wA'].get()
            S.dma('sp', wt2[:], wbu[l][bi], writes=[wk2])
            return (wt, wk, wt2, wk2)
        q = [loadgu(bi) for bi in range(min(PF, nb))]
        for bi in range(nb):
            if bi + PF < nb:
                q.append(loadgu(bi + PF))
            wt, wk, wt2, wk2 = q.pop(0)
            for sub in range(2):
                m = bi * 2 + sub
                pg, pgk = ps_pool.get()
                pu, puk = ps_pool.get()
                for (ps, pk, w_, wk_) in ((pg, pgk, wt, wk), (pu, puk, wt2, wk2)):
                    for kc in range(16):
                        S.op('pe', lambda: E['pe'].matmul(ps[:], lhsT=w_[:, kc, sub * 128:(sub + 1) * 128],
                                                          rhs=hT[:, kc, :], start=(kc == 0), stop=(kc == 15)),
                             reads=[wk_, ('hT', kc)], writes=[pk], inc=(kc == 15))
                sg, sgk = st['f32'].get()
                S.op('act', lambda: E['act'].activation(out=sg[:], in_=pg[:], func=AF.Silu), reads=[pgk], writes=[sgk])
                S.op('dve', lambda: E['dve'].tensor_tensor(out=aT[:, m, :], in0=sg[:], in1=pu[:], op=ALU.mult),
                     reads=[sgk, puk], writes=[('aT', m)])
        gemm_blocks(st, wd, l, 16, lambda kc: (aT[:, kc, :], ('aT', kc)), 44, resid_epi(st, t, 0.5), st['wB'])

    def gemm_scope():
        stack = ExitStack()
        st = {'stack': stack}
        st['hT'] = sb("hT", [128, 16, TT], BF16, stack)
        st['aT'] = sb("aT", [128, 44, TT], BF16, stack)
        st['wA'] = Pool([sb("wA%d" % i, [128, 16, 256], BF16, stack) for i in range(4)], "wA")
        st['wB'] = Pool([sb("wB%d" % i, [128, 44, 128], BF16, stack) for i in range(3)], "wB")
        st['xs'] = Pool([sb("xs%d" % i, [128, TT], F32, stack) for i in range(4)], "xs")
        st['sq'] = Pool([sb("sq%d" % i, [128, TT], BF16, stack) for i in range(2)], "sq")
        st['f32'] = Pool([sb("f32_%d" % i, [128, TT], F32, stack) for i in range(6)], "f32")
        st['ob'] = Pool([sb("ob%d" % i, [128, TT], BF16, stack) for i in range(6)], "ob")
        st['rstd'] = sb("rstd", [128, TT], F32, stack)
        st['rot'] = Pool([sb("rot%d" % i, [128, 2, TT], F32, stack) for i in range(2)], "rot")
        st['g3'] = Pool([sb("g3_%d" % i, [128, 3, TT], BF16, stack) for i in range(2)], "g3")
        return st

    def io_scope():
        stack = ExitStack()
        st = {'stack': stack}
        st['xs'] = Pool([sb("ixs%d" % i, [128, TT], F32, stack) for i in range(4)], "xs")
        st['sq'] = Pool([sb("isq%d" % i, [128, TT], BF16, stack) for i in range(2)], "sq")
        st['rstd'] = sb("irstd", [128, TT], F32, stack)
        st['xrow'] = sb("xrow", [128, D], F32, stack)
        st['stg'] = sb("stg", [128, 16, TT], F32, stack)
        return st

    def phase_in(st):
        stg = st['stg']
        xr = st['xrow']
        for t in range(NT):
            for s4 in range(4):
                S.dma('sp', xr[:], x_in[t * TT + s4 * 128:t * TT + (s4 + 1) * 128, :], writes=['xrow'])
                for mg in range(4):
                    ps, pk = ps_pool.get()
                    for j in range(4):
                        m = mg * 4 + j
                        S.op('pe', lambda: E['pe'].transpose(ps[:, j * 128:(j + 1) * 128], xr[:, m * 128:(m + 1) * 128], ident_f[:]),
                             reads=['xrow', 'identf'], writes=[pk], inc=(j == 3))
                    S.op('act', lambda: E['act'].activation(
                        out=stg[:, mg * 4:(mg + 1) * 4, s4 * 128:(s4 + 1) * 128],
                        in_=ps[:].rearrange("p (j c) -> p j c", j=4), func=AF.Copy),
                        reads=[pk], writes=['stg'])
            S.dma('sp', xTv[:, :, t * TT:(t + 1) * TT], stg[:], reads=['stg'])
        S.barrier()

    def phase_inproj(st, l, t):
        norm_to_hT(st, t, c_norm(1, l))
        hT = st['hT']
        tsl = slice(t * TT, (t + 1) * TT)

        def store(dst, m_local, ob, obk):
            S.dma('sp', dst[m_local * 128:(m_local + 1) * 128, tsl], ob[:], reads=[obk])

        def epi(bi, pss):
            (p0, k0), (p1, k1) = pss
            if bi < 8:
                isq = bi < 4
                h = bi if isq else bi - 4
                tab = rotq_in if isq else rotk_in
                dst = qrT if isq else krT
                rt, rtk = st['rot'].get()
                S.dma('sp', rt[:], tab[h, :, :, tsl].rearrange("c p t -> p c t"), writes=[rtk])
                t1, t1k = st['f32'].get()
                t2, t2k = st['f32'].get()
                S.op('act', lambda: E['act'].activation(out=t1[:], in_=p0[:], func=AF.Copy), reads=[k0], writes=[t1k])
                S.op('act', lambda: E['act'].activation(out=t2[:], in_=p1[:], func=AF.Copy), reads=[k1], writes=[t2k])
                a, ak = st['f32'].get()
                b, bk = st['f32'].get()
                o1, o1k = st['ob'].get()
                o2, o2k = st['ob'].get()
                S.op('pool', lambda: E['pool'].tensor_tensor(out=a[:], in0=t1[:], in1=rt[:, 0, :], op=ALU.mult), reads=[t1k, rtk], writes=[ak])
                S.op('pool', lambda: E['pool'].tensor_tensor(out=b[:], in0=t2[:], in1=rt[:, 1, :], op=ALU.mult), reads=[t2k, rtk], writes=[bk])
                S.op('pool', lambda: E['pool'].tensor_tensor(out=o1[:], in0=a[:], in1=b[:], op=ALU.subtract), reads=[ak, bk], writes=[o1k])
                S.op('dve', lambda: E['dve'].tensor_tensor(out=t1[:], in0=t1[:], in1=rt[:, 1, :], op=ALU.mult), reads=[t1k, rtk, ak], writes=[t1k])
                S.op('dve', lambda: E['dve'].tensor_tensor(out=t2[:], in0=t2[:], in1=rt[:, 0, :], op=ALU.mult), reads=[t2k, rtk, bk], writes=[t2k])
                S.op('dve', lambda: E['dve'].tensor_tensor(out=o2[:], in0=t1[:], in1=t2[:], op=ALU.add), reads=[t1k, t2k], writes=[o2k])
                store(dst, 2 * h, o1, o1k)
                store(dst, 2 * h + 1, o2, o2k)
                return
            for sub, (ps, pk) in enumerate(pss):
                m = bi * 2 + sub
                if m < 32:
                    ob, obk = st['ob'].get()
                    if sub == 0:
                        S.op('act', lambda: E['act'].activation(out=ob[:], in_=ps[:], func=AF.Copy), reads=[pk], writes=[obk])
                    else:
                        S.op('dve', lambda: E['dve'].tensor_copy(out=ob[:], in_=ps[:]), reads=[pk], writes=[obk])
                    store(rvT, m - 16, ob, obk)
                elif m < 48:
                    ob, obk = st['ob'].get()
                    S.op('act', lambda: E['act'].activation(out=ob[:], in_=ps[:], func=AF.Silu), reads=[pk], writes=[obk])
                    store(rgT, m - 32, ob, obk)
                elif m < 64:
                    xs_, xk_ = st['f32'].get()
                    u, uk = st['f32'].get()
                    S.op('act', lambda: E['act'].activation(out=xs_[:], in_=ps[:], func=AF.Copy), reads=[pk], writes=[xk_])
                    S.op('act', lambda: E['act'].activation(out=u[:], in_=ps[:], func=AF.Square), reads=[pk], writes=[uk])
                    S.op('dve', lambda: E['dve'].tensor_scalar(out=u[:], in0=u[:], scalar1=0.044715, scalar2=1.0, op0=ALU.mult, op1=ALU.add),
                         reads=[uk], writes=[uk])
                    S.op('dve', lambda: E['dve'].tensor_tensor(out=u[:], in0=u[:], in1=xs_[:], op=ALU.mult), reads=[uk, xk_], writes=[uk])
                    S.op('act', lambda: E['act'].activation(out=u[:], in_=u[:], func=AF.Sigmoid, scale=1.5957691216057308),
                         reads=[uk], writes=[uk])
                    if m < 56:
                        ob, obk = st['ob'].get()
                        S.op('dve', lambda: E['dve'].tensor_tensor(out=ob[:], in0=u[:], in1=xs_[:], op=ALU.mult), reads=[uk, xk_], writes=[obk])
                        store(suT, m - 48, ob, obk)
                    else:
                        S.op('dve', lambda: E['dve'].tensor_tensor(out=u[:], in0=u[:], in1=xs_[:], op=ALU.mult), reads=[uk, xk_], writes=[uk])
                        store(svT, m - 56, u, uk)
                elif m < 88:
                    ob, obk = st['ob'].get()
                    sc = (128 ** -0.5) if m < 72 else 1.0
                    dst = aqT if m < 72 else (akT if m < 80 else avT)
                    mb = 64 if m < 72 else (72 if m < 80 else 80)
                    S.op('act', lambda: E['act'].activation(out=ob[:], in_=ps[:], func=AF.Copy, scale=sc), reads=[pk], writes=[obk])
                    store(dst, m - mb, ob, obk)
                else:
                    ob, obk = st['ob'].get()
                    S.op('act', lambda: E['act'].activation(out=ob[:], in_=ps[:], func=AF.Sigmoid, bias=colv(C_BG + l * 48 + (m - 88)), scale=1.0),
                         reads=[pk, 'cols'], writes=[obk])
                    store(gT, m - 88, ob, obk)
        gemm_blocks(st, "win", l, 68, lambda kc: (hT[:, kc, :], ('hT', kc)), 16, epi, st['wA'])

    def phase_proj(st, l, t):
        tsl = slice(t * TT, (t + 1) * TT)
        rt_ = st['hT']
        aT = st['aT']
        sgt = aT[:, 0:8, :]
        att = aT[:, 8:16, :]
        mg = aT[:, 16:32, :]
        S.dma('sp', rt_[:], retoT.rearrange("(m p) t -> p m t", p=128)[:, :, tsl], writes=[('hT', k) for k in range(16)])
        S.dma('sp', sgt, sgoT.rearrange("(m p) t -> p m t", p=128)[:, :, tsl], writes=[('aT', k) for k in range(8)])
        S.dma('sp', att, attoT.rearrange("(m p) t -> p m t", p=128)[:, :, tsl], writes=[('aT', 8 + k) for k in range(8)])
        gTv = gT.rearrange("(b m p) t -> p b m t", b=3, p=128)
        wbr, _, _ = WB["wpr"]
        wbs, _, _ = WB["wps"]
        wba, _, _ = WB["wpa"]
        for bi in range(8):
            w1, w1k = st['wA'].get()
            S.dma('sp', w1[:], wbr[l][bi], writes=[w1k])
            w2, w2k = st['wA'].get()
            S.dma('sp', w2[:, 0:8, :], wbs[l][bi], writes=[w2k])
            S.dma('sp', w2[:, 8:16, :], wba[l][bi], writes=[w2k])
            for sub in range(2):
                m = bi * 2 + sub
                pa, pak = ps_pool.get()
                pb, pbk = ps_pool.get()
                pc, pck = ps_pool.get()
                for kc in range(16):
                    S.op('pe', lambda: E['pe'].matmul(pa[:], lhsT=w1[:, kc, sub * 128:(sub + 1) * 128], rhs=rt_[:, kc, :],
                                                      start=(kc == 0), stop=(kc == 15)),
                         reads=[w1k, ('hT', kc)], writes=[pak], inc=(kc == 15))
                for kc in range(8):
                    S.op('pe', lambda: E['pe'].matmul(pb[:], lhsT=w2[:, kc, sub * 128:(sub + 1) * 128], rhs=sgt[:, kc, :],
                                                      start=(kc == 0), stop=(kc == 7)),
                         reads=[w2k, ('aT', kc)], writes=[pbk], inc=(kc == 7))
                for kc in range(8):
                    S.op('pe', lambda: E['pe'].matmul(pc[:], lhsT=w2[:, 8 + kc, sub * 128:(sub + 1) * 128], rhs=att[:, kc, :],
                                                      start=(kc == 0), stop=(kc == 7)),
                         reads=[w2k, ('aT', 8 + kc)], writes=[pck], inc=(kc == 7))
                g3, g3k = st['g3'].get()
                S.dma('sp', g3[:], gTv[:, :, m, tsl], writes=[g3k])
                t1, t1k = st['f32'].get()
                t2, t2k = st['f32'].get()
                t3, t3k = st['f32'].get()
                S.op('dve', lambda: E['dve'].tensor_tensor(out=t1[:], in0=pa[:], in1=g3[:, 0, :], op=ALU.mult), reads=[pak, g3k], writes=[t1k])
                S.op('dve', lambda: E['dve'].tensor_tensor(out=t2[:], in0=pb[:], in1=g3[:, 1, :], op=ALU.mult), reads=[pbk, g3k], writes=[t2k])
                S.op('dve', lambda: E['dve'].tensor_tensor(out=t3[:], in0=pc[:], in1=g3[:, 2, :], op=ALU.mult), reads=[pck, g3k], writes=[t3k])
                S.op('pool', lambda: E['pool'].tensor_tensor(out=t1[:], in0=t1[:], in1=t2[:], op=ALU.add), reads=[t1k, t2k], writes=[t1k])
                S.op('pool', lambda: E['pool'].tensor_tensor(out=mg[:, m, :], in0=t1[:], in1=t3[:], op=ALU.add), reads=[t1k, t3k], writes=[('aT', 16 + m)])
        gemm_blocks(st, "wo", l, 8, lambda kc: (mg[:, kc, :], ('aT', 16 + kc)), 16, resid_epi(st, t, 1.0), st['wA'])

    def phase_out(st):
        stg = st['stg']
        orow = st['xrow']
        for t in range(NT):
            def o(m, xs, xk, rs):
                S.op('dve', lambda: E['dve'].scalar_tensor_tensor(out=stg[:, m, :], in0=xs[:], scalar=colv(C_FINAL + m),
                                                                  in1=rs[:], op0=ALU.mult, op1=ALU.mult),
                     reads=[xk, 'rstd', 'cols'], writes=[('stg', m)])
            rms_norm(st, t, C_FINAL, o)
            for s4 in range(4):
                for mg_ in range(4):
                    ps, pk = ps_pool.get()
                    for j in range(4):
                        m = mg_ * 4 + j
                        S.op('pe', lambda: E['pe'].transpose(ps[:, j * 128:(j + 1) * 128], stg[:, m, s4 * 128:(s4 + 1) * 128], ident_f[:]),
                             reads=[('stg', m), 'identf'], writes=[pk], inc=(j == 3))
                    S.op('act', lambda: E['act'].activation(out=orow[:, mg_ * 512:(mg_ + 1) * 512], in_=ps[:], func=AF.Copy),
                         reads=[pk], writes=['xrow'])
                S.dma('sp', y_out[t * TT + s4 * 128:t * TT + (s4 + 1) * 128, :], orow[:], reads=['xrow'])
            S.barrier()

    def phase_ret(l):
        stack = ExitStack()
        SC = 256
        qv = qrT.rearrange("(m p) t -> p m t", p=128)
        kv = krT.rearrange("(m p) t -> p m t", p=128)
        vv = rvT.rearrange("(m p) t -> p m t", p=128)
        gv = rgT.rearrange("(m p) t -> p m t", p=128)
        ov = retoT.rearrange("(m p) t -> p m t", p=128)
        qs = Pool([sb("rq%d" % i, [128, 8, SC], BF16, stack) for i in range(2)], "rq")
        ks = Pool([sb("rk%d" % i, [128, 8, SC], BF16, stack) for i in range(2)], "rk")
        vs = Pool([sb("rv%d" % i, [128, 16, SC], BF16, stack) for i in range(2)], "rv")
        gs = Pool([sb("rg%d" % i, [128, 16, SC], BF16, stack) for i in range(2)], "rg")
        os_ = Pool([sb("ro%d" % i, [128, 16, SC], BF16, stack) for i in range(2)], "ro")
        R32 = [sb("R32_%d" % h, [128, 2, 512], F32, stack) for h in range(4)]
        Rbf = [sb("Rbf_%d" % h, [128, 2, 512], BF16, stack) for h in range(4)]
        kh = Pool([sb("kh%d" % i, [128, 256], BF16, stack) for i in range(3)], "kh")
        vt = Pool([sb("vt%d" % i, [128, 512], BF16, stack) for i in range(3)], "vt")
        At = Pool([sb("At%d" % i, [128, 128], BF16, stack) for i in range(3)], "At")
        osb = Pool([sb("osb%d" % i, [128, 512], F32, stack) for i in range(3)], "osb")
        sqb = Pool([sb("sqb%d" % i, [128, 512], BF16, stack) for i in range(3)], "sqb")
        rin = Pool([sb("rin%d" % i, [128, 128], F32, stack) for i in range(3)], "rin")
        cmb = sb("cmb", [128, 128], F32, stack)
        for h in range(4):
            S.op('pool', lambda: E['pool'].memset(R32[h][:], 0.0), writes=[('R32', h)])
            S.op('pool', lambda: E['pool'].memset(Rbf[h][:], 0.0), writes=[('Rbf', h)])
        bA = [(PSB[0], ('psb', 0)), (PSB[1], ('psb', 1))]
        bB = [(PSB[2], ('psb', 2)), (PSB[3], ('psb', 3))]
        bC = [(PSB[4], ('psb', 4)), (PSB[5], ('psb', 5))]
        bU = [(PSB[6], ('psb', 6)), (PSB[7], ('psb', 7))]
        u = 0
        for sc in range(T // SC):
            tsl = slice(sc * SC, (sc + 1) * SC)
            q_, qk = qs.get()
            k_, kk = ks.get()
            v_, vk = vs.get()
            g_, gk = gs.get()
            o_, ok = os_.get()
            S.dma('sp', q_[:], qv[:, :, tsl], writes=[qk])
            S.dma('sp', k_[:], kv[:, :, tsl], writes=[kk])
            S.dma('sp', v_[:], vv[:, :, tsl], writes=[vk])
            S.dma('sp', g_[:], gv[:, :, tsl], writes=[gk])
            for cc in range(SC // 128):
                csl = slice(cc * 128, (cc + 1) * 128)
                for h in range(4):
                    gamma = 1.0 - 2.0 ** (-5.0 - h)
                    gC = float(np.float32(np.exp(np.float32(128.0) * np.log1p(np.float32(-(2.0 ** (-5.0 - h)))))))
                    pA, pAk = bA[u % 2]
                    pB, pBk = bB[u % 2]
                    pC, pCk = bC[u % 2]
                    u += 1
                    pAb = pA[:].bitcast(BF16)
                    for dch in range(2):
                        S.op('pe', lambda: E['pe'].transpose(pAb[:, dch * 128:(dch + 1) * 128], k_[:, 2 * h + dch, csl], ident_b[:]),
                             reads=[kk, 'identb'], writes=[pAk], inc=False)
                    for ech in range(4):
                        S.op('pe', lambda: E['pe'].transpose(pAb[:, 256 + ech * 128:256 + (ech + 1) * 128], v_[:, 4 * h + ech, csl], ident_b[:]),
                             reads=[vk, 'identb'], writes=[pAk], inc=(ech == 3))
                    kh_, khk = kh.get()
                    vt_, vtk = vt.get()
                    S.op('act', lambda: E['act'].activation(out=kh_[:], in_=pAb[:, 0:256], func=AF.Copy, scale=gC), reads=[pAk], writes=[khk])
                    S.op('act', lambda: E['act'].activation(out=vt_[:], in_=pAb[:, 256:768], func=AF.Copy), reads=[pAk], writes=[vtk])
                    for dch in range(2):
                        S.op('pe', lambda: E['pe'].matmul(pB[:, 0:128], lhsT=k_[:, 2 * h + dch, csl], rhs=q_[:, 2 * h + dch, csl],
                                                          start=(dch == 0), stop=(dch == 1)),
                             reads=[kk, qk], writes=[(pBk, 0)], inc=(dch == 1))
                    A_, Ak = At.get()
                    S.op('dve', lambda: E['dve'].tensor_tensor(out=A_[:], in0=pB[:, 0:128], in1=cmask[:], op=ALU.mult),
                         reads=[(pBk, 0), 'cmask'], writes=[Ak])
                    for ech in range(4):
                        oc = pC[:, ech * 128:(ech + 1) * 128]
                        S.op('pe', lambda: E['pe'].matmul(oc, lhsT=vt_[:, ech * 128:(ech + 1) * 128], rhs=A_[:], start=True, stop=False),
                             reads=[vtk, Ak], writes=[pCk], inc=False)
                        for dch in range(2):
                            S.op('pe', lambda: E['pe'].matmul(oc, lhsT=Rbf[h][:, dch, ech * 128:(ech + 1) * 128], rhs=q_[:, 2 * h + dch, csl],
                                                              start=False, stop=(dch == 1)),
                                 reads=[('Rbf', h), qk], writes=[pCk], inc=(dch == 1 and ech == 3))
                    ob_, obk = osb.get()
                    sq_, sqk = sqb.get()
                    S.op('act', lambda: E['act'].activation(out=ob_[:], in_=pC[:], func=AF.Copy), reads=[pCk], writes=[obk])
                    S.op('act', lambda: E['act'].activation(out=sq_[:], in_=pC[:], func=AF.Square), reads=[pCk], writes=[sqk])
                    for ech in range(4):
                        S.op('pe', lambda: E['pe'].matmul(pB[:, 128:256], lhsT=ones_b[:], rhs=sq_[:, ech * 128:(ech + 1) * 128],
                                                          start=(ech == 0), stop=(ech == 3)),
                             reads=['onesb', sqk], writes=[(pBk, 1)], inc=(ech == 3))
                    ri, rik = rin.get()
                    S.op('act', lambda: E['act'].activation(out=ri[:], in_=pB[:, 128:256], func=AF.Sqrt, bias=epsc[:], scale=1.0 / 512),
                         reads=[(pBk, 1), 'epsc'], writes=[rik])
                    S.op('dve', lambda: E['dve'].reciprocal(out=ri[:], in_=ri[:]), reads=[rik], writes=[rik])
                    S.op('dve', lambda: E['dve'].tensor_tensor(out=ob_[:].rearrange("p (e i) -> p e i", e=4),
                                                               in0=ob_[:].rearrange("p (e i) -> p e i", e=4),
                                                               in1=ri[:].unsqueeze(1).broadcast_to([128, 4, 128]), op=ALU.mult),
                         reads=[obk, rik], writes=[obk])
                    S.op('pool', lambda: E['pool'].tensor_tensor(out=o_[:, 4 * h:4 * h + 4, csl],
                                                                 in0=ob_[:].rearrange("p (e i) -> p e i", e=4),
                                                                 in1=g_[:, 4 * h:4 * h + 4, csl], op=ALU.mult),
                         reads=[obk, gk], writes=[ok])
                    for dch in range(2):
                        pU, pUk = bU[dch]
                        S.op('pe', lambda: E['pe'].matmul(pU[:], lhsT=kh_[:, dch * 128:(dch + 1) * 128], rhs=vt_[:], start=True, stop=True),
                             reads=[khk, vtk], writes=[pUk], inc=True)
                        S.op('dve', lambda: E['dve'].scalar_tensor_tensor(out=R32[h][:, dch, :], in0=R32[h][:, dch, :], scalar=gC,
                                                                          in1=pU[:], op0=ALU.mult, op1=ALU.add),
                             reads=[pUk, ('R32', h)], writes=[('R32', h)])
                    S.op('act', lambda: E['act'].activation(out=Rbf[h][:], in_=R32[h][:], func=AF.Copy),
                         reads=[('R32', h)], writes=[('Rbf', h)])
            S.dma('sp', ov[:, :, tsl], o_[:], reads=[ok])
        S.barrier()
        stack.close()

    def phase_sgu(l):
        stack = ExitStack()
        SC = 256
        uv = suT.rearrange("(m p) t -> p m t", p=128)
        vv = svT.rearrange("(m p) t -> p m t", p=128)
        ov = sgoT.rearrange("(m p) t -> p m t", p=128)
        us = Pool([sb("su%d" % i, [128, 8, SC], BF16, stack) for i in range(2)], "su")
        vs = Pool([sb("sv%d" % i, [128, 8, SC], F32, stack) for i in range(2)], "sv")
        os_ = Pool([sb("so%d" % i, [128, 8, SC], BF16, stack) for i in range(2)], "so")
        lng = sb("lng", [128, 1024], F32, stack)
        lnb = sb("lnb", [128, 1024], F32, stack)
        wcf = sb("wcf", [128, 4, 128], F32, stack)
        wcb = sb("wcb", [128, 4, 128], BF16, stack)
        sgb = sb("sgb", [128, 4, 128], F32, stack)
        xn = Pool([sb("xn%d" % i, [128, 1024], F32, stack) for i in range(2)], "xn")
        vb = Pool([sb("vb%d" % i, [128, 1024], BF16, stack) for i in range(2)], "vb")
        stt = Pool([sb("stt%d" % i, [128, 16], F32, stack) for i in range(2)], "stt")
        tm = Pool([sb("tm%d" % i, [128, 8, 128], F32, stack) for i in range(2)], "tm")
        S.dma('sp', lng[:], lng_in[l].partition_broadcast(128), writes=['lng'])
        S.dma('sp', lnb[:], lnb_in[l].partition_broadcast(128), writes=['lnb'])
        S.dma('sp', wcf[:], sgwT_in[l].rearrange("g s t -> s g t"), writes=['wcf'])
        S.dma('sp', sgb[:], sgb_in[l].partition_broadcast(128), writes=['sgb'])
        S.op('dve', lambda: E['dve'].tensor_tensor(out=wcb[:], in0=wcf[:], in1=cmask[:].unsqueeze(1).broadcast_to([128, 4, 128]), op=ALU.mult),
             reads=['wcf', 'cmask'], writes=['wcb'])
        u = 0
        for sc in range(T // SC):
            tsl = slice(sc * SC, (sc + 1) * SC)
            u_, uk = us.get()
            v_, vk = vs.get()
            o_, ok = os_.get()
            S.dma('sp', u_[:], uv[:, :, tsl], writes=[uk])
            S.dma('sp', v_[:], vv[:, :, tsl], writes=[vk])
            for cc in range(SC // 128):
                csl = slice(cc * 128, (cc + 1) * 128)
                b0 = (u % 2) * 4
                u += 1
                p0, p0k = PSB[b0], ('psb', b0)
                p1, p1k = PSB[b0 + 1], ('psb', b0 + 1)
                p2, p2k = PSB[b0 + 2], ('psb', b0 + 2)
                p3, p3k = PSB[b0 + 3], ('psb', b0 + 3)
                for f in range(8):
                    pp = p0 if f < 4 else p1
                    ppk = p0k if f < 4 else p1k
                    S.op('pe', lambda: E['pe'].transpose(pp[:, (f % 4) * 128:(f % 4 + 1) * 128], v_[:, f, csl], ident_f[:]),
                         reads=[vk, 'identf'], writes=[ppk], inc=(f % 4 == 3))
                st_, stk = stt.get()
                S.op('dve', lambda: E['dve'].bn_stats(out=st_[:, 0:6], in_=p0[:]), reads=[p0k], writes=[stk])
                S.op('dve', lambda: E['dve'].bn_stats(out=st_[:, 6:12], in_=p1[:]), reads=[p1k, stk], writes=[stk])
                S.op('dve', lambda: E['dve'].bn_aggr(out=st_[:, 12:14], in_=st_[:, 0:12]), reads=[stk], writes=[stk])
                S.op('act', lambda: E['act'].activation(out=st_[:, 14:15], in_=st_[:, 13:14], func=AF.Sqrt, bias=epsc[:], scale=1.0),
                     reads=[stk, 'epsc'], writes=[stk])
                S.op('dve', lambda: E['dve'].reciprocal(out=st_[:, 15:16], in_=st_[:, 14:15]), reads=[stk], writes=[stk])
                xn_, xnk = xn.get()
                for half, (pp, ppk) in enumerate(((p0, p0k), (p1, p1k))):
                    S.op('dve', lambda: E['dve'].tensor_scalar(out=xn_[:, half * 512:(half + 1) * 512], in0=pp[:], scalar1=st_[:, 12:13],
                                                               scalar2=st_[:, 15:16], op0=ALU.subtract, op1=ALU.mult),
                         reads=[ppk, stk], writes=[xnk])
                vb_, vbk = vb.get()
                S.op('pool', lambda: E['pool'].tensor_tensor(out=xn_[:], in0=xn_[:], in1=lng[:], op=ALU.mult), reads=[xnk, 'lng'], writes=[xnk])
                S.op('pool', lambda: E['pool'].tensor_tensor(out=vb_[:], in0=xn_[:], in1=lnb[:], op=ALU.add), reads=[xnk, 'lnb'], writes=[vbk])
                for f in range(8):
                    pp = p2 if f < 4 else p3
                    ppk = p2k if f < 4 else p3k
                    S.op('pe', lambda: E['pe'].matmul(pp[:, (f % 4) * 128:(f % 4 + 1) * 128], lhsT=vb_[:, f * 128:(f + 1) * 128], rhs=wcb[:, f // 2, :],
                                                      start=True, stop=True),
                         reads=[vbk, 'wcb'], writes=[ppk], inc=(f % 4 == 3))
                tm_, tmk = tm.get()
                for half, (pp, ppk) in enumerate(((p2, p2k), (p3, p3k))):
                    S.op('dve', lambda: E['dve'].tensor_tensor(
                        out=tm_[:, half * 4:(half + 1) * 4, :].rearrange("p (g two) t -> p g two t", two=2),
                        in0=pp[:].rearrange("p (g two t) -> p g two t", g=2, two=2),
                        in1=sgb[:, 2 * half:2 * half + 2, :].unsqueeze(2).broadcast_to([128, 2, 2, 128]), op=ALU.add),
                        reads=[ppk, 'sgb'], writes=[tmk])
                S.op('pool', lambda: E['pool'].tensor_tensor(out=o_[:, :, csl], in0=tm_[:], in1=u_[:, :, csl], op=ALU.mult),
                     reads=[tmk, uk], writes=[ok])
            S.dma('sp', ov[:, :, tsl], o_[:], reads=[ok])
        S.barrier()
        stack.close()

    def phase_att(l):
        stack = ExitStack()
        WIN = 2048
        NW = T // WIN
        qv = aqT.rearrange("(m p) t -> p m t", p=128)
        kv = akT.rearrange("(m p) t -> p m t", p=128)
        vv = avT.rearrange("(m p) t -> p m t", p=128)
        ov = attoT.rearrange("(m p) t -> p m t", p=128)
        Em = sb("Em", [128, 48, 128], F32, stack)
        qb = Pool([sb("aq%d" % i, [128, WIN], BF16, stack) for i in range(2)], "aq")
        kb = Pool([sb("ak%d" % i, [128, 2 * WIN], BF16, stack) for i in range(2)], "ak")
        vbf = Pool([sb("av%d" % i, [128, 2 * WIN], BF16, stack) for i in range(2)], "av")
        acc = Pool([sb("acc%d" % i, [128, 2, WIN], F32, stack) for i in range(2)], "acc")
        oo = Pool([sb("ao%d" % i, [128, WIN], BF16, stack) for i in range(2)], "ao")
        ex = Pool([sb("ex%d" % i, [128, 2, 128], F32, stack) for i in range(3)], "ex")
        Pb = Pool([sb("Pb%d" % i, [128, 2, 128], BF16, stack) for i in range(3)], "Pb")
        vtk_ = Pool([sb("avt%d" % i, [128, 2, 128], BF16, stack) for i in range(3)], "avt")
        S.dma('sp', Em[:], bm_in.rearrange("c h k j i -> j (c h k) i"), writes=['Em'])
        S.op('act', lambda: E['act'].activation(out=Em[:], in_=Em[:], func=AF.Exp), reads=['Em'], writes=['Em'])
        u = 0
        for h in range(8):
            for w in range(NW):
                q_, qk = qb.get()
                k_, kk = kb.get()
                v_, vk = vbf.get()
                a_, ak_ = acc.get()
                o_, ok = oo.get()
                w0 = w * WIN
                lo = w0 - WIN if w > 0 else 0
                S.dma('sp', q_[:], qv[:, h, w0:w0 + WIN], writes=[qk])
                if w > 0:
                    S.dma('sp', k_[:], kv[:, h, w0 - WIN:w0 + WIN], writes=[kk])
                    S.dma('sp', v_[:], vv[:, h, w0 - WIN:w0 + WIN], writes=[vk])
                else:
                    S.dma('sp', k_[:, WIN:], kv[:, h, 0:WIN], writes=[kk])
                    S.dma('sp', v_[:, WIN:], vv[:, h, 0:WIN], writes=[vk])
                for ci, dil in enumerate((1, 4, 16)):
                    span = 128 * dil
                    for nb_ in range(WIN // span):
                        for rho in range(dil):
                            base = nb_ * span + rho
                            qsl = slice(base, base + 127 * dil + 1, dil)
                            gpos = w0 + nb_ * span
                            has_prev = gpos > 0
                            ksl_c = slice(WIN + base, WIN + base + 127 * dil + 1, dil)
                            ksl_p = slice(WIN + base - span, WIN + base - span + 127 * dil + 1, dil)
                            bs = (u % 4) * 2
                            u += 1
                            pS, pSk = PSB[bs], ('psb', bs)
                            pO, pOk = PSB[bs + 1], ('psb', bs + 1)
                            pVb = pS[:].bitcast(BF16)[:, 512:768].rearrange("p (k e) -> p k e", k=2)
                            tiles = ([(0, ksl_p)] if has_prev else []) + [(1, ksl_c)]
                            for (pc, ksl) in tiles:
                                S.op('pe', lambda: E['pe'].matmul(pS[:, pc * 128:(pc + 1) * 128], lhsT=k_[:, ksl], rhs=q_[:, qsl], start=True, stop=True),
                                     reads=[kk, qk], writes=[(pSk, 's')], inc=False)
                            for ii, (pc, ksl) in enumerate(tiles):
                                S.op('pe', lambda: E['pe'].transpose(pVb[:, pc, :], v_[:, ksl], ident_b[:]),
                                     reads=[vk, 'identb'], writes=[(pSk, 'v')], inc=(ii == len(tiles) - 1))
                            p_lo = 0 if has_prev else 1
                            ex_, exk = ex.get()
                            S.op('act', lambda: E['act'].activation(out=ex_[:, p_lo:2, :], in_=pS[:, p_lo * 128:256].rearrange("p (k i) -> p k i", i=128), func=AF.Exp),
                                 reads=[(pSk, 's')], writes=[exk])
                            vt_, vtk = vtk_.get()
                            S.op('act', lambda: E['act'].activation(out=vt_[:, p_lo:2, :], in_=pVb[:, p_lo:2, :], func=AF.Copy),
                                 reads=[(pSk, 'v')], writes=[vtk])
                            P_, Pk = Pb.get()
                            e0 = (ci * 8 + h) * 2
                            S.op('pool', lambda: E['pool'].tensor_tensor(out=P_[:, p_lo:2, :], in0=ex_[:, p_lo:2, :], in1=Em[:, e0 + p_lo:e0 + 2, :], op=ALU.mult),
                                 reads=[exk, 'Em'], writes=[Pk])
                            for ii, (pc, ksl) in enumerate(tiles):
                                S.op('pe', lambda: E['pe'].matmul(pO[:, 0:128], lhsT=vt_[:, pc, :], rhs=P_[:, pc, :], start=(ii == 0), stop=(ii == len(tiles) - 1)),
                                     reads=[vtk, Pk], writes=[pOk], inc=False)
                            for ii, (pc, ksl) in enumerate(tiles):
                                S.op('pe', lambda: E['pe'].matmul(pO[:, 128:256], lhsT=ones_b[:], rhs=P_[:, pc, :], start=(ii == 0), stop=(ii == len(tiles) - 1)),
                                     reads=['onesb', Pk], writes=[pOk], inc=(ii == len(tiles) - 1))
                            pOv = pO[:, 0:256].rearrange("p (k i) -> p k i", k=2)
                            if ci == 0:
                                S.op('dve', lambda: E['dve'].tensor_copy(out=a_[:, :, qsl], in_=pOv), reads=[pOk], writes=[ak_])
                            else:
                                S.op('dve', lambda: E['dve'].tensor_tensor(out=a_[:, :, qsl], in0=a_[:, :, qsl], in1=pOv, op=ALU.add),
                                     reads=[pOk, ak_], writes=[ak_])
                S.op('dve', lambda: E['dve'].reciprocal(out=a_[:, 1, :], in_=a_[:, 1, :]), reads=[ak_], writes=[ak_])
                S.op('pool', lambda: E['pool'].tensor_tensor(out=o_[:], in0=a_[:, 0, :], in1=a_[:, 1, :], op=ALU.mult), reads=[ak_], writes=[ok])
                S.dma('sp', ov[:, h, w0:w0 + WIN], o_[:], reads=[ok])
        S.barrier()
        stack.close()

    st = io_scope()
    phase_in(st)
    st['stack'].close()
    for l in range(L):
        st = gemm_scope()
        for t in range(NT):
            ffn(st, l, t, 1)
            S.barrier()
            phase_inproj(st, l, t)
            S.barrier()
        st['stack'].close()
        phase_ret(l)
        phase_sgu(l)
        phase_att(l)
        st = gemm_scope()
        for t in range(NT):
            phase_proj(st, l, t)
            S.barrier()
            ffn(st, l, t, 2)
            S.barrier()
        st['stack'].close()
    st = io_scope()
    phase_out(st)
    st['stack'].close()
    es.close()
    return nc


def _t5_bucket_np(dist):
    max_exact = 16
    d_f = np.maximum(dist, 1).astype(np.float32)
    large = max_exact + (np.log(d_f / np.float32(max_exact)) / np.float32(math.log(2048 / max_exact))
                         * np.float32(32 - max_exact)).astype(np.int32)
    large = np.minimum(large, 31)
    return np.where(dist < max_exact, dist, large)


def host_consts(T, L, inp):
    c = {}
    colsl = []
    for nm in ("ffn1_norm", "mix_norm", "ffn2_norm"):
        for l in range(L):
            colsl.append(np.asarray(inp[nm][l]).reshape(16, 128).T)
    colsl.append(np.asarray(inp["final_norm"]).reshape(16, 128).T)
    for l in range(L):
        colsl.append(np.asarray(inp["b_gate"][l]).reshape(48, 128).T)
    c["cols"] = np.ascontiguousarray(np.concatenate(colsl, axis=1), dtype=np.float32)
    c["lng"] = np.ascontiguousarray(inp["sg_ln_g"][:L], dtype=np.float32)
    c["lnb"] = np.ascontiguousarray(inp["sg_ln_b"][:L], dtype=np.float32)
    c["sgwT"] = np.ascontiguousarray(np.swapaxes(np.asarray(inp["sg_w"][:L]), 2, 3), dtype=np.float32)
    c["sgb"] = np.ascontiguousarray(inp["sg_b"][:L], dtype=np.float32)
    rb = np.asarray(inp["rel_bias"], dtype=np.float32)
    bm = np.empty((3, 8, 2, 128, 128), np.float32)
    i = np.arange(128)[None, :]
    j = np.arange(128)[:, None]
    for ci, dil in enumerate((1, 4, 16)):
        for pc in range(2):
            steps = (128 + i - j) if pc == 0 else (i - j)
            valid = (steps >= 0) & (steps <= 128)
            bucket = _t5_bucket_np(dil * np.maximum(steps, 0))
            for h in range(8):
                bm[ci, h, pc] = np.where(valid, rb[bucket, h], np.float32(-30000.0))
    c["bm"] = bm
    pos = np.arange(T, dtype=np.float32)
    inv = (np.float32(10000.0) ** (-np.arange(0, 256, 2, dtype=np.float32) / np.float32(256))).astype(np.float32)
    ang = (pos[None, :] * inv[:, None]).astype(np.float32)
    cos, sin = np.cos(ang).astype(np.float32), np.sin(ang).astype(np.float32)
    rotq = np.empty((4, 2, 128, T), np.float32)
    rotk = np.empty((4, 2, 128, T), np.float32)
    pm = (np.arange(T) % 128).astype(np.float64)
    for h in range(4):
        lg = math.log1p(-(2.0 ** (-5.0 - h)))
        dq = np.exp((pm + 1.0) * lg)
        dk = np.exp(-(pm + 1.0) * lg) / 16.0
        rotq[h, 0] = cos * dq[None, :]
        rotq[h, 1] = sin * dq[None, :]
        rotk[h, 0] = cos * dk[None, :]
        rotk[h, 1] = sin * dk[None, :]
    c["rotq"] = rotq
    c["rotk"] = rotk
    c["cmask"] = np.triu(np.ones((128, 128), np.float32))
    c["ident"] = np.eye(128, dtype=np.float32)
    return c


_WMAP = {"wg1": "ffn1_w_gate", "wu1": "ffn1_w_up", "wd1": "ffn1_w_down", "win": "w_in", "wpr": "w_proj_ret",
         "wps": "w_proj_sg", "wpa": "w_proj_att", "wo": "w_out", "wg2": "ffn2_w_gate", "wu2": "ffn2_w_up",
         "wd2": "ffn2_w_down"}


def run(inp, T, L, ncores, nseq, dbg=False, trace=False):
    nc = build(T, L, dbg)
    c = host_consts(T, L, inp)
    base = dict(c)
    for k, v in _WMAP.items():
        base[k] = np.ascontiguousarray(np.asarray(inp[v])[:L], dtype=np.float32)
    in_maps = []
    for i in range(ncores):
        m = dict(base)
        m["x"] = np.ascontiguousarray(np.asarray(inp["x"])[i % nseq, :T], dtype=np.float32)
        in_maps.append(m)
    res = run_bass_kernel_spmd(nc, in_maps, core_ids=list(range(ncores)), **({"trace": True} if trace else {}))
    return res


def kernel(**inputs):
    res = run(inputs, SEQ, LDEPTH, 8, 2)
    out = np.stack([np.asarray(res.results[0]["y"]), np.asarray(res.results[1]["y"])], axis=0)
    return out.astype(np.float32)
```

```python
import math
import numpy as np
from contextlib import ExitStack
import concourse.bass as bass
import concourse.mybir as mybir
from concourse.bass_utils import run_bass_kernel_spmd

F32 = mybir.dt.float32
BF16 = mybir.dt.bfloat16
AF = mybir.ActivationFunctionType
ALU = mybir.AluOpType

D = 2048
DFF = 5632
DIN = 17408
TT = 512
EPS = 1e-6
NDS = 6
LDEPTH = 4
SEQ = 8192


class Sched:
    def __init__(s, nc, es):
        s.nc = nc
        s.E = {'pe': nc.tensor, 'act': nc.scalar, 'dve': nc.vector, 'pool': nc.gpsimd, 'sp': nc.sync}
        s.sem = {}
        s.cnt = {}
        for e in s.E:
            s.sem[e] = es.enter_context(nc.semaphore('s_' + e))
            s.cnt[e] = 0
        s.waited = {e: {} for e in s.E}
        s.hist = {}
        s.dq = {}
        s.dqi = {}
        for q in ('sp', 'act', 'pool'):
            s.dq[q] = [[es.enter_context(nc.semaphore('d_%s%d' % (q, i))), 0] for i in range(NDS)]
            s.dqi[q] = 0

    def _wait(s, e, toks):
        need = {}
        for t in toks:
            if t is None:
                continue
            sem, val = t
            k = id(sem)
            if e == 'pe' and sem is s.sem['pe']:
                continue
            if s.waited[e].get(k, 0) >= val:
                continue
            if k not in need or need[k][1] < val:
                need[k] = (sem, val)
        for k, (sem, val) in need.items():
            s.E[e].wait_ge(sem, val)
            s.waited[e][k] = val

    def _deps(s, reads, writes):
        toks = []
        for r in reads:
            h = s.hist.get(r)
            if h:
                toks.append(h[0])
        for w in writes:
            h = s.hist.get(w)
            if h:
                toks.append(h[0])
                toks.extend(h[1].values())
        return toks

    def _record(s, key, tok, reads, writes):
        for r in reads:
            h = s.hist.setdefault(r, [None, {}])
            h[1][key] = tok
        for w in writes:
            s.hist[w] = [tok, {}]

    def op(s, e, fn, reads=(), writes=(), inc=True):
        s._wait(e, s._deps(reads, writes))
        ins = fn()
        if inc:
            s.cnt[e] += 1
            ins.then_inc(s.sem[e], 1)
            tok = (s.sem[e], s.cnt[e])
        else:
            tok = (s.sem[e], s.cnt[e] + 1)
        s._record(e, tok, reads, writes)
        return tok

    def dma(s, q, out, in_, reads=(), writes=()):
        slot = s.dq[q][s.dqi[q] % NDS]
        s.dqi[q] += 1
        toks = s._deps(reads, writes)
        if slot[1] > 0:
            toks.append((slot[0], slot[1]))
        s._wait(q, toks)
        ins = s.E[q].dma_start(out=out, in_=in_)
        slot[1] += 16
        ins.then_inc(slot[0], 16)
        tok = (slot[0], slot[1])
        s._record(id(slot[0]), tok, reads, writes)
        return tok

    def barrier(s):
        toks = [(s.sem[e], s.cnt[e]) for e in s.E if s.cnt[e] > 0]
        for q in s.dq:
            for sl in s.dq[q]:
                if sl[1] > 0:
                    toks.append((sl[0], sl[1]))
        for e in s.E:
            s._wait(e, toks)
        s.hist.clear()


class Pool:
    def __init__(s, tiles, name):
        s.tiles = tiles
        s.name = name
        s.i = 0

    def get(s):
        k = s.i % len(s.tiles)
        s.i += 1
        return s.tiles[k], (s.name, k)


def build(T, L, dbg=False):
    NT = T // TT
    NCH = T // 128
    nc = bass.Bass("TRN2", target_bir_lowering=False)
    es = ExitStack()
    KIN = "ExternalInput"
    KSC = "ExternalOutput" if dbg else "Internal"

    def din(name, shape, dt=F32):
        return nc.dram_tensor(name, list(shape), dt, kind=KIN).ap()

    def dsc(name, shape, dt):
        return nc.dram_tensor(name, list(shape), dt, kind=KSC).ap()

    x_in = din("x", [T, D])
    W = {}
    wspec = [("wg1", D, DFF), ("wu1", D, DFF), ("wd1", DFF, D), ("win", D, DIN), ("wpr", 2048, D),
             ("wps", 1024, D), ("wpa", 1024, D), ("wo", D, D), ("wg2", D, DFF), ("wu2", D, DFF), ("wd2", DFF, D)]
    WB = {}
    for nm, K, N in wspec:
        W[nm] = din(nm, [L, K, N])
        cb = 128 if nm in ("wd1", "wd2") else 256
        WB[nm] = ([nc.dram_tensor("%sb%d" % (nm, l_), [N // cb, 128, K // 128, cb], BF16, kind="Internal").ap() for l_ in range(L)], K // 128, cb)
    NCOLS = (3 * L + 1) * 16 + L * 48
    cols_in = din("cols", [128, NCOLS])
    lng_in = din("lng", [L, 1024])
    lnb_in = din("lnb", [L, 1024])
    sgwT_in = din("sgwT", [L, 4, 128, 128])
    sgb_in = din("sgb", [L, 4, 128])
    bm_in = din("bm", [3, 8, 2, 128, 128])
    rotq_in = din("rotq", [4, 2, 128, T])
    rotk_in = din("rotk", [4, 2, 128, T])
    cmask_in = din("cmask", [128, 128])
    ident_in = din("ident", [128, 128])
    y_out = nc.dram_tensor("y", [T, D], F32, kind="ExternalOutput").ap()

    xT = dsc("xT", [D, T], F32)
    qrT = dsc("qrT", [1024, T], BF16)
    krT = dsc("krT", [1024, T], BF16)
    rvT = dsc("rvT", [2048, T], BF16)
    rgT = dsc("rgT", [2048, T], BF16)
    suT = dsc("suT", [1024, T], BF16)
    svT = dsc("svT", [1024, T], F32)
    aqT = dsc("aqT", [1024, T], BF16)
    akT = dsc("akT", [1024, T], BF16)
    avT = dsc("avT", [1024, T], BF16)
    gT = dsc("gT", [6144, T], BF16)
    retoT = dsc("retoT", [2048, T], BF16)
    sgoT = dsc("sgoT", [1024, T], BF16)
    attoT = dsc("attoT", [1024, T], BF16)

    S = Sched(nc, es)
    E = S.E

    uid = [0]

    def sb(name, shape, dt, stack=None):
        uid[0] += 1
        return (stack or es).enter_context(nc.sbuf_tensor("sb%d_%s" % (uid[0], name), list(shape), dt))

    PSB = [es.enter_context(nc.psum_tensor("psb%d" % i, [128, 512], F32)) for i in range(8)]
    ps_pool = Pool(PSB, "psb")

    cols = sb("cols", [128, NCOLS], F32)
    ident_f = sb("identf", [128, 128], F32)
    ident_b = sb("identb", [128, 128], BF16)
    ones_b = sb("onesb", [128, 128], BF16)
    cmask = sb("cmaskf", [128, 128], F32)
    epsc = sb("epsc", [128, 1], F32)
    S.dma('sp', cols[:], cols_in, writes=['cols'])
    S.dma('sp', ident_f[:], ident_in, writes=['identf'])
    S.dma('sp', cmask[:], cmask_in, writes=['cmask'])
    S.op('pool', lambda: E['pool'].memset(ones_b[:], 1.0), writes=['onesb'])
    S.op('pool', lambda: E['pool'].memset(epsc[:], EPS), writes=['epsc'])
    S.op('dve', lambda: E['dve'].tensor_copy(out=ident_b[:], in_=ident_f[:]), reads=['identf'], writes=['identb'])

    def cast_jobs(l):
        for nm, K, N in wspec:
            wb, KC, cb = WB[nm]
            for c0 in range(N // cb):
                src = W[nm][l][:, c0 * cb:(c0 + 1) * cb].rearrange("(kc p) c -> p kc c", p=128)
                yield (wb[l][c0], src)

    filler = [iter(())]

    def fill(n=1):
        for _ in range(n):
            j = next(filler[0], None)
            if j is None:
                return
            S.dma('pool', j[0], j[1])

    def drain_fill():
        while True:
            j = next(filler[0], None)
            if j is None:
                return
            S.dma('pool', j[0], j[1])

    for j in cast_jobs(0):
        S.dma('pool', j[0], j[1])
    S.barrier()

    def colv(idx):
        return cols[:, idx:idx + 1]

    def c_norm(kind, l):
        return (kind * L + l) * 16

    C_FINAL = 3 * L * 16
    C_BG = (3 * L + 1) * 16

    xTv = xT.rearrange("(m p) t -> p m t", p=128)

    def gemm_blocks(st, wname, l, nblocks, rhs_fn, KC, epi, wpool, b0=0):
        wb, KCw, cb = WB[wname]
        assert KCw == KC
        nsub = cb // 128
        pend = []
        PF = 2
        blocks = list(range(b0, b0 + nblocks))
        loaded = {}

        def load(bi):
            wt, wk = wpool.get()
            S.dma('sp', wt[:, 0:KC, 0:cb], wb[l][bi], writes=[wk])
            loaded[bi] = (wt, wk)
        for bi in blocks[:PF]:
            load(bi)
        for ii, bi in enumerate(blocks):
            if ii + PF < len(blocks):
                load(blocks[ii + PF])
            wt, wk = loaded.pop(bi)
            fill()
            pss = []
            for sub in range(nsub):
                ps, pk = ps_pool.get()
                for kc in range(KC):
                    rap, rk = rhs_fn(kc)
                    S.op('pe', lambda: E['pe'].matmul(ps[:], lhsT=wt[:, kc, sub * 128:(sub + 1) * 128], rhs=rap,
                                                      start=(kc == 0), stop=(kc == KC - 1)),
                         reads=[wk, rk], writes=[pk], inc=(kc == KC - 1))
                pss.append((ps, pk))
            epi(bi, pss)

    def rms_norm(st, t, gcol0, out_fn):
        ps, pk = ps_pool.get()
        for m in range(16):
            xs, xk = st['xs'].get()
            S.dma('sp', xs[:], xTv[:, m, t * TT:(t + 1) * TT], writes=[xk])
            sq, sk = st['sq'].get()
            S.op('act', lambda: E['act'].activation(out=sq[:], in_=xs[:], func=AF.Square), reads=[xk], writes=[sk])
            S.op('pe', lambda: E['pe'].matmul(ps[:], lhsT=ones_b[:], rhs=sq[:], start=(m == 0), stop=(m == 15)),
                 reads=['onesb', sk], writes=[pk], inc=True)
        rs = st['rstd']
        S.op('act', lambda: E['act'].activation(out=rs[:], in_=ps[:], func=AF.Sqrt, bias=epsc[:], scale=1.0 / D),
             reads=[pk, 'epsc'], writes=['rstd'])
        S.op('dve', lambda: E['dve'].reciprocal(out=rs[:], in_=rs[:]), reads=['rstd'], writes=['rstd'])
        for m in range(16):
            xs, xk = st['xs'].get()
            S.dma('sp', xs[:], xTv[:, m, t * TT:(t + 1) * TT], writes=[xk])
            out_fn(m, xs, xk, rs)

    def norm_to_hT(st, t, gcol0):
        hT = st['hT']

        def o(m, xs, xk, rs):
            S.op('dve', lambda: E['dve'].scalar_tensor_tensor(out=hT[:, m, :], in0=xs[:], scalar=colv(gcol0 + m),
                                                              in1=rs[:], op0=ALU.mult, op1=ALU.mult),
                 reads=[xk, 'rstd', 'cols'], writes=[('hT', m)])
        rms_norm(st, t, gcol0, o)

    def resid_epi(st, t, scale):
        def epi(bi, pss):
            for sub, (ps, pk) in enumerate(pss):
                m = bi * len(pss) + sub
                xs, xk = st['xs'].get()
                S.dma('sp', xs[:], xTv[:, m, t * TT:(t + 1) * TT], writes=[xk])
                S.op('dve', lambda: E['dve'].scalar_tensor_tensor(out=xs[:], in0=ps[:], scalar=scale, in1=xs[:],
                                                                  op0=ALU.mult, op1=ALU.add),
                     reads=[pk, xk], writes=[xk])
                S.dma('sp', xTv[:, m, t * TT:(t + 1) * TT], xs[:], reads=[xk])
        return epi

    def ffn(st, l, t, which):
        wg, wu, wd = ("wg1", "wu1", "wd1") if which == 1 else ("wg2", "wu2", "wd2")
        norm_to_hT(st, t, c_norm(0 if which == 1 else 2, l))
        hT = st['hT']
        aT = st['aT']
        wbg, _, _ = WB[wg]
        wbu, _, _ = WB[wu]
        PF = 1
        nb = DFF // 256

        def loadgu(bi):
            wt, wk = st['wA'].get()
            S.dma('sp', wt[:], wbg[l][bi], writes=[wk])
            wt2, wk2 = st['wA'].get()
            S.dma('sp', wt2[:], wbu[l][bi], writes=[wk2])
            return (wt, wk, wt2, wk2)
        q = [loadgu(bi) for bi in range(min(PF, nb))]
        for bi in range(nb):
            if bi + PF < nb:
                q.append(loadgu(bi + PF))
            wt, wk, wt2, wk2 = q.pop(0)
            fill()
            for sub in range(2):
                m = bi * 2 + sub
                pg, pgk = ps_pool.get()
                pu, puk = ps_pool.get()
                for (ps, pk, w_, wk_) in ((pg, pgk, wt, wk), (pu, puk, wt2, wk2)):
                    for kc in range(16):
                        S.op('pe', lambda: E['pe'].matmul(ps[:], lhsT=w_[:, kc, sub * 128:(sub + 1) * 128],
                                                          rhs=hT[:, kc, :], start=(kc == 0), stop=(kc == 15)),
                             reads=[wk_, ('hT', kc)], writes=[pk], inc=(kc == 15))
                sg, sgk = st['f32'].get()
                S.op('act', lambda: E['act'].activation(out=sg[:], in_=pg[:], func=AF.Silu), reads=[pgk], writes=[sgk])
                S.op('dve', lambda: E['dve'].tensor_tensor(out=aT[:, m, :], in0=sg[:], in1=pu[:], op=ALU.mult),
                     reads=[sgk, puk], writes=[('aT', m)])
        gemm_blocks(st, wd, l, 16, lambda kc: (aT[:, kc, :], ('aT', kc)), 44, resid_epi(st, t, 0.5), st['wB'])

    def gemm_scope():
        stack = ExitStack()
        st = {'stack': stack}
        st['hT'] = sb("hT", [128, 16, TT], BF16, stack)
        st['aT'] = sb("aT", [128, 44, TT], BF16, stack)
        st['wA'] = Pool([sb("wA%d" % i, [128, 16, 256], BF16, stack) for i in range(4)], "wA")
        st['wB'] = Pool([sb("wB%d" % i, [128, 44, 128], BF16, stack) for i in range(3)], "wB")
        st['xs'] = Pool([sb("xs%d" % i, [128, TT], F32, stack) for i in range(4)], "xs")
        st['sq'] = Pool([sb("sq%d" % i, [128, TT], BF16, stack) for i in range(2)], "sq")
        st['f32'] = Pool([sb("f32_%d" % i, [128, TT], F32, stack) for i in range(6)], "f32")
        st['ob'] = Pool([sb("ob%d" % i, [128, TT], BF16, stack) for i in range(6)], "ob")
        st['rstd'] = sb("rstd", [128, TT], F32, stack)
        st['rot'] = Pool([sb("rot%d" % i, [128, 2, TT], F32, stack) for i in range(2)], "rot")
        st['g3'] = Pool([sb("g3_%d" % i, [128, 3, TT], BF16, stack) for i in range(2)], "g3")
        return st

    def io_scope():
        stack = ExitStack()
        st = {'stack': stack}
        st['xs'] = Pool([sb("ixs%d" % i, [128, TT], F32, stack) for i in range(4)], "xs")
        st['sq'] = Pool([sb("isq%d" % i, [128, TT], BF16, stack) for i in range(2)], "sq")
        st['rstd'] = sb("irstd", [128, TT], F32, stack)
        st['xrow'] = sb("xrow", [128, D], F32, stack)
        st['stg'] = sb("stg", [128, 16, TT], F32, stack)
        return st

    def phase_in(st):
        stg = st['stg']
        xr = st['xrow']
        for t in range(NT):
            for s4 in range(4):
                S.dma('sp', xr[:], x_in[t * TT + s4 * 128:t * TT + (s4 + 1) * 128, :], writes=['xrow'])
                for mg in range(4):
                    ps, pk = ps_pool.get()
                    for j in range(4):
                        m = mg * 4 + j
                        S.op('pe', lambda: E['pe'].transpose(ps[:, j * 128:(j + 1) * 128], xr[:, m * 128:(m + 1) * 128], ident_f[:]),
                             reads=['xrow', 'identf'], writes=[pk], inc=(j == 3))
                    S.op('act', lambda: E['act'].activation(
                        out=stg[:, mg * 4:(mg + 1) * 4, s4 * 128:(s4 + 1) * 128],
                        in_=ps[:].rearrange("p (j c) -> p j c", j=4), func=AF.Copy),
                        reads=[pk], writes=['stg'])
            S.dma('sp', xTv[:, :, t * TT:(t + 1) * TT], stg[:], reads=['stg'])
        S.barrier()

    def phase_inproj(st, l, t):
        norm_to_hT(st, t, c_norm(1, l))
        hT = st['hT']
        tsl = slice(t * TT, (t + 1) * TT)

        def store(dst, m_local, ob, obk):
            S.dma('sp', dst[m_local * 128:(m_local + 1) * 128, tsl], ob[:], reads=[obk])

        def epi(bi, pss):
            (p0, k0), (p1, k1) = pss
            if bi < 8:
                isq = bi < 4
                h = bi if isq else bi - 4
                tab = rotq_in if isq else rotk_in
                dst = qrT if isq else krT
                rt, rtk = st['rot'].get()
                S.dma('sp', rt[:], tab[h, :, :, tsl].rearrange("c p t -> p c t"), writes=[rtk])
                t1, t1k = st['f32'].get()
                t2, t2k = st['f32'].get()
                S.op('act', lambda: E['act'].activation(out=t1[:], in_=p0[:], func=AF.Copy), reads=[k0], writes=[t1k])
                S.op('act', lambda: E['act'].activation(out=t2[:], in_=p1[:], func=AF.Copy), reads=[k1], writes=[t2k])
                a, ak = st['f32'].get()
                b, bk = st['f32'].get()
                o1, o1k = st['ob'].get()
                o2, o2k = st['ob'].get()
                S.op('pool', lambda: E['pool'].tensor_tensor(out=a[:], in0=t1[:], in1=rt[:, 0, :], op=ALU.mult), reads=[t1k, rtk], writes=[ak])
                S.op('pool', lambda: E['pool'].tensor_tensor(out=b[:], in0=t2[:], in1=rt[:, 1, :], op=ALU.mult), reads=[t2k, rtk], writes=[bk])
                S.op('pool', lambda: E['pool'].tensor_tensor(out=o1[:], in0=a[:], in1=b[:], op=ALU.subtract), reads=[ak, bk], writes=[o1k])
                S.op('dve', lambda: E['dve'].tensor_tensor(out=t1[:], in0=t1[:], in1=rt[:, 1, :], op=ALU.mult), reads=[t1k, rtk, ak], writes=[t1k])
                S.op('dve', lambda: E['dve'].tensor_tensor(out=t2[:], in0=t2[:], in1=rt[:, 0, :], op=ALU.mult), reads=[t2k, rtk, bk], writes=[t2k])
                S.op('dve', lambda: E['dve'].tensor_tensor(out=o2[:], in0=t1[:], in1=t2[:], op=ALU.add), reads=[t1k, t2k], writes=[o2k])
                store(dst, 2 * h, o1, o1k)
                store(dst, 2 * h + 1, o2, o2k)
                return
            for sub, (ps, pk) in enumerate(pss):
                m = bi * 2 + sub
                if m < 32:
                    ob, obk = st['ob'].get()
                    if sub == 0:
                        S.op('act', lambda: E['act'].activation(out=ob[:], in_=ps[:], func=AF.Copy), reads=[pk], writes=[obk])
                    else:
                        S.op('dve', lambda: E['dve'].tensor_copy(out=ob[:], in_=ps[:]), reads=[pk], writes=[obk])
                    store(rvT, m - 16, ob, obk)
                elif m < 48:
                    ob, obk = st['ob'].get()
                    S.op('act', lambda: E['act'].activation(out=ob[:], in_=ps[:], func=AF.Silu), reads=[pk], writes=[obk])
                    store(rgT, m - 32, ob, obk)
                elif m < 64:
                    xs_, xk_ = st['f32'].get()
                    u, uk = st['f32'].get()
                    S.op('act', lambda: E['act'].activation(out=xs_[:], in_=ps[:], func=AF.Copy), reads=[pk], writes=[xk_])
                    S.op('act', lambda: E['act'].activation(out=u[:], in_=ps[:], func=AF.Square), reads=[pk], writes=[uk])
                    S.op('dve', lambda: E['dve'].tensor_scalar(out=u[:], in0=u[:], scalar1=0.044715, scalar2=1.0, op0=ALU.mult, op1=ALU.add),
                         reads=[uk], writes=[uk])
                    S.op('dve', lambda: E['dve'].tensor_tensor(out=u[:], in0=u[:], in1=xs_[:], op=ALU.mult), reads=[uk, xk_], writes=[uk])
                    S.op('act', lambda: E['act'].activation(out=u[:], in_=u[:], func=AF.Sigmoid, scale=1.5957691216057308),
                         reads=[uk], writes=[uk])
                    if m < 56:
                        ob, obk = st['ob'].get()
                        S.op('dve', lambda: E['dve'].tensor_tensor(out=ob[:], in0=u[:], in1=xs_[:], op=ALU.mult), reads=[uk, xk_], writes=[obk])
                        store(suT, m - 48, ob, obk)
                    else:
                        S.op('dve', lambda: E['dve'].tensor_tensor(out=u[:], in0=u[:], in1=xs_[:], op=ALU.mult), reads=[uk, xk_], writes=[uk])
                        store(svT, m - 56, u, uk)
                elif m < 88:
                    ob, obk = st['ob'].get()
                    sc = (128 ** -0.5) if m < 72 else 1.0
                    dst = aqT if m < 72 else (akT if m < 80 else avT)
                    mb = 64 if m < 72 else (72 if m < 80 else 80)
                    S.op('act', lambda: E['act'].activation(out=ob[:], in_=ps[:], func=AF.Copy, scale=sc), reads=[pk], writes=[obk])
                    store(dst, m - mb, ob, obk)
                else:
                    ob, obk = st['ob'].get()
                    S.op('act', lambda: E['act'].activation(out=ob[:], in_=ps[:], func=AF.Sigmoid, bias=colv(C_BG + l * 48 + (m - 88)), scale=1.0),
                         reads=[pk, 'cols'], writes=[obk])
                    store(gT, m - 88, ob, obk)
        gemm_blocks(st, "win", l, 68, lambda kc: (hT[:, kc, :], ('hT', kc)), 16, epi, st['wA'])

    def phase_proj(st, l, t):
        tsl = slice(t * TT, (t + 1) * TT)
        rt_ = st['hT']
        aT = st['aT']
        sgt = aT[:, 0:8, :]
        att = aT[:, 8:16, :]
        mg = aT[:, 16:32, :]
        S.dma('sp', rt_[:], retoT.rearrange("(m p) t -> p m t", p=128)[:, :, tsl], writes=[('hT', k) for k in range(16)])
        S.dma('sp', sgt, sgoT.rearrange("(m p) t -> p m t", p=128)[:, :, tsl], writes=[('aT', k) for k in range(8)])
        S.dma('sp', att, attoT.rearrange("(m p) t -> p m t", p=128)[:, :, tsl], writes=[('aT', 8 + k) for k in range(8)])
        gTv = gT.rearrange("(b m p) t -> p b m t", b=3, p=128)
        wbr, _, _ = WB["wpr"]
        wbs, _, _ = WB["wps"]
        wba, _, _ = WB["wpa"]
        for bi in range(8):
            w1, w1k = st['wA'].get()
            S.dma('sp', w1[:], wbr[l][bi], writes=[w1k])
            w2, w2k = st['wA'].get()
            S.dma('sp', w2[:, 0:8, :], wbs[l][bi], writes=[w2k])
            S.dma('sp', w2[:, 8:16, :], wba[l][bi], writes=[w2k])
            for sub in range(2):
                m = bi * 2 + sub
                pa, pak = ps_pool.get()
                pb, pbk = ps_pool.get()
                pc, pck = ps_pool.get()
                for kc in range(16):
                    S.op('pe', lambda: E['pe'].matmul(pa[:], lhsT=w1[:, kc, sub * 128:(sub + 1) * 128], rhs=rt_[:, kc, :],
                                                      start=(kc == 0), stop=(kc == 15)),
                         reads=[w1k, ('hT', kc)], writes=[pak], inc=(kc == 15))
                for kc in range(8):
                    S.op('pe', lambda: E['pe'].matmul(pb[:], lhsT=w2[:, kc, sub * 128:(sub + 1) * 128], rhs=sgt[:, kc, :],
                                                      start=(kc == 0), stop=(kc == 7)),
                         reads=[w2k, ('aT', kc)], writes=[pbk], inc=(kc == 7))
                for kc in range(8):
                    S.op('pe', lambda: E['pe'].matmul(pc[:], lhsT=w2[:, 8 + kc, sub * 128:(sub + 1) * 128], rhs=att[:, kc, :],
                                                      start=(kc == 0), stop=(kc == 7)),
                         reads=[w2k, ('aT', 8 + kc)], writes=[pck], inc=(kc == 7))
                g3, g3k = st['g3'].get()
                S.dma('sp', g3[:], gTv[:, :, m, tsl], writes=[g3k])
                t1, t1k = st['f32'].get()
                t2, t2k = st['f32'].get()
                t3, t3k = st['f32'].get()
                S.op('dve', lambda: E['dve'].tensor_tensor(out=t1[:], in0=pa[:], in1=g3[:, 0, :], op=ALU.mult), reads=[pak, g3k], writes=[t1k])
                S.op('dve', lambda: E['dve'].tensor_tensor(out=t2[:], in0=pb[:], in1=g3[:, 1, :], op=ALU.mult), reads=[pbk, g3k], writes=[t2k])
                S.op('dve', lambda: E['dve'].tensor_tensor(out=t3[:], in0=pc[:], in1=g3[:, 2, :], op=ALU.mult), reads=[pck, g3k], writes=[t3k])
                S.op('pool', lambda: E['pool'].tensor_tensor(out=t1[:], in0=t1[:], in1=t2[:], op=ALU.add), reads=[t1k, t2k], writes=[t1k])
                S.op('pool', lambda: E['pool'].tensor_tensor(out=mg[:, m, :], in0=t1[:], in1=t3[:], op=ALU.add), reads=[t1k, t3k], writes=[('aT', 16 + m)])
        gemm_blocks(st, "wo", l, 8, lambda kc: (mg[:, kc, :], ('aT', 16 + kc)), 16, resid_epi(st, t, 1.0), st['wA'])

    def phase_out(st):
        stg = st['stg']
        orow = st['xrow']
        for t in range(NT):
            def o(m, xs, xk, rs):
                S.op('dve', lambda: E['dve'].scalar_tensor_tensor(out=stg[:, m, :], in0=xs[:], scalar=colv(C_FINAL + m),
                                                                  in1=rs[:], op0=ALU.mult, op1=ALU.mult),
                     reads=[xk, 'rstd', 'cols'], writes=[('stg', m)])
            rms_norm(st, t, C_FINAL, o)
            for s4 in range(4):
                for mg_ in range(4):
                    ps, pk = ps_pool.get()
                    for j in range(4):
                        m = mg_ * 4 + j
                        S.op('pe', lambda: E['pe'].transpose(ps[:, j * 128:(j + 1) * 128], stg[:, m, s4 * 128:(s4 + 1) * 128], ident_f[:]),
                             reads=[('stg', m), 'identf'], writes=[pk], inc=(j == 3))
                    S.op('act', lambda: E['act'].activation(out=orow[:, mg_ * 512:(mg_ + 1) * 512], in_=ps[:], func=AF.Copy),
                         reads=[pk], writes=['xrow'])
                S.dma('sp', y_out[t * TT + s4 * 128:t * TT + (s4 + 1) * 128, :], orow[:], reads=['xrow'])
            S.barrier()

    def phase_ret(l):
        stack = ExitStack()
        SC = 256
        qv = qrT.rearrange("(m p) t -> p m t", p=128)
        kv = krT.rearrange("(m p) t -> p m t", p=128)
        vv = rvT.rearrange("(m p) t -> p m t", p=128)
        gv = rgT.rearrange("(m p) t -> p m t", p=128)
        ov = retoT.rearrange("(m p) t -> p m t", p=128)
        qs = Pool([sb("rq%d" % i, [128, 8, SC], BF16, stack) for i in range(2)], "rq")
        ks = Pool([sb("rk%d" % i, [128, 8, SC], BF16, stack) for i in range(2)], "rk")
        vs = Pool([sb("rv%d" % i, [128, 16, SC], BF16, stack) for i in range(2)], "rv")
        gs = Pool([sb("rg%d" % i, [128, 16, SC], BF16, stack) for i in range(2)], "rg")
        os_ = Pool([sb("ro%d" % i, [128, 16, SC], BF16, stack) for i in range(2)], "ro")
        R32 = [sb("R32_%d" % h, [128, 2, 512], F32, stack) for h in range(4)]
        Rbf = [sb("Rbf_%d" % h, [128, 2, 512], BF16, stack) for h in range(4)]
        kh = Pool([sb("kh%d" % i, [128, 256], BF16, stack) for i in range(4)], "kh")
        vt = Pool([sb("vt%d" % i, [128, 512], BF16, stack) for i in range(4)], "vt")
        At = Pool([sb("At%d" % i, [128, 128], BF16, stack) for i in range(4)], "At")
        osb = Pool([sb("osb%d" % i, [128, 512], F32, stack) for i in range(4)], "osb")
        sqb = Pool([sb("sqb%d" % i, [128, 512], BF16, stack) for i in range(4)], "sqb")
        rin = Pool([sb("rin%d" % i, [128, 128], F32, stack) for i in range(4)], "rin")
        for h in range(4):
            S.op('pool', lambda: E['pool'].memset(R32[h][:], 0.0), writes=[('R32', h)])
            S.op('pool', lambda: E['pool'].memset(Rbf[h][:], 0.0), writes=[('Rbf', h)])
        bA = [(PSB[0], ('psb', 0)), (PSB[1], ('psb', 1))]
        bB = [(PSB[2], ('psb', 2)), (PSB[3], ('psb', 3))]
        bC = [(PSB[4], ('psb', 4)), (PSB[5], ('psb', 5))]
        bU = [(PSB[6], ('psb', 6)), (PSB[7], ('psb', 7))]
        for sc in range(T // SC):
            tsl = slice(sc * SC, (sc + 1) * SC)
            q_, qk = qs.get()
            k_, kk = ks.get()
            v_, vk = vs.get()
            g_, gk = gs.get()
            o_, ok = os_.get()
            S.dma('sp', q_[:], qv[:, :, tsl], writes=[qk])
            S.dma('sp', k_[:], kv[:, :, tsl], writes=[kk])
            S.dma('sp', v_[:], vv[:, :, tsl], writes=[vk])
            S.dma('sp', g_[:], gv[:, :, tsl], writes=[gk])
            for cc in range(SC // 128):
                csl = slice(cc * 128, (cc + 1) * 128)
                for hp in range(2):
                    U_ = []
                    for ui, h in enumerate((2 * hp, 2 * hp + 1)):
                        c = {'h': h}
                        c['gC'] = float(np.float32(np.exp(np.float32(128.0) * np.log1p(np.float32(-(2.0 ** (-5.0 - h)))))))
                        c['pA'], c['pAk'] = bA[ui]
                        c['pB'], c['pBk'] = bB[ui]
                        c['pC'], c['pCk'] = bC[ui]
                        c['pU'], c['pUk'] = bU[ui]
                        c['pAb'] = c['pA'][:].bitcast(BF16)
                        U_.append(c)
                    for c in U_:
                        h = c['h']
                        pAb, pAk, pB, pBk = c['pAb'], c['pAk'], c['pB'], c['pBk']
                        for dch in range(2):
                            S.op('pe', lambda: E['pe'].transpose(pAb[:, dch * 128:(dch + 1) * 128], k_[:, 2 * h + dch, csl], ident_b[:]),
                                 reads=[kk, 'identb'], writes=[pAk], inc=False)
                        for ech in range(4):
                            S.op('pe', lambda: E['pe'].transpose(pAb[:, 256 + ech * 128:256 + (ech + 1) * 128], v_[:, 4 * h + ech, csl], ident_b[:]),
                                 reads=[vk, 'identb'], writes=[pAk], inc=(ech == 3))
                        for dch in range(2):
                            S.op('pe', lambda: E['pe'].matmul(pB[:, 0:128], lhsT=k_[:, 2 * h + dch, csl], rhs=q_[:, 2 * h + dch, csl],
                                                              start=(dch == 0), stop=(dch == 1)),
                                 reads=[kk, qk], writes=[pBk], inc=(dch == 1))
                    for c in U_:
                        pAb, pAk, pB, pBk = c['pAb'], c['pAk'], c['pB'], c['pBk']
                        c['kh'], c['khk'] = kh.get()
                        c['vt'], c['vtk'] = vt.get()
                        c['A'], c['Ak'] = At.get()
                        kh_, vt_, A_ = c['kh'], c['vt'], c['A']
                        gC = c['gC']
                        S.op('act', lambda: E['act'].activation(out=kh_[:], in_=pAb[:, 0:256], func=AF.Copy, scale=gC), reads=[pAk], writes=[c['khk']])
                        S.op('act', lambda: E['act'].activation(out=vt_[:], in_=pAb[:, 256:768], func=AF.Copy), reads=[pAk], writes=[c['vtk']])
                        S.op('dve', lambda: E['dve'].tensor_tensor(out=A_[:], in0=pB[:, 0:128], in1=cmask[:], op=ALU.mult),
                             reads=[pBk, 'cmask'], writes=[c['Ak']])
                    for c in U_:
                        h = c['h']
                        pC, pCk, vt_, A_, kh_ = c['pC'], c['pCk'], c['vt'], c['A'], c['kh']
                        for ech in range(4):
                            oc = pC[:, ech * 128:(ech + 1) * 128]
                            S.op('pe', lambda: E['pe'].matmul(oc, lhsT=vt_[:, ech * 128:(ech + 1) * 128], rhs=A_[:], start=True, stop=False),
                                 reads=[c['vtk'], c['Ak']], writes=[pCk], inc=False)
                            for dch in range(2):
                                S.op('pe', lambda: E['pe'].matmul(oc, lhsT=Rbf[h][:, dch, ech * 128:(ech + 1) * 128], rhs=q_[:, 2 * h + dch, csl],
                                                                  start=False, stop=(dch == 1)),
                                     reads=[('Rbf', h), qk], writes=[pCk], inc=(dch == 1 and ech == 3))
                    for c in U_:
                        pC, pCk = c['pC'], c['pCk']
                        c['ob'], c['obk'] = osb.get()
                        c['sq'], c['sqk'] = sqb.get()
                        ob_, sq_ = c['ob'], c['sq']
                        S.op('act', lambda: E['act'].activation(out=ob_[:], in_=pC[:], func=AF.Copy), reads=[pCk], writes=[c['obk']])
                        S.op('act', lambda: E['act'].activation(out=sq_[:], in_=pC[:], func=AF.Square), reads=[pCk], writes=[c['sqk']])
                    for c in U_:
                        pB, pBk, sq_ = c['pB'], c['pBk'], c['sq']
                        for ech in range(4):
                            S.op('pe', lambda: E['pe'].matmul(pB[:, 128:256], lhsT=ones_b[:], rhs=sq_[:, ech * 128:(ech + 1) * 128],
                                                              start=(ech == 0), stop=(ech == 3)),
                                 reads=['onesb', c['sqk']], writes=[pBk], inc=(ech == 3))
                    for dch in range(2):
                        for c in U_:
                            h = c['h']
                            pU, pUk, kh_, vt_ = c['pU'], c['pUk'], c['kh'], c['vt']
                            gC = c['gC']
                            S.op('pe', lambda: E['pe'].matmul(pU[:], lhsT=kh_[:, dch * 128:(dch + 1) * 128], rhs=vt_[:], start=True, stop=True),
                                 reads=[c['khk'], c['vtk']], writes=[pUk], inc=True)
                            S.op('dve', lambda: E['dve'].scalar_tensor_tensor(out=R32[h][:, dch, :], in0=R32[h][:, dch, :], scalar=gC,
                                                                              in1=pU[:], op0=ALU.mult, op1=ALU.add),
                                 reads=[pUk, ('R32', h)], writes=[('R32', h)])
                    for c in U_:
                        h = c['h']
                        pB, pBk, ob_ = c['pB'], c['pBk'], c['ob']
                        ri, rik = rin.get()
                        S.op('act', lambda: E['act'].activation(out=Rbf[h][:], in_=R32[h][:], func=AF.Copy),
                             reads=[('R32', h)], writes=[('Rbf', h)])
                        S.op('act', lambda: E['act'].activation(out=ri[:], in_=pB[:, 128:256], func=AF.Sqrt, bias=epsc[:], scale=1.0 / 512),
                             reads=[pBk, 'epsc'], writes=[rik])
                        S.op('dve', lambda: E['dve'].reciprocal(out=ri[:], in_=ri[:]), reads=[rik], writes=[rik])
                        S.op('dve', lambda: E['dve'].tensor_tensor(out=ob_[:].rearrange("p (e i) -> p e i", e=4),
                                                                   in0=ob_[:].rearrange("p (e i) -> p e i", e=4),
                                                                   in1=ri[:].unsqueeze(1).broadcast_to([128, 4, 128]), op=ALU.mult),
                             reads=[c['obk'], rik], writes=[c['obk']])
                        S.op('pool', lambda: E['pool'].tensor_tensor(out=o_[:, 4 * h:4 * h + 4, csl],
                                                                     in0=ob_[:].rearrange("p (e i) -> p e i", e=4),
                                                                     in1=g_[:, 4 * h:4 * h + 4, csl], op=ALU.mult),
                             reads=[c['obk'], gk], writes=[ok])
            S.dma('sp', ov[:, :, tsl], o_[:], reads=[ok])
        S.barrier()
        stack.close()

    def phase_sgu(l):
        stack = ExitStack()
        SC = 256
        uv = suT.rearrange("(m p) t -> p m t", p=128)
        vv = svT.rearrange("(m p) t -> p m t", p=128)
        ov = sgoT.rearrange("(m p) t -> p m t", p=128)
        us = Pool([sb("su%d" % i, [128, 8, SC], BF16, stack) for i in range(2)], "su")
        vs = Pool([sb("sv%d" % i, [128, 8, SC], F32, stack) for i in range(2)], "sv")
        os_ = Pool([sb("so%d" % i, [128, 8, SC], BF16, stack) for i in range(2)], "so")
        lng = sb("lng", [128, 1024], F32, stack)
        lnb = sb("lnb", [128, 1024], F32, stack)
        wcf = sb("wcf", [128, 4, 128], F32, stack)
        wcb = sb("wcb", [128, 4, 128], BF16, stack)
        sgb = sb("sgb", [128, 4, 128], F32, stack)
        xn = Pool([sb("xn%d" % i, [128, 1024], F32, stack) for i in range(2)], "xn")
        vb = Pool([sb("vb%d" % i, [128, 1024], BF16, stack) for i in range(2)], "vb")
        stt = Pool([sb("stt%d" % i, [128, 16], F32, stack) for i in range(2)], "stt")
        tm = Pool([sb("tm%d" % i, [128, 8, 128], F32, stack) for i in range(2)], "tm")
        S.dma('sp', lng[:], lng_in[l].partition_broadcast(128), writes=['lng'])
        S.dma('sp', lnb[:], lnb_in[l].partition_broadcast(128), writes=['lnb'])
        S.dma('sp', wcf[:], sgwT_in[l].rearrange("g s t -> s g t"), writes=['wcf'])
        S.dma('sp', sgb[:], sgb_in[l].partition_broadcast(128), writes=['sgb'])
        S.op('dve', lambda: E['dve'].tensor_tensor(out=wcb[:], in0=wcf[:], in1=cmask[:].unsqueeze(1).broadcast_to([128, 4, 128]), op=ALU.mult),
             reads=['wcf', 'cmask'], writes=['wcb'])
        for sc in range(T // SC):
            tsl = slice(sc * SC, (sc + 1) * SC)
            u_, uk = us.get()
            v_, vk = vs.get()
            o_, ok = os_.get()
            S.dma('sp', u_[:], uv[:, :, tsl], writes=[uk])
            S.dma('sp', v_[:], vv[:, :, tsl], writes=[vk])
            U_ = []
            for cc in range(SC // 128):
                c = {'csl': slice(cc * 128, (cc + 1) * 128)}
                b0 = (cc % 2) * 4
                c['p'] = [(PSB[b0 + i], ('psb', b0 + i)) for i in range(4)]
                U_.append(c)
            for c in U_:
                (p0, p0k), (p1, p1k) = c['p'][0], c['p'][1]
                for f in range(8):
                    pp, ppk = (p0, p0k) if f < 4 else (p1, p1k)
                    S.op('pe', lambda: E['pe'].transpose(pp[:, (f % 4) * 128:(f % 4 + 1) * 128], v_[:, f, c['csl']], ident_f[:]),
                         reads=[vk, 'identf'], writes=[ppk], inc=(f % 4 == 3))
            for c in U_:
                (p0, p0k), (p1, p1k) = c['p'][0], c['p'][1]
                c['st'], c['stk'] = stt.get()
                st_, stk = c['st'], c['stk']
                S.op('dve', lambda: E['dve'].bn_stats(out=st_[:, 0:6], in_=p0[:]), reads=[p0k], writes=[stk])
                S.op('dve', lambda: E['dve'].bn_stats(out=st_[:, 6:12], in_=p1[:]), reads=[p1k, stk], writes=[stk])
                S.op('dve', lambda: E['dve'].bn_aggr(out=st_[:, 12:14], in_=st_[:, 0:12]), reads=[stk], writes=[stk])
                S.op('act', lambda: E['act'].activation(out=st_[:, 14:15], in_=st_[:, 13:14], func=AF.Sqrt, bias=epsc[:], scale=1.0),
                     reads=[stk, 'epsc'], writes=[stk])
            for c in U_:
                (p0, p0k), (p1, p1k) = c['p'][0], c['p'][1]
                st_, stk = c['st'], c['stk']
                S.op('dve', lambda: E['dve'].reciprocal(out=st_[:, 15:16], in_=st_[:, 14:15]), reads=[stk], writes=[stk])
                c['xn'], c['xnk'] = xn.get()
                xn_, xnk = c['xn'], c['xnk']
                for half, (pp, ppk) in enumerate(((p0, p0k), (p1, p1k))):
                    S.op('dve', lambda: E['dve'].tensor_scalar(out=xn_[:, half * 512:(half + 1) * 512], in0=pp[:], scalar1=st_[:, 12:13],
                                                               scalar2=st_[:, 15:16], op0=ALU.subtract, op1=ALU.mult),
                         reads=[ppk, stk], writes=[xnk])
            for c in U_:
                xn_, xnk = c['xn'], c['xnk']
                c['vb'], c['vbk'] = vb.get()
                vb_, vbk = c['vb'], c['vbk']
                S.op('pool', lambda: E['pool'].tensor_tensor(out=xn_[:], in0=xn_[:], in1=lng[:], op=ALU.mult), reads=[xnk, 'lng'], writes=[xnk])
                S.op('pool', lambda: E['pool'].tensor_tensor(out=vb_[:], in0=xn_[:], in1=lnb[:], op=ALU.add), reads=[xnk, 'lnb'], writes=[vbk])
            for c in U_:
                (p2, p2k), (p3, p3k) = c['p'][2], c['p'][3]
                vb_, vbk = c['vb'], c['vbk']
                for f in range(8):
                    pp, ppk = (p2, p2k) if f < 4 else (p3, p3k)
                    S.op('pe', lambda: E['pe'].matmul(pp[:, (f % 4) * 128:(f % 4 + 1) * 128], lhsT=vb_[:, f * 128:(f + 1) * 128], rhs=wcb[:, f // 2, :],
                                                      start=True, stop=True),
                         reads=[vbk, 'wcb'], writes=[ppk], inc=(f % 4 == 3))
            for c in U_:
                (p2, p2k), (p3, p3k) = c['p'][2], c['p'][3]
                c['tm'], c['tmk'] = tm.get()
                tm_, tmk = c['tm'], c['tmk']
                for half, (pp, ppk) in enumerate(((p2, p2k), (p3, p3k))):
                    S.op('dve', lambda: E['dve'].tensor_tensor(
                        out=tm_[:, half * 4:(half + 1) * 4, :].rearrange("p (g two) t -> p g two t", two=2),
                        in0=pp[:].rearrange("p (g two t) -> p g two t", g=2, two=2),
                        in1=sgb[:, 2 * half:2 * half + 2, :].unsqueeze(2).broadcast_to([128, 2, 2, 128]), op=ALU.add),
                        reads=[ppk, 'sgb'], writes=[tmk])
            for c in U_:
                tm_, tmk = c['tm'], c['tmk']
                S.op('pool', lambda: E['pool'].tensor_tensor(out=o_[:, :, c['csl']], in0=tm_[:], in1=u_[:, :, c['csl']], op=ALU.mult),
                     reads=[tmk, uk], writes=[ok])
            S.dma('sp', ov[:, :, tsl], o_[:], reads=[ok])
        S.barrier()
        stack.close()

    def phase_att(l):
        stack = ExitStack()
        WIN = 2048
        NW = T // WIN
        qv = aqT.rearrange("(m p) t -> p m t", p=128)
        kv = akT.rearrange("(m p) t -> p m t", p=128)
        vv = avT.rearrange("(m p) t -> p m t", p=128)
        ov = attoT.rearrange("(m p) t -> p m t", p=128)
        Em = sb("Em", [128, 48, 128], F32, stack)
        qb = Pool([sb("aq%d" % i, [128, WIN], BF16, stack) for i in range(2)], "aq")
        kb = Pool([sb("ak%d" % i, [128, 2 * WIN], BF16, stack) for i in range(2)], "ak")
        vbf = Pool([sb("av%d" % i, [128, 2 * WIN], BF16, stack) for i in range(2)], "av")
        acc = Pool([sb("acc%d" % i, [128, 2, WIN], F32, stack) for i in range(2)], "acc")
        oo = Pool([sb("ao%d" % i, [128, WIN], BF16, stack) for i in range(2)], "ao")
        ex = Pool([sb("ex%d" % i, [128, 2, 128], F32, stack) for i in range(8)], "ex")
        Pb = Pool([sb("Pb%d" % i, [128, 2, 128], BF16, stack) for i in range(8)], "Pb")
        vtk_ = Pool([sb("avt%d" % i, [128, 2, 128], BF16, stack) for i in range(8)], "avt")
        S.dma('sp', Em[:], bm_in.rearrange("c h k j i -> j (c h k) i"), writes=['Em'])
        S.op('act', lambda: E['act'].activation(out=Em[:], in_=Em[:], func=AF.Exp), reads=['Em'], writes=['Em'])
        u = 0
        for h in range(8):
            for w in range(NW):
                q_, qk = qb.get()
                k_, kk = kb.get()
                v_, vk = vbf.get()
                a_, ak_ = acc.get()
                o_, ok = oo.get()
                w0 = w * WIN
                S.dma('sp', q_[:], qv[:, h, w0:w0 + WIN], writes=[qk])
                if w > 0:
                    S.dma('sp', k_[:], kv[:, h, w0 - WIN:w0 + WIN], writes=[kk])
                    S.dma('sp', v_[:], vv[:, h, w0 - WIN:w0 + WIN], writes=[vk])
                else:
                    S.dma('sp', k_[:, WIN:], kv[:, h, 0:WIN], writes=[kk])
                    S.dma('sp', v_[:, WIN:], vv[:, h, 0:WIN], writes=[vk])
                units = []
                for ci, dil in enumerate((1, 4, 16)):
                    span = 128 * dil
                    for nb_ in range(WIN // span):
                        for rho in range(dil):
                            units.append((ci, dil, span, nb_, rho))
                for g0 in range(0, len(units), 4):
                    U_ = []
                    for (ci, dil, span, nb_, rho) in units[g0:g0 + 4]:
                        c = {'ci': ci}
                        base = nb_ * span + rho
                        c['qsl'] = slice(base, base + 127 * dil + 1, dil)
                        has_prev = (w0 + nb_ * span) > 0
                        ksl_c = slice(WIN + base, WIN + base + 127 * dil + 1, dil)
                        ksl_p = slice(WIN + base - span, WIN + base - span + 127 * dil + 1, dil)
                        bs = (u % 4) * 2
                        u += 1
                        c['pS'], c['pSk'] = PSB[bs], ('psb', bs)
                        c['pO'], c['pOk'] = PSB[bs + 1], ('psb', bs + 1)
                        c['pVb'] = c['pS'][:].bitcast(BF16)[:, 512:768].rearrange("p (k e) -> p k e", k=2)
                        c['tiles'] = ([(0, ksl_p)] if has_prev else []) + [(1, ksl_c)]
                        c['p_lo'] = 0 if has_prev else 1
                        c['e0'] = (ci * 8 + h) * 2
                        U_.append(c)
                    for c in U_:
                        pS, pSk, pVb, tiles = c['pS'], c['pSk'], c['pVb'], c['tiles']
                        for (pc, ksl) in tiles:
                            S.op('pe', lambda: E['pe'].matmul(pS[:, pc * 128:(pc + 1) * 128], lhsT=k_[:, ksl], rhs=q_[:, c['qsl']], start=True, stop=True),
                                 reads=[kk, qk], writes=[pSk], inc=False)
                        for ii, (pc, ksl) in enumerate(tiles):
                            S.op('pe', lambda: E['pe'].transpose(pVb[:, pc, :], v_[:, ksl], ident_b[:]),
                                 reads=[vk, 'identb'], writes=[pSk], inc=(ii == len(tiles) - 1))
                    for c in U_:
                        pS, pSk, pVb, p_lo = c['pS'], c['pSk'], c['pVb'], c['p_lo']
                        c['ex'], c['exk'] = ex.get()
                        c['vt'], c['vtk'] = vtk_.get()
                        ex_, vt_ = c['ex'], c['vt']
                        S.op('act', lambda: E['act'].activation(out=ex_[:, p_lo:2, :], in_=pS[:, p_lo * 128:256].rearrange("p (k i) -> p k i", i=128), func=AF.Exp),
                             reads=[pSk], writes=[c['exk']])
                        S.op('act', lambda: E['act'].activation(out=vt_[:, p_lo:2, :], in_=pVb[:, p_lo:2, :], func=AF.Copy),
                             reads=[pSk], writes=[c['vtk']])
                    for c in U_:
                        p_lo, e0, ex_ = c['p_lo'], c['e0'], c['ex']
                        c['P'], c['Pk'] = Pb.get()
                        P_ = c['P']
                        S.op('pool', lambda: E['pool'].tensor_tensor(out=P_[:, p_lo:2, :], in0=ex_[:, p_lo:2, :], in1=Em[:, e0 + p_lo:e0 + 2, :], op=ALU.mult),
                             reads=[c['exk'], 'Em'], writes=[c['Pk']])
                    for c in U_:
                        pO, pOk, tiles, vt_, P_ = c['pO'], c['pOk'], c['tiles'], c['vt'], c['P']
                        for ii, (pc, ksl) in enumerate(tiles):
                            S.op('pe', lambda: E['pe'].matmul(pO[:, 0:128], lhsT=vt_[:, pc, :], rhs=P_[:, pc, :], start=(ii == 0), stop=(ii == len(tiles) - 1)),
                                 reads=[c['vtk'], c['Pk']], writes=[pOk], inc=False)
                        for ii, (pc, ksl) in enumerate(tiles):
                            S.op('pe', lambda: E['pe'].matmul(pO[:, 128:256], lhsT=ones_b[:], rhs=P_[:, pc, :], start=(ii == 0), stop=(ii == len(tiles) - 1)),
                                 reads=['onesb', c['Pk']], writes=[pOk], inc=(ii == len(tiles) - 1))
                    for c in U_:
                        pO, pOk, qsl = c['pO'], c['pOk'], c['qsl']
                        pOv = pO[:, 0:256].rearrange("p (k i) -> p k i", k=2)
                        if c['ci'] == 0:
                            S.op('dve', lambda: E['dve'].tensor_copy(out=a_[:, :, qsl], in_=pOv), reads=[pOk], writes=[ak_])
                        else:
                            S.op('dve', lambda: E['dve'].tensor_tensor(out=a_[:, :, qsl], in0=a_[:, :, qsl], in1=pOv, op=ALU.add),
                                 reads=[pOk, ak_], writes=[ak_])
                S.op('dve', lambda: E['dve'].reciprocal(out=a_[:, 1, :], in_=a_[:, 1, :]), reads=[ak_], writes=[ak_])
                S.op('pool', lambda: E['pool'].tensor_tensor(out=o_[:], in0=a_[:, 0, :], in1=a_[:, 1, :], op=ALU.mult), reads=[ak_], writes=[ok])
                S.dma('sp', ov[:, h, w0:w0 + WIN], o_[:], reads=[ok])
        S.barrier()
        stack.close()

    st = io_scope()
    phase_in(st)
    st['stack'].close()
    for l in range(L):
        st = gemm_scope()
        if l + 1 < L:
            filler[0] = cast_jobs(l + 1)
        for t in range(NT):
            ffn(st, l, t, 1)
            S.barrier()
            phase_inproj(st, l, t)
            S.barrier()
        drain_fill()
        S.barrier()
        st['stack'].close()
        phase_ret(l)
        phase_sgu(l)
        phase_att(l)
        st = gemm_scope()
        for t in range(NT):
            phase_proj(st, l, t)
            S.barrier()
            ffn(st, l, t, 2)
            S.barrier()
        st['stack'].close()
    st = io_scope()
    phase_out(st)
    st['stack'].close()
    es.close()
    return nc


def _t5_bucket_np(dist):
    max_exact = 16
    d_f = np.maximum(dist, 1).astype(np.float32)
    large = max_exact + (np.log(d_f / np.float32(max_exact)) / np.float32(math.log(2048 / max_exact))
                         * np.float32(32 - max_exact)).astype(np.int32)
    large = np.minimum(large, 31)
    return np.where(dist < max_exact, dist, large)


def host_consts(T, L, inp):
    c = {}
    colsl = []
    for nm in ("ffn1_norm", "mix_norm", "ffn2_norm"):
        for l in range(L):
            colsl.append(np.asarray(inp[nm][l]).reshape(16, 128).T)
    colsl.append(np.asarray(inp["final_norm"]).reshape(16, 128).T)
    for l in range(L):
        colsl.append(np.asarray(inp["b_gate"][l]).reshape(48, 128).T)
    c["cols"] = np.ascontiguousarray(np.concatenate(colsl, axis=1), dtype=np.float32)
    c["lng"] = np.ascontiguousarray(inp["sg_ln_g"][:L], dtype=np.float32)
    c["lnb"] = np.ascontiguousarray(inp["sg_ln_b"][:L], dtype=np.float32)
    c["sgwT"] = np.ascontiguousarray(np.swapaxes(np.asarray(inp["sg_w"][:L]), 2, 3), dtype=np.float32)
    c["sgb"] = np.ascontiguousarray(inp["sg_b"][:L], dtype=np.float32)
    rb = np.asarray(inp["rel_bias"], dtype=np.float32)
    bm = np.empty((3, 8, 2, 128, 128), np.float32)
    i = np.arange(128)[None, :]
    j = np.arange(128)[:, None]
    for ci, dil in enumerate((1, 4, 16)):
        for pc in range(2):
            steps = (128 + i - j) if pc == 0 else (i - j)
            valid = (steps >= 0) & (steps <= 128)
            bucket = _t5_bucket_np(dil * np.maximum(steps, 0))
            for h in range(8):
                bm[ci, h, pc] = np.where(valid, rb[bucket, h], np.float32(-30000.0))
    c["bm"] = bm
    pos = np.arange(T, dtype=np.float32)
    inv = (np.float32(10000.0) ** (-np.arange(0, 256, 2, dtype=np.float32) / np.float32(256))).astype(np.float32)
    ang = (pos[None, :] * inv[:, None]).astype(np.float32)
    cos, sin = np.cos(ang).astype(np.float32), np.sin(ang).astype(np.float32)
    rotq = np.empty((4, 2, 128, T), np.float32)
    rotk = np.empty((4, 2, 128, T), np.float32)
    pm = (np.arange(T) % 128).astype(np.float64)
    for h in range(4):
        lg = math.log1p(-(2.0 ** (-5.0 - h)))
        dq = np.exp((pm + 1.0) * lg)
        dk = np.exp(-(pm + 1.0) * lg) / 16.0
        rotq[h, 0] = cos * dq[None, :]
        rotq[h, 1] = sin * dq[None, :]
        rotk[h, 0] = cos * dk[None, :]
        rotk[h, 1] = sin * dk[None, :]
    c["rotq"] = rotq
    c["rotk"] = rotk
    c["cmask"] = np.triu(np.ones((128, 128), np.float32))
    c["ident"] = np.eye(128, dtype=np.float32)
    return c


_WMAP = {"wg1": "ffn1_w_gate", "wu1": "ffn1_w_up", "wd1": "ffn1_w_down", "win": "w_in", "wpr": "w_proj_ret",
         "wps": "w_proj_sg", "wpa": "w_proj_att", "wo": "w_out", "wg2": "ffn2_w_gate", "wu2": "ffn2_w_up",
         "wd2": "ffn2_w_down"}


def run(inp, T, L, ncores, nseq, dbg=False, trace=False):
    nc = build(T, L, dbg)
    c = host_consts(T, L, inp)
    base = dict(c)
    for k, v in _WMAP.items():
        base[k] = np.ascontiguousarray(np.asarray(inp[v])[:L], dtype=np.float32)
    in_maps = []
    for i in range(ncores):
        m = dict(base)
        m["x"] = np.ascontiguousarray(np.asarray(inp["x"])[i % nseq, :T], dtype=np.float32)
        in_maps.append(m)
    res = run_bass_kernel_spmd(nc, in_maps, core_ids=list(range(ncores)), **({"trace": True} if trace else {}))
    return res


def kernel(**inputs):
    res = run(inputs, SEQ, LDEPTH, 2, 2)
    out = np.stack([np.asarray(res.results[0]["y"]), np.asarray(res.results[1]["y"])], axis=0)
    return out.astype(np.float32)
```

```python
import math
import numpy as np
from contextlib import ExitStack
import concourse.bass as bass
import concourse.mybir as mybir
from concourse.bass_utils import run_bass_kernel_spmd

F32 = mybir.dt.float32
BF16 = mybir.dt.bfloat16
AF = mybir.ActivationFunctionType
ALU = mybir.AluOpType

D = 2048
DFF = 5632
DIN = 17408
TT = 512
EPS = 1e-6
NDS = 6
LDEPTH = 4
SEQ = 8192


class Sched:
    def __init__(s, nc, es):
        s.nc = nc
        s.E = {'pe': nc.tensor, 'act': nc.scalar, 'dve': nc.vector, 'pool': nc.gpsimd, 'sp': nc.sync}
        s.sem = {}
        s.cnt = {}
        for e in s.E:
            s.sem[e] = es.enter_context(nc.semaphore('s_' + e))
            s.cnt[e] = 0
        s.waited = {e: {} for e in s.E}
        s.hist = {}
        s.dq = {}
        s.dqi = {}
        for q in ('sp', 'act', 'pool'):
            s.dq[q] = [[es.enter_context(nc.semaphore('d_%s%d' % (q, i))), 0] for i in range(NDS)]
            s.dqi[q] = 0

    def _wait(s, e, toks):
        need = {}
        for t in toks:
            if t is None:
                continue
            sem, val = t
            k = id(sem)
            if e == 'pe' and sem is s.sem['pe']:
                continue
            if s.waited[e].get(k, 0) >= val:
                continue
            if k not in need or need[k][1] < val:
                need[k] = (sem, val)
        for k, (sem, val) in need.items():
            s.E[e].wait_ge(sem, val)
            s.waited[e][k] = val

    def _deps(s, reads, writes):
        toks = []
        for r in reads:
            h = s.hist.get(r)
            if h:
                toks.append(h[0])
        for w in writes:
            h = s.hist.get(w)
            if h:
                toks.append(h[0])
                toks.extend(h[1].values())
        return toks

    def _record(s, key, tok, reads, writes):
        for r in reads:
            h = s.hist.setdefault(r, [None, {}])
            h[1][key] = tok
        for w in writes:
            s.hist[w] = [tok, {}]

    def op(s, e, fn, reads=(), writes=(), inc=True):
        s._wait(e, s._deps(reads, writes))
        ins = fn()
        if inc:
            s.cnt[e] += 1
            ins.then_inc(s.sem[e], 1)
            tok = (s.sem[e], s.cnt[e])
        else:
            tok = (s.sem[e], s.cnt[e] + 1)
        s._record(e, tok, reads, writes)
        return tok

    def dma(s, q, out, in_, reads=(), writes=()):
        slot = s.dq[q][s.dqi[q] % NDS]
        s.dqi[q] += 1
        toks = s._deps(reads, writes)
        if slot[1] > 0:
            toks.append((slot[0], slot[1]))
        s._wait(q, toks)
        ins = s.E[q].dma_start(out=out, in_=in_)
        slot[1] += 16
        ins.then_inc(slot[0], 16)
        tok = (slot[0], slot[1])
        s._record(id(slot[0]), tok, reads, writes)
        return tok

    def barrier(s):
        toks = [(s.sem[e], s.cnt[e]) for e in s.E if s.cnt[e] > 0]
        for q in s.dq:
            for sl in s.dq[q]:
                if sl[1] > 0:
                    toks.append((sl[0], sl[1]))
        for e in s.E:
            s._wait(e, toks)
        s.hist.clear()


class Pool:
    def __init__(s, tiles, name):
        s.tiles = tiles
        s.name = name
        s.i = 0

    def get(s):
        k = s.i % len(s.tiles)
        s.i += 1
        return s.tiles[k], (s.name, k)


def build(T, L, dbg=False):
    NT = T // TT
    NCH = T // 128
    nc = bass.Bass("TRN2", target_bir_lowering=False)
    es = ExitStack()
    KIN = "ExternalInput"
    KSC = "ExternalOutput" if dbg else "Internal"

    def din(name, shape, dt=F32):
        return nc.dram_tensor(name, list(shape), dt, kind=KIN).ap()

    def dsc(name, shape, dt):
        return nc.dram_tensor(name, list(shape), dt, kind=KSC).ap()

    x_in = din("x", [T, D])
    W = {}
    wspec = [("wg1", D, DFF), ("wu1", D, DFF), ("wd1", DFF, D), ("win", D, DIN), ("wpr", 2048, D),
             ("wps", 1024, D), ("wpa", 1024, D), ("wo", D, D), ("wg2", D, DFF), ("wu2", D, DFF), ("wd2", DFF, D)]
    WB = {}
    for nm, K, N in wspec:
        W[nm] = din(nm, [L, K, N])
        cb = 128 if nm in ("wd1", "wd2") else 256
        WB[nm] = ([nc.dram_tensor("%sb%d" % (nm, l_), [N // cb, 128, K // 128, cb], BF16, kind="Internal").ap() for l_ in range(L)], K // 128, cb)
    NCOLS = (3 * L + 1) * 16 + L * 48
    cols_in = din("cols", [128, NCOLS])
    lng_in = din("lng", [L, 1024])
    lnb_in = din("lnb", [L, 1024])
    sgwT_in = din("sgwT", [L, 4, 128, 128])
    sgb_in = din("sgb", [L, 4, 128])
    bm_in = din("bm", [3, 8, 2, 128, 128])
    rotq_in = din("rotq", [4, 2, 128, T])
    rotk_in = din("rotk", [4, 2, 128, T])
    cmask_in = din("cmask", [128, 128])
    ident_in = din("ident", [128, 128])
    y_out = nc.dram_tensor("y", [T, D], F32, kind="ExternalOutput").ap()

    xT = dsc("xT", [D, T], F32)
    qrT = dsc("qrT", [1024, T], BF16)
    krT = dsc("krT", [1024, T], BF16)
    rvT = dsc("rvT", [2048, T], BF16)
    rgT = dsc("rgT", [2048, T], BF16)
    suT = dsc("suT", [1024, T], BF16)
    svT = dsc("svT", [1024, T], F32)
    aqT = dsc("aqT", [1024, T], BF16)
    akT = dsc("akT", [1024, T], BF16)
    avT = dsc("avT", [1024, T], BF16)
    gT = dsc("gT", [6144, T], BF16)
    retoT = dsc("retoT", [2048, T], BF16)
    sgoT = dsc("sgoT", [1024, T], BF16)
    attoT = dsc("attoT", [1024, T], BF16)

    S = Sched(nc, es)
    E = S.E

    uid = [0]

    def sb(name, shape, dt, stack=None):
        uid[0] += 1
        return (stack or es).enter_context(nc.sbuf_tensor("sb%d_%s" % (uid[0], name), list(shape), dt))

    PSB = [es.enter_context(nc.psum_tensor("psb%d" % i, [128, 512], F32)) for i in range(8)]
    ps_pool = Pool(PSB, "psb")

    cols = sb("cols", [128, NCOLS], F32)
    ident_f = sb("identf", [128, 128], F32)
    ident_b = sb("identb", [128, 128], BF16)
    ones_b = sb("onesb", [128, 128], BF16)
    cmask = sb("cmaskf", [128, 128], F32)
    epsc = sb("epsc", [128, 1], F32)
    S.dma('sp', cols[:], cols_in, writes=['cols'])
    S.dma('sp', ident_f[:], ident_in, writes=['identf'])
    S.dma('sp', cmask[:], cmask_in, writes=['cmask'])
    S.op('pool', lambda: E['pool'].memset(ones_b[:], 1.0), writes=['onesb'])
    S.op('pool', lambda: E['pool'].memset(epsc[:], EPS), writes=['epsc'])
    S.op('dve', lambda: E['dve'].tensor_copy(out=ident_b[:], in_=ident_f[:]), reads=['identf'], writes=['identb'])

    def cast_jobs(l):
        for nm, K, N in wspec:
            wb, KC, cb = WB[nm]
            for c0 in range(N // cb):
                src = W[nm][l][:, c0 * cb:(c0 + 1) * cb].rearrange("(kc p) c -> p kc c", p=128)
                yield (wb[l][c0], src)

    filler = [iter(())]

    def fill(n=1):
        for _ in range(n):
            j = next(filler[0], None)
            if j is None:
                return
            S.dma('pool', j[0], j[1])

    def drain_fill():
        while True:
            j = next(filler[0], None)
            if j is None:
                return
            S.dma('pool', j[0], j[1])

    for j in cast_jobs(0):
        S.dma('pool', j[0], j[1])
    S.barrier()

    def colv(idx):
        return cols[:, idx:idx + 1]

    def c_norm(kind, l):
        return (kind * L + l) * 16

    C_FINAL = 3 * L * 16
    C_BG = (3 * L + 1) * 16

    xTv = xT.rearrange("(m p) t -> p m t", p=128)

    def gemm_blocks(st, wname, l, nblocks, rhs_fn, KC, epi, wpool, b0=0):
        wb, KCw, cb = WB[wname]
        assert KCw == KC
        nsub = cb // 128
        pend = []
        PF = 2
        blocks = list(range(b0, b0 + nblocks))
        loaded = {}

        def load(bi):
            wt, wk = wpool.get()
            S.dma('sp', wt[:, 0:KC, 0:cb], wb[l][bi], writes=[wk])
            loaded[bi] = (wt, wk)
        for bi in blocks[:PF]:
            load(bi)
        for ii, bi in enumerate(blocks):
            if ii + PF < len(blocks):
                load(blocks[ii + PF])
            wt, wk = loaded.pop(bi)
            fill()
            pss = []
            for sub in range(nsub):
                ps, pk = ps_pool.get()
                for kc in range(KC):
                    rap, rk = rhs_fn(kc)
                    S.op('pe', lambda: E['pe'].matmul(ps[:], lhsT=wt[:, kc, sub * 128:(sub + 1) * 128], rhs=rap,
                                                      start=(kc == 0), stop=(kc == KC - 1)),
                         reads=[wk, rk], writes=[pk], inc=(kc == KC - 1))
                pss.append((ps, pk))
            epi(bi, pss)

    def rms_norm(st, t, gcol0, out_fn):
        ps, pk = ps_pool.get()
        xn = st['xn16']
        for m in range(16):
            S.dma('sp', xn[:, m, :], xTv[:, m, t * TT:(t + 1) * TT], writes=[('xn16', m)])
        for m in range(16):
            sq, sk = st['sq'].get()
            S.op('act', lambda: E['act'].activation(out=sq[:], in_=xn[:, m, :], func=AF.Square), reads=[('xn16', m)], writes=[sk])
            S.op('pe', lambda: E['pe'].matmul(ps[:], lhsT=ones_b[:], rhs=sq[:], start=(m == 0), stop=(m == 15)),
                 reads=['onesb', sk], writes=[pk], inc=True)
        rs = st['rstd']
        S.op('act', lambda: E['act'].activation(out=rs[:], in_=ps[:], func=AF.Sqrt, bias=epsc[:], scale=1.0 / D),
             reads=[pk, 'epsc'], writes=['rstd'])
        S.op('dve', lambda: E['dve'].reciprocal(out=rs[:], in_=rs[:]), reads=['rstd'], writes=['rstd'])
        for m in range(16):
            out_fn(m, xn[:, m, :], ('xn16', m), rs)

    def norm_to_hT(st, t, gcol0):
        hT = st['hT']

        def o(m, xs, xk, rs):
            S.op('dve', lambda: E['dve'].scalar_tensor_tensor(out=hT[:, m, :], in0=xs, scalar=colv(gcol0 + m),
                                                              in1=rs[:], op0=ALU.mult, op1=ALU.mult),
                 reads=[xk, 'rstd', 'cols'], writes=[('hT', m)])
        rms_norm(st, t, gcol0, o)

    def resid_epi(st, t, scale):
        def epi(bi, pss):
            for sub, (ps, pk) in enumerate(pss):
                m = bi * len(pss) + sub
                xs, xk = st['xs'].get()
                S.dma('sp', xs[:], xTv[:, m, t * TT:(t + 1) * TT], writes=[xk])
                S.op('dve', lambda: E['dve'].scalar_tensor_tensor(out=xs[:], in0=ps[:], scalar=scale, in1=xs[:],
                                                                  op0=ALU.mult, op1=ALU.add),
                     reads=[pk, xk], writes=[xk])
                S.dma('pool', xTv[:, m, t * TT:(t + 1) * TT], xs[:], reads=[xk])
        return epi

    def ffn(st, l, t, which):
        wg, wu, wd = ("wg1", "wu1", "wd1") if which == 1 else ("wg2", "wu2", "wd2")
        norm_to_hT(st, t, c_norm(0 if which == 1 else 2, l))
        hT = st['hT']
        aT = st['aT']
        wbg, _, _ = WB[wg]
        wbu, _, _ = WB[wu]
        PF = 1
        nb = DFF // 256

        def loadgu(bi):
            wt, wk = st['wA'].get()
            S.dma('sp', wt[:], wbg[l][bi], writes=[wk])
            wt2, wk2 = st['wA'].get()
            S.dma('sp', wt2[:], wbu[l][bi], writes=[wk2])
            return (wt, wk, wt2, wk2)
        q = [loadgu(bi) for bi in range(min(PF, nb))]
        for bi in range(nb):
            if bi + PF < nb:
                q.append(loadgu(bi + PF))
            wt, wk, wt2, wk2 = q.pop(0)
            fill()
            for sub in range(2):
                m = bi * 2 + sub
                pg, pgk = ps_pool.get()
                pu, puk = ps_pool.get()
                for (ps, pk, w_, wk_) in ((pg, pgk, wt, wk), (pu, puk, wt2, wk2)):
                    for kc in range(16):
                        S.op('pe', lambda: E['pe'].matmul(ps[:], lhsT=w_[:, kc, sub * 128:(sub + 1) * 128],
                                                          rhs=hT[:, kc, :], start=(kc == 0), stop=(kc == 15)),
                             reads=[wk_, ('hT', kc)], writes=[pk], inc=(kc == 15))
                sg, sgk = st['f32'].get()
                S.op('act', lambda: E['act'].activation(out=sg[:], in_=pg[:], func=AF.Silu), reads=[pgk], writes=[sgk])
                S.op('dve', lambda: E['dve'].tensor_tensor(out=aT[:, m, :], in0=sg[:], in1=pu[:], op=ALU.mult),
                     reads=[sgk, puk], writes=[('aT', m)])
        gemm_blocks(st, wd, l, 16, lambda kc: (aT[:, kc, :], ('aT', kc)), 44, resid_epi(st, t, 0.5), st['wB'])

    def gemm_scope():
        stack = ExitStack()
        st = {'stack': stack}
        st['hT'] = sb("hT", [128, 16, TT], BF16, stack)
        st['aT'] = sb("aT", [128, 44, TT], BF16, stack)
        st['wA'] = Pool([sb("wA%d" % i, [128, 16, 256], BF16, stack) for i in range(4)], "wA")
        st['wB'] = Pool([sb("wB%d" % i, [128, 44, 128], BF16, stack) for i in range(3)], "wB")
        st['xs'] = Pool([sb("xs%d" % i, [128, TT], F32, stack) for i in range(3)], "xs")
        st['xn16'] = sb("xn16", [128, 16, TT], F32, stack)
        st['sq'] = Pool([sb("sq%d" % i, [128, TT], BF16, stack) for i in range(3)], "sq")
        st['f32'] = Pool([sb("f32_%d" % i, [128, TT], F32, stack) for i in range(6)], "f32")
        st['ob'] = Pool([sb("ob%d" % i, [128, TT], BF16, stack) for i in range(5)], "ob")
        st['rstd'] = sb("rstd", [128, TT], F32, stack)
        st['rot'] = Pool([sb("rot%d" % i, [128, 2, TT], F32, stack) for i in range(2)], "rot")
        st['g3'] = Pool([sb("g3_%d" % i, [128, 3, TT], BF16, stack) for i in range(2)], "g3")
        return st

    def io_scope():
        stack = ExitStack()
        st = {'stack': stack}
        st['xn16'] = sb("ixn16", [128, 16, TT], F32, stack)
        st['sq'] = Pool([sb("isq%d" % i, [128, TT], BF16, stack) for i in range(2)], "sq")
        st['rstd'] = sb("irstd", [128, TT], F32, stack)
        st['xrow'] = sb("xrow", [128, D], F32, stack)
        st['stg'] = sb("stg", [128, 16, TT], F32, stack)
        return st

    def phase_in(st):
        stg = st['stg']
        xr = st['xrow']
        for t in range(NT):
            for s4 in range(4):
                S.dma('sp', xr[:], x_in[t * TT + s4 * 128:t * TT + (s4 + 1) * 128, :], writes=['xrow'])
                for mg in range(4):
                    ps, pk = ps_pool.get()
                    for j in range(4):
                        m = mg * 4 + j
                        S.op('pe', lambda: E['pe'].transpose(ps[:, j * 128:(j + 1) * 128], xr[:, m * 128:(m + 1) * 128], ident_f[:]),
                             reads=['xrow', 'identf'], writes=[pk], inc=(j == 3))
                    S.op('act', lambda: E['act'].activation(
                        out=stg[:, mg * 4:(mg + 1) * 4, s4 * 128:(s4 + 1) * 128],
                        in_=ps[:].rearrange("p (j c) -> p j c", j=4), func=AF.Copy),
                        reads=[pk], writes=['stg'])
            S.dma('pool', xTv[:, :, t * TT:(t + 1) * TT], stg[:], reads=['stg'])
        S.barrier()

    def phase_inproj(st, l, t):
        norm_to_hT(st, t, c_norm(1, l))
        hT = st['hT']
        tsl = slice(t * TT, (t + 1) * TT)

        def store(dst, m_local, ob, obk):
            S.dma('pool', dst[m_local * 128:(m_local + 1) * 128, tsl], ob[:], reads=[obk])

        def epi(bi, pss):
            (p0, k0), (p1, k1) = pss
            if bi < 8:
                isq = bi < 4
                h = bi if isq else bi - 4
                tab = rotq_in if isq else rotk_in
                dst = qrT if isq else krT
                rt, rtk = st['rot'].get()
                S.dma('sp', rt[:], tab[h, :, :, tsl].rearrange("c p t -> p c t"), writes=[rtk])
                t1, t1k = st['f32'].get()
                t2, t2k = st['f32'].get()
                S.op('act', lambda: E['act'].activation(out=t1[:], in_=p0[:], func=AF.Copy), reads=[k0], writes=[t1k])
                S.op('act', lambda: E['act'].activation(out=t2[:], in_=p1[:], func=AF.Copy), reads=[k1], writes=[t2k])
                a, ak = st['f32'].get()
                b, bk = st['f32'].get()
                o1, o1k = st['ob'].get()
                o2, o2k = st['ob'].get()
                S.op('pool', lambda: E['pool'].tensor_tensor(out=a[:], in0=t1[:], in1=rt[:, 0, :], op=ALU.mult), reads=[t1k, rtk], writes=[ak])
                S.op('pool', lambda: E['pool'].tensor_tensor(out=b[:], in0=t2[:], in1=rt[:, 1, :], op=ALU.mult), reads=[t2k, rtk], writes=[bk])
                S.op('pool', lambda: E['pool'].tensor_tensor(out=o1[:], in0=a[:], in1=b[:], op=ALU.subtract), reads=[ak, bk], writes=[o1k])
                S.op('dve', lambda: E['dve'].tensor_tensor(out=t1[:], in0=t1[:], in1=rt[:, 1, :], op=ALU.mult), reads=[t1k, rtk, ak], writes=[t1k])
                S.op('dve', lambda: E['dve'].tensor_tensor(out=t2[:], in0=t2[:], in1=rt[:, 0, :], op=ALU.mult), reads=[t2k, rtk, bk], writes=[t2k])
                S.op('dve', lambda: E['dve'].tensor_tensor(out=o2[:], in0=t1[:], in1=t2[:], op=ALU.add), reads=[t1k, t2k], writes=[o2k])
                store(dst, 2 * h, o1, o1k)
                store(dst, 2 * h + 1, o2, o2k)
                return
            for sub, (ps, pk) in enumerate(pss):
                m = bi * 2 + sub
                if m < 32:
                    ob, obk = st['ob'].get()
                    if sub == 0:
                        S.op('act', lambda: E['act'].activation(out=ob[:], in_=ps[:], func=AF.Copy), reads=[pk], writes=[obk])
                    else:
                        S.op('dve', lambda: E['dve'].tensor_copy(out=ob[:], in_=ps[:]), reads=[pk], writes=[obk])
                    store(rvT, m - 16, ob, obk)
                elif m < 48:
                    ob, obk = st['ob'].get()
                    S.op('act', lambda: E['act'].activation(out=ob[:], in_=ps[:], func=AF.Silu), reads=[pk], writes=[obk])
                    store(rgT, m - 32, ob, obk)
                elif m < 64:
                    xs_, xk_ = st['f32'].get()
                    u, uk = st['f32'].get()
                    S.op('act', lambda: E['act'].activation(out=xs_[:], in_=ps[:], func=AF.Copy), reads=[pk], writes=[xk_])
                    S.op('act', lambda: E['act'].activation(out=u[:], in_=ps[:], func=AF.Square), reads=[pk], writes=[uk])
                    S.op('dve', lambda: E['dve'].tensor_scalar(out=u[:], in0=u[:], scalar1=0.044715, scalar2=1.0, op0=ALU.mult, op1=ALU.add),
                         reads=[uk], writes=[uk])
                    S.op('dve', lambda: E['dve'].tensor_tensor(out=u[:], in0=u[:], in1=xs_[:], op=ALU.mult), reads=[uk, xk_], writes=[uk])
                    S.op('act', lambda: E['act'].activation(out=u[:], in_=u[:], func=AF.Sigmoid, scale=1.5957691216057308),
                         reads=[uk], writes=[uk])
                    if m < 56:
                        ob, obk = st['ob'].get()
                        S.op('dve', lambda: E['dve'].tensor_tensor(out=ob[:], in0=u[:], in1=xs_[:], op=ALU.mult), reads=[uk, xk_], writes=[obk])
                        store(suT, m - 48, ob, obk)
                    else:
                        S.op('dve', lambda: E['dve'].tensor_tensor(out=u[:], in0=u[:], in1=xs_[:], op=ALU.mult), reads=[uk, xk_], writes=[uk])
                        store(svT, m - 56, u, uk)
                elif m < 88:
                    ob, obk = st['ob'].get()
                    sc = (128 ** -0.5) if m < 72 else 1.0
                    dst = aqT if m < 72 else (akT if m < 80 else avT)
                    mb = 64 if m < 72 else (72 if m < 80 else 80)
                    S.op('act', lambda: E['act'].activation(out=ob[:], in_=ps[:], func=AF.Copy, scale=sc), reads=[pk], writes=[obk])
                    store(dst, m - mb, ob, obk)
                else:
                    ob, obk = st['ob'].get()
                    S.op('act', lambda: E['act'].activation(out=ob[:], in_=ps[:], func=AF.Sigmoid, bias=colv(C_BG + l * 48 + (m - 88)), scale=1.0),
                         reads=[pk, 'cols'], writes=[obk])
                    store(gT, m - 88, ob, obk)
        gemm_blocks(st, "win", l, 68, lambda kc: (hT[:, kc, :], ('hT', kc)), 16, epi, st['wA'])

    def phase_proj(st, l, t):
        tsl = slice(t * TT, (t + 1) * TT)
        rt_ = st['hT']
        aT = st['aT']
        sgt = aT[:, 0:8, :]
        att = aT[:, 8:16, :]
        mg = aT[:, 16:32, :]
        S.dma('sp', rt_[:], retoT.rearrange("(m p) t -> p m t", p=128)[:, :, tsl], writes=[('hT', k) for k in range(16)])
        S.dma('sp', sgt, sgoT.rearrange("(m p) t -> p m t", p=128)[:, :, tsl], writes=[('aT', k) for k in range(8)])
        S.dma('sp', att, attoT.rearrange("(m p) t -> p m t", p=128)[:, :, tsl], writes=[('aT', 8 + k) for k in range(8)])
        gTv = gT.rearrange("(b m p) t -> p b m t", b=3, p=128)
        wbr, _, _ = WB["wpr"]
        wbs, _, _ = WB["wps"]
        wba, _, _ = WB["wpa"]
        def loadp(bi):
            w1, w1k = st['wA'].get()
            S.dma('sp', w1[:], wbr[l][bi], writes=[w1k])
            w2, w2k = st['wA'].get()
            S.dma('sp', w2[:, 0:8, :], wbs[l][bi], writes=[w2k])
            S.dma('sp', w2[:, 8:16, :], wba[l][bi], writes=[w2k])
            return (w1, w1k, w2, w2k)
        pq = [loadp(0)]
        for bi in range(8):
            if bi + 1 < 8:
                pq.append(loadp(bi + 1))
            w1, w1k, w2, w2k = pq.pop(0)
            for sub in range(2):
                m = bi * 2 + sub
                pa, pak = ps_pool.get()
                pb, pbk = ps_pool.get()
                pc, pck = ps_pool.get()
                for kc in range(16):
                    S.op('pe', lambda: E['pe'].matmul(pa[:], lhsT=w1[:, kc, sub * 128:(sub + 1) * 128], rhs=rt_[:, kc, :],
                                                      start=(kc == 0), stop=(kc == 15)),
                         reads=[w1k, ('hT', kc)], writes=[pak], inc=(kc == 15))
                for kc in range(8):
                    S.op('pe', lambda: E['pe'].matmul(pb[:], lhsT=w2[:, kc, sub * 128:(sub + 1) * 128], rhs=sgt[:, kc, :],
                                                      start=(kc == 0), stop=(kc == 7)),
                         reads=[w2k, ('aT', kc)], writes=[pbk], inc=(kc == 7))
                for kc in range(8):
                    S.op('pe', lambda: E['pe'].matmul(pc[:], lhsT=w2[:, 8 + kc, sub * 128:(sub + 1) * 128], rhs=att[:, kc, :],
                                                      start=(kc == 0), stop=(kc == 7)),
                         reads=[w2k, ('aT', 8 + kc)], writes=[pck], inc=(kc == 7))
                g3, g3k = st['g3'].get()
                S.dma('sp', g3[:], gTv[:, :, m, tsl], writes=[g3k])
                t1, t1k = st['f32'].get()
                t2, t2k = st['f32'].get()
                t3, t3k = st['f32'].get()
                S.op('dve', lambda: E['dve'].tensor_tensor(out=t1[:], in0=pa[:], in1=g3[:, 0, :], op=ALU.mult), reads=[pak, g3k], writes=[t1k])
                S.op('dve', lambda: E['dve'].tensor_tensor(out=t2[:], in0=pb[:], in1=g3[:, 1, :], op=ALU.mult), reads=[pbk, g3k], writes=[t2k])
                S.op('dve', lambda: E['dve'].tensor_tensor(out=t3[:], in0=pc[:], in1=g3[:, 2, :], op=ALU.mult), reads=[pck, g3k], writes=[t3k])
                S.op('pool', lambda: E['pool'].tensor_tensor(out=t1[:], in0=t1[:], in1=t2[:], op=ALU.add), reads=[t1k, t2k], writes=[t1k])
                S.op('pool', lambda: E['pool'].tensor_tensor(out=mg[:, m, :], in0=t1[:], in1=t3[:], op=ALU.add), reads=[t1k, t3k], writes=[('aT', 16 + m)])
        gemm_blocks(st, "wo", l, 8, lambda kc: (mg[:, kc, :], ('aT', 16 + kc)), 16, resid_epi(st, t, 1.0), st['wA'])

    def phase_out(st):
        stg = st['stg']
        orow = st['xrow']
        for t in range(NT):
            def o(m, xs, xk, rs):
                S.op('dve', lambda: E['dve'].scalar_tensor_tensor(out=stg[:, m, :], in0=xs, scalar=colv(C_FINAL + m),
                                                                  in1=rs[:], op0=ALU.mult, op1=ALU.mult),
                     reads=[xk, 'rstd', 'cols'], writes=[('stg', m)])
            rms_norm(st, t, C_FINAL, o)
            for s4 in range(4):
                for mg_ in range(4):
                    ps, pk = ps_pool.get()
                    for j in range(4):
                        m = mg_ * 4 + j
                        S.op('pe', lambda: E['pe'].transpose(ps[:, j * 128:(j + 1) * 128], stg[:, m, s4 * 128:(s4 + 1) * 128], ident_f[:]),
                             reads=[('stg', m), 'identf'], writes=[pk], inc=(j == 3))
                    S.op('act', lambda: E['act'].activation(out=orow[:, mg_ * 512:(mg_ + 1) * 512], in_=ps[:], func=AF.Copy),
                         reads=[pk], writes=['xrow'])
                S.dma('pool', y_out[t * TT + s4 * 128:t * TT + (s4 + 1) * 128, :], orow[:], reads=['xrow'])
            S.barrier()

    def phase_ret(l):
        stack = ExitStack()
        SC = 256
        qv = qrT.rearrange("(m p) t -> p m t", p=128)
        kv = krT.rearrange("(m p) t -> p m t", p=128)
        vv = rvT.rearrange("(m p) t -> p m t", p=128)
        gv = rgT.rearrange("(m p) t -> p m t", p=128)
        ov = retoT.rearrange("(m p) t -> p m t", p=128)
        qs = Pool([sb("rq%d" % i, [128, 8, SC], BF16, stack) for i in range(2)], "rq")
        ks = Pool([sb("rk%d" % i, [128, 8, SC], BF16, stack) for i in range(2)], "rk")
        vs = Pool([sb("rv%d" % i, [128, 16, SC], BF16, stack) for i in range(2)], "rv")
        gs = Pool([sb("rg%d" % i, [128, 16, SC], BF16, stack) for i in range(2)], "rg")
        os_ = Pool([sb("ro%d" % i, [128, 16, SC], BF16, stack) for i in range(2)], "ro")
        R32 = [sb("R32_%d" % h, [128, 2, 512], F32, stack) for h in range(4)]
        Rbf = [sb("Rbf_%d" % h, [128, 2, 512], BF16, stack) for h in range(4)]
        kh = Pool([sb("kh%d" % i, [128, 256], BF16, stack) for i in range(4)], "kh")
        vt = Pool([sb("vt%d" % i, [128, 512], BF16, stack) for i in range(4)], "vt")
        At = Pool([sb("At%d" % i, [128, 128], BF16, stack) for i in range(4)], "At")
        osb = Pool([sb("osb%d" % i, [128, 512], F32, stack) for i in range(4)], "osb")
        sqb = Pool([sb("sqb%d" % i, [128, 512], BF16, stack) for i in range(4)], "sqb")
        rin = Pool([sb("rin%d" % i, [128, 128], F32, stack) for i in range(4)], "rin")
        for h in range(4):
            S.op('pool', lambda: E['pool'].memset(R32[h][:], 0.0), writes=[('R32', h)])
            S.op('pool', lambda: E['pool'].memset(Rbf[h][:], 0.0), writes=[('Rbf', h)])
        bA = [(PSB[0], ('psb', 0)), (PSB[1], ('psb', 1))]
        bB = [(PSB[2], ('psb', 2)), (PSB[3], ('psb', 3))]
        bC = [(PSB[4], ('psb', 4)), (PSB[5], ('psb', 5))]
        bU = [(PSB[6], ('psb', 6)), (PSB[7], ('psb', 7))]
        for sc in range(T // SC):
            tsl = slice(sc * SC, (sc + 1) * SC)
            q_, qk = qs.get()
            k_, kk = ks.get()
            v_, vk = vs.get()
            g_, gk = gs.get()
            o_, ok = os_.get()
            S.dma('sp', q_[:], qv[:, :, tsl], writes=[qk])
            S.dma('sp', k_[:], kv[:, :, tsl], writes=[kk])
            S.dma('sp', v_[:], vv[:, :, tsl], writes=[vk])
            S.dma('sp', g_[:], gv[:, :, tsl], writes=[gk])
            for cc in range(SC // 128):
                csl = slice(cc * 128, (cc + 1) * 128)
                for hp in range(2):
                    U_ = []
                    for ui, h in enumerate((2 * hp, 2 * hp + 1)):
                        c = {'h': h}
                        c['gC'] = float(np.float32(np.exp(np.float32(128.0) * np.log1p(np.float32(-(2.0 ** (-5.0 - h)))))))
                        c['pA'], c['pAk'] = bA[ui]
                        c['pB'], c['pBk'] = bB[ui]
                        c['pC'], c['pCk'] = bC[ui]
                        c['pU'], c['pUk'] = bU[ui]
                        c['pAb'] = c['pA'][:].bitcast(BF16)
                        U_.append(c)
                    for c in U_:
                        h = c['h']
                        pAb, pAk, pB, pBk = c['pAb'], c['pAk'], c['pB'], c['pBk']
                        for dch in range(2):
                            S.op('pe', lambda: E['pe'].transpose(pAb[:, dch * 128:(dch + 1) * 128], k_[:, 2 * h + dch, csl], ident_b[:]),
                                 reads=[kk, 'identb'], writes=[pAk], inc=False)
                        for ech in range(4):
                            S.op('pe', lambda: E['pe'].transpose(pAb[:, 256 + ech * 128:256 + (ech + 1) * 128], v_[:, 4 * h + ech, csl], ident_b[:]),
                                 reads=[vk, 'identb'], writes=[pAk], inc=(ech == 3))
                        for dch in range(2):
                            S.op('pe', lambda: E['pe'].matmul(pB[:, 0:128], lhsT=k_[:, 2 * h + dch, csl], rhs=q_[:, 2 * h + dch, csl],
                                                              start=(dch == 0), stop=(dch == 1)),
                                 reads=[kk, qk], writes=[pBk], inc=(dch == 1))
                    for c in U_:
                        pAb, pAk, pB, pBk = c['pAb'], c['pAk'], c['pB'], c['pBk']
                        c['kh'], c['khk'] = kh.get()
                        c['vt'], c['vtk'] = vt.get()
                        c['A'], c['Ak'] = At.get()
                        kh_, vt_, A_ = c['kh'], c['vt'], c['A']
                        gC = c['gC']
                        S.op('act', lambda: E['act'].activation(out=kh_[:], in_=pAb[:, 0:256], func=AF.Copy, scale=gC), reads=[pAk], writes=[c['khk']])
                        S.op('act', lambda: E['act'].activation(out=vt_[:], in_=pAb[:, 256:768], func=AF.Copy), reads=[pAk], writes=[c['vtk']])
                        S.op('dve', lambda: E['dve'].tensor_tensor(out=A_[:], in0=pB[:, 0:128], in1=cmask[:], op=ALU.mult),
                             reads=[pBk, 'cmask'], writes=[c['Ak']])
                    for c in U_:
                        h = c['h']
                        pC, pCk, vt_, A_, kh_ = c['pC'], c['pCk'], c['vt'], c['A'], c['kh']
                        for ech in range(4):
                            oc = pC[:, ech * 128:(ech + 1) * 128]
                            S.op('pe', lambda: E['pe'].matmul(oc, lhsT=vt_[:, ech * 128:(ech + 1) * 128], rhs=A_[:], start=True, stop=False),
                                 reads=[c['vtk'], c['Ak']], writes=[pCk], inc=False)
                            for dch in range(2):
                                S.op('pe', lambda: E['pe'].matmul(oc, lhsT=Rbf[h][:, dch, ech * 128:(ech + 1) * 128], rhs=q_[:, 2 * h + dch, csl],
                                                                  start=False, stop=(dch == 1)),
                                     reads=[('Rbf', h), qk], writes=[pCk], inc=(dch == 1 and ech == 3))
                    for c in U_:
                        pC, pCk = c['pC'], c['pCk']
                        c['ob'], c['obk'] = osb.get()
                        c['sq'], c['sqk'] = sqb.get()
                        ob_, sq_ = c['ob'], c['sq']
                        S.op('act', lambda: E['act'].activation(out=ob_[:], in_=pC[:], func=AF.Copy), reads=[pCk], writes=[c['obk']])
                        S.op('act', lambda: E['act'].activation(out=sq_[:], in_=pC[:], func=AF.Square), reads=[pCk], writes=[c['sqk']])
                    for c in U_:
                        pB, pBk, sq_ = c['pB'], c['pBk'], c['sq']
                        for ech in range(4):
                            S.op('pe', lambda: E['pe'].matmul(pB[:, 128:256], lhsT=ones_b[:], rhs=sq_[:, ech * 128:(ech + 1) * 128],
                                                              start=(ech == 0), stop=(ech == 3)),
                                 reads=['onesb', c['sqk']], writes=[pBk], inc=(ech == 3))
                    for dch in range(2):
                        for c in U_:
                            h = c['h']
                            pU, pUk, kh_, vt_ = c['pU'], c['pUk'], c['kh'], c['vt']
                            gC = c['gC']
                            S.op('pe', lambda: E['pe'].matmul(pU[:], lhsT=kh_[:, dch * 128:(dch + 1) * 128], rhs=vt_[:], start=True, stop=True),
                                 reads=[c['khk'], c['vtk']], writes=[pUk], inc=True)
                            S.op('dve', lambda: E['dve'].scalar_tensor_tensor(out=R32[h][:, dch, :], in0=R32[h][:, dch, :], scalar=gC,
                                                                              in1=pU[:], op0=ALU.mult, op1=ALU.add),
                                 reads=[pUk, ('R32', h)], writes=[('R32', h)])
                    for c in U_:
                        h = c['h']
                        pB, pBk, ob_ = c['pB'], c['pBk'], c['ob']
                        ri, rik = rin.get()
                        S.op('act', lambda: E['act'].activation(out=Rbf[h][:], in_=R32[h][:], func=AF.Copy),
                             reads=[('R32', h)], writes=[('Rbf', h)])
                        S.op('act', lambda: E['act'].activation(out=ri[:], in_=pB[:, 128:256], func=AF.Sqrt, bias=epsc[:], scale=1.0 / 512),
                             reads=[pBk, 'epsc'], writes=[rik])
                        S.op('dve', lambda: E['dve'].reciprocal(out=ri[:], in_=ri[:]), reads=[rik], writes=[rik])
                        S.op('dve', lambda: E['dve'].tensor_tensor(out=ob_[:].rearrange("p (e i) -> p e i", e=4),
                                                                   in0=ob_[:].rearrange("p (e i) -> p e i", e=4),
                                                                   in1=ri[:].unsqueeze(1).broadcast_to([128, 4, 128]), op=ALU.mult),
                             reads=[c['obk'], rik], writes=[c['obk']])
                        S.op('pool', lambda: E['pool'].tensor_tensor(out=o_[:, 4 * h:4 * h + 4, csl],
                                                                     in0=ob_[:].rearrange("p (e i) -> p e i", e=4),
                                                                     in1=g_[:, 4 * h:4 * h + 4, csl], op=ALU.mult),
                             reads=[c['obk'], gk], writes=[ok])
            S.dma('pool', ov[:, :, tsl], o_[:], reads=[ok])
        S.barrier()
        stack.close()

    def phase_sgu(l):
        stack = ExitStack()
        SC = 256
        uv = suT.rearrange("(m p) t -> p m t", p=128)
        vv = svT.rearrange("(m p) t -> p m t", p=128)
        ov = sgoT.rearrange("(m p) t -> p m t", p=128)
        us = Pool([sb("su%d" % i, [128, 8, SC], BF16, stack) for i in range(2)], "su")
        vs = Pool([sb("sv%d" % i, [128, 8, SC], F32, stack) for i in range(2)], "sv")
        os_ = Pool([sb("so%d" % i, [128, 8, SC], BF16, stack) for i in range(2)], "so")
        lng = sb("lng", [128, 1024], F32, stack)
        lnb = sb("lnb", [128, 1024], F32, stack)
        wcf = sb("wcf", [128, 4, 128], F32, stack)
        wcb = sb("wcb", [128, 4, 128], BF16, stack)
        sgb = sb("sgb", [128, 4, 128], F32, stack)
        xn = Pool([sb("xn%d" % i, [128, 1024], F32, stack) for i in range(2)], "xn")
        vb = Pool([sb("vb%d" % i, [128, 1024], BF16, stack) for i in range(2)], "vb")
        stt = Pool([sb("stt%d" % i, [128, 16], F32, stack) for i in range(2)], "stt")
        tm = Pool([sb("tm%d" % i, [128, 8, 128], F32, stack) for i in range(2)], "tm")
        S.dma('sp', lng[:], lng_in[l].partition_broadcast(128), writes=['lng'])
        S.dma('sp', lnb[:], lnb_in[l].partition_broadcast(128), writes=['lnb'])
        S.dma('sp', wcf[:], sgwT_in[l].rearrange("g s t -> s g t"), writes=['wcf'])
        S.dma('sp', sgb[:], sgb_in[l].partition_broadcast(128), writes=['sgb'])
        S.op('dve', lambda: E['dve'].tensor_tensor(out=wcb[:], in0=wcf[:], in1=cmask[:].unsqueeze(1).broadcast_to([128, 4, 128]), op=ALU.mult),
             reads=['wcf', 'cmask'], writes=['wcb'])
        for sc in range(T // SC):
            tsl = slice(sc * SC, (sc + 1) * SC)
            u_, uk = us.get()
            v_, vk = vs.get()
            o_, ok = os_.get()
            S.dma('sp', u_[:], uv[:, :, tsl], writes=[uk])
            S.dma('sp', v_[:], vv[:, :, tsl], writes=[vk])
            U_ = []
            for cc in range(SC // 128):
                c = {'csl': slice(cc * 128, (cc + 1) * 128)}
                b0 = (cc % 2) * 4
                c['p'] = [(PSB[b0 + i], ('psb', b0 + i)) for i in range(4)]
                U_.append(c)
            for c in U_:
                (p0, p0k), (p1, p1k) = c['p'][0], c['p'][1]
                for f in range(8):
                    pp, ppk = (p0, p0k) if f < 4 else (p1, p1k)
                    S.op('pe', lambda: E['pe'].transpose(pp[:, (f % 4) * 128:(f % 4 + 1) * 128], v_[:, f, c['csl']], ident_f[:]),
                         reads=[vk, 'identf'], writes=[ppk], inc=(f % 4 == 3))
            for c in U_:
                (p0, p0k), (p1, p1k) = c['p'][0], c['p'][1]
                c['st'], c['stk'] = stt.get()
                st_, stk = c['st'], c['stk']
                S.op('dve', lambda: E['dve'].bn_stats(out=st_[:, 0:6], in_=p0[:]), reads=[p0k], writes=[stk])
                S.op('dve', lambda: E['dve'].bn_stats(out=st_[:, 6:12], in_=p1[:]), reads=[p1k, stk], writes=[stk])
                S.op('dve', lambda: E['dve'].bn_aggr(out=st_[:, 12:14], in_=st_[:, 0:12]), reads=[stk], writes=[stk])
                S.op('act', lambda: E['act'].activation(out=st_[:, 14:15], in_=st_[:, 13:14], func=AF.Sqrt, bias=epsc[:], scale=1.0),
                     reads=[stk, 'epsc'], writes=[stk])
            for c in U_:
                (p0, p0k), (p1, p1k) = c['p'][0], c['p'][1]
                st_, stk = c['st'], c['stk']
                S.op('dve', lambda: E['dve'].reciprocal(out=st_[:, 15:16], in_=st_[:, 14:15]), reads=[stk], writes=[stk])
                c['xn'], c['xnk'] = xn.get()
                xn_, xnk = c['xn'], c['xnk']
                for half, (pp, ppk) in enumerate(((p0, p0k), (p1, p1k))):
                    S.op('dve', lambda: E['dve'].tensor_scalar(out=xn_[:, half * 512:(half + 1) * 512], in0=pp[:], scalar1=st_[:, 12:13],
                                                               scalar2=st_[:, 15:16], op0=ALU.subtract, op1=ALU.mult),
                         reads=[ppk, stk], writes=[xnk])
            for c in U_:
                xn_, xnk = c['xn'], c['xnk']
                c['vb'], c['vbk'] = vb.get()
                vb_, vbk = c['vb'], c['vbk']
                S.op('pool', lambda: E['pool'].tensor_tensor(out=xn_[:], in0=xn_[:], in1=lng[:], op=ALU.mult), reads=[xnk, 'lng'], writes=[xnk])
                S.op('pool', lambda: E['pool'].tensor_tensor(out=vb_[:], in0=xn_[:], in1=lnb[:], op=ALU.add), reads=[xnk, 'lnb'], writes=[vbk])
            for c in U_:
                (p2, p2k), (p3, p3k) = c['p'][2], c['p'][3]
                vb_, vbk = c['vb'], c['vbk']
                for f in range(8):
                    pp, ppk = (p2, p2k) if f < 4 else (p3, p3k)
                    S.op('pe', lambda: E['pe'].matmul(pp[:, (f % 4) * 128:(f % 4 + 1) * 128], lhsT=vb_[:, f * 128:(f + 1) * 128], rhs=wcb[:, f // 2, :],
                                                      start=True, stop=True),
                         reads=[vbk, 'wcb'], writes=[ppk], inc=(f % 4 == 3))
            for c in U_:
                (p2, p2k), (p3, p3k) = c['p'][2], c['p'][3]
                c['tm'], c['tmk'] = tm.get()
                tm_, tmk = c['tm'], c['tmk']
                for half, (pp, ppk) in enumerate(((p2, p2k), (p3, p3k))):
                    S.op('dve', lambda: E['dve'].tensor_tensor(
                        out=tm_[:, half * 4:(half + 1) * 4, :].rearrange("p (g two) t -> p g two t", two=2),
                        in0=pp[:].rearrange("p (g two t) -> p g two t", g=2, two=2),
                        in1=sgb[:, 2 * half:2 * half + 2, :].unsqueeze(2).broadcast_to([128, 2, 2, 128]), op=ALU.add),
                        reads=[ppk, 'sgb'], writes=[tmk])
            for c in U_:
                tm_, tmk = c['tm'], c['tmk']
                S.op('pool', lambda: E['pool'].tensor_tensor(out=o_[:, :, c['csl']], in0=tm_[:], in1=u_[:, :, c['csl']], op=ALU.mult),
                     reads=[tmk, uk], writes=[ok])
            S.dma('pool', ov[:, :, tsl], o_[:], reads=[ok])
        S.barrier()
        stack.close()

    def phase_att(l):
        stack = ExitStack()
        WIN = 2048
        NW = T // WIN
        qv = aqT.rearrange("(m p) t -> p m t", p=128)
        kv = akT.rearrange("(m p) t -> p m t", p=128)
        vv = avT.rearrange("(m p) t -> p m t", p=128)
        ov = attoT.rearrange("(m p) t -> p m t", p=128)
        Em = sb("Em", [128, 48, 128], F32, stack)
        qb = Pool([sb("aq%d" % i, [128, WIN], BF16, stack) for i in range(2)], "aq")
        kb = Pool([sb("ak%d" % i, [128, 2 * WIN], BF16, stack) for i in range(2)], "ak")
        vbf = Pool([sb("av%d" % i, [128, 2 * WIN], BF16, stack) for i in range(2)], "av")
        acc = Pool([sb("acc%d" % i, [128, 2, WIN], F32, stack) for i in range(2)], "acc")
        oo = Pool([sb("ao%d" % i, [128, WIN], BF16, stack) for i in range(2)], "ao")
        ex = Pool([sb("ex%d" % i, [128, 2, 128], F32, stack) for i in range(8)], "ex")
        Pb = Pool([sb("Pb%d" % i, [128, 2, 128], BF16, stack) for i in range(8)], "Pb")
        vtk_ = Pool([sb("avt%d" % i, [128, 2, 128], BF16, stack) for i in range(8)], "avt")
        S.dma('sp', Em[:], bm_in.rearrange("c h k j i -> j (c h k) i"), writes=['Em'])
        S.op('act', lambda: E['act'].activation(out=Em[:], in_=Em[:], func=AF.Exp), reads=['Em'], writes=['Em'])
        u = 0
        for h in range(8):
            for w in range(NW):
                q_, qk = qb.get()
                k_, kk = kb.get()
                v_, vk = vbf.get()
                a_, ak_ = acc.get()
                o_, ok = oo.get()
                w0 = w * WIN
                S.dma('sp', q_[:], qv[:, h, w0:w0 + WIN], writes=[qk])
                if w > 0:
                    S.dma('sp', k_[:], kv[:, h, w0 - WIN:w0 + WIN], writes=[kk])
                    S.dma('sp', v_[:], vv[:, h, w0 - WIN:w0 + WIN], writes=[vk])
                else:
                    S.dma('sp', k_[:, WIN:], kv[:, h, 0:WIN], writes=[kk])
                    S.dma('sp', v_[:, WIN:], vv[:, h, 0:WIN], writes=[vk])
                units = []
                for ci, dil in enumerate((1, 4, 16)):
                    span = 128 * dil
                    for nb_ in range(WIN // span):
                        for rho in range(dil):
                            units.append((ci, dil, span, nb_, rho))
                for g0 in range(0, len(units), 4):
                    U_ = []
                    for (ci, dil, span, nb_, rho) in units[g0:g0 + 4]:
                        c = {'ci': ci}
                        base = nb_ * span + rho
                        c['qsl'] = slice(base, base + 127 * dil + 1, dil)
                        has_prev = (w0 + nb_ * span) > 0
                        ksl_c = slice(WIN + base, WIN + base + 127 * dil + 1, dil)
                        ksl_p = slice(WIN + base - span, WIN + base - span + 127 * dil + 1, dil)
                        bs = (u % 4) * 2
                        u += 1
                        c['pS'], c['pSk'] = PSB[bs], ('psb', bs)
                        c['pO'], c['pOk'] = PSB[bs + 1], ('psb', bs + 1)
                        c['pVb'] = c['pS'][:].bitcast(BF16)[:, 512:768].rearrange("p (k e) -> p k e", k=2)
                        c['tiles'] = ([(0, ksl_p)] if has_prev else []) + [(1, ksl_c)]
                        c['p_lo'] = 0 if has_prev else 1
                        c['e0'] = (ci * 8 + h) * 2
                        U_.append(c)
                    for c in U_:
                        pS, pSk, pVb, tiles = c['pS'], c['pSk'], c['pVb'], c['tiles']
                        for (pc, ksl) in tiles:
                            S.op('pe', lambda: E['pe'].matmul(pS[:, pc * 128:(pc + 1) * 128], lhsT=k_[:, ksl], rhs=q_[:, c['qsl']], start=True, stop=True),
                                 reads=[kk, qk], writes=[pSk], inc=False)
                        for ii, (pc, ksl) in enumerate(tiles):
                            S.op('pe', lambda: E['pe'].transpose(pVb[:, pc, :], v_[:, ksl], ident_b[:]),
                                 reads=[vk, 'identb'], writes=[pSk], inc=(ii == len(tiles) - 1))
                    for c in U_:
                        pS, pSk, pVb, p_lo = c['pS'], c['pSk'], c['pVb'], c['p_lo']
                        c['ex'], c['exk'] = ex.get()
                        c['vt'], c['vtk'] = vtk_.get()
                        ex_, vt_ = c['ex'], c['vt']
                        S.op('act', lambda: E['act'].activation(out=ex_[:, p_lo:2, :], in_=pS[:, p_lo * 128:256].rearrange("p (k i) -> p k i", i=128), func=AF.Exp),
                             reads=[pSk], writes=[c['exk']])
                        S.op('act', lambda: E['act'].activation(out=vt_[:, p_lo:2, :], in_=pVb[:, p_lo:2, :], func=AF.Copy),
                             reads=[pSk], writes=[c['vtk']])
                    for c in U_:
                        p_lo, e0, ex_ = c['p_lo'], c['e0'], c['ex']
                        c['P'], c['Pk'] = Pb.get()
                        P_ = c['P']
                        S.op('pool', lambda: E['pool'].tensor_tensor(out=P_[:, p_lo:2, :], in0=ex_[:, p_lo:2, :], in1=Em[:, e0 + p_lo:e0 + 2, :], op=ALU.mult),
                             reads=[c['exk'], 'Em'], writes=[c['Pk']])
                    for c in U_:
                        pO, pOk, tiles, vt_, P_ = c['pO'], c['pOk'], c['tiles'], c['vt'], c['P']
                        for ii, (pc, ksl) in enumerate(tiles):
                            S.op('pe', lambda: E['pe'].matmul(pO[:, 0:128], lhsT=vt_[:, pc, :], rhs=P_[:, pc, :], start=(ii == 0), stop=(ii == len(tiles) - 1)),
                                 reads=[c['vtk'], c['Pk']], writes=[pOk], inc=False)
                        for ii, (pc, ksl) in enumerate(tiles):
                            S.op('pe', lambda: E['pe'].matmul(pO[:, 128:256], lhsT=ones_b[:], rhs=P_[:, pc, :], start=(ii == 0), stop=(ii == len(tiles) - 1)),
                                 reads=['onesb', c['Pk']], writes=[pOk], inc=(ii == len(tiles) - 1))
                    for c in U_:
                        pO, pOk, qsl = c['pO'], c['pOk'], c['qsl']
                        pOv = pO[:, 0:256].rearrange("p (k i) -> p k i", k=2)
                        if c['ci'] == 0:
                            S.op('dve', lambda: E['dve'].tensor_copy(out=a_[:, :, qsl], in_=pOv), reads=[pOk], writes=[ak_])
                        else:
                            S.op('dve', lambda: E['dve'].tensor_tensor(out=a_[:, :, qsl], in0=a_[:, :, qsl], in1=pOv, op=ALU.add),
                                 reads=[pOk, ak_], writes=[ak_])
                S.op('dve', lambda: E['dve'].reciprocal(out=a_[:, 1, :], in_=a_[:, 1, :]), reads=[ak_], writes=[ak_])
                S.op('pool', lambda: E['pool'].tensor_tensor(out=o_[:], in0=a_[:, 0, :], in1=a_[:, 1, :], op=ALU.mult), reads=[ak_], writes=[ok])
                S.dma('pool', ov[:, h, w0:w0 + WIN], o_[:], reads=[ok])
        S.barrier()
        stack.close()

    st = io_scope()
    phase_in(st)
    st['stack'].close()
    for l in range(L):
        st = gemm_scope()
        if l + 1 < L:
            filler[0] = cast_jobs(l + 1)
        for t in range(NT):
            ffn(st, l, t, 1)
            S.barrier()
            phase_inproj(st, l, t)
            S.barrier()
        drain_fill()
        S.barrier()
        st['stack'].close()
        phase_ret(l)
        phase_sgu(l)
        phase_att(l)
        st = gemm_scope()
        for t in range(NT):
            phase_proj(st, l, t)
            S.barrier()
            ffn(st, l, t, 2)
            S.barrier()
        st['stack'].close()
    st = io_scope()
    phase_out(st)
    st['stack'].close()
    es.close()
    return nc


def _t5_bucket_np(dist):
    max_exact = 16
    d_f = np.maximum(dist, 1).astype(np.float32)
    large = max_exact + (np.log(d_f / np.float32(max_exact)) / np.float32(math.log(2048 / max_exact))
                         * np.float32(32 - max_exact)).astype(np.int32)
    large = np.minimum(large, 31)
    return np.where(dist < max_exact, dist, large)


def host_consts(T, L, inp):
    c = {}
    colsl = []
    for nm in ("ffn1_norm", "mix_norm", "ffn2_norm"):
        for l in range(L):
            colsl.append(np.asarray(inp[nm][l]).reshape(16, 128).T)
    colsl.append(np.asarray(inp["final_norm"]).reshape(16, 128).T)
    for l in range(L):
        colsl.append(np.asarray(inp["b_gate"][l]).reshape(48, 128).T)
    c["cols"] = np.ascontiguousarray(np.concatenate(colsl, axis=1), dtype=np.float32)
    c["lng"] = np.ascontiguousarray(inp["sg_ln_g"][:L], dtype=np.float32)
    c["lnb"] = np.ascontiguousarray(inp["sg_ln_b"][:L], dtype=np.float32)
    c["sgwT"] = np.ascontiguousarray(np.swapaxes(np.asarray(inp["sg_w"][:L]), 2, 3), dtype=np.float32)
    c["sgb"] = np.ascontiguousarray(inp["sg_b"][:L], dtype=np.float32)
    rb = np.asarray(inp["rel_bias"], dtype=np.float32)
    bm = np.empty((3, 8, 2, 128, 128), np.float32)
    i = np.arange(128)[None, :]
    j = np.arange(128)[:, None]
    for ci, dil in enumerate((1, 4, 16)):
        for pc in range(2):
            steps = (128 + i - j) if pc == 0 else (i - j)
            valid = (steps >= 0) & (steps <= 128)
            bucket = _t5_bucket_np(dil * np.maximum(steps, 0))
            for h in range(8):
                bm[ci, h, pc] = np.where(valid, rb[bucket, h], np.float32(-30000.0))
    c["bm"] = bm
    pos = np.arange(T, dtype=np.float32)
    inv = (np.float32(10000.0) ** (-np.arange(0, 256, 2, dtype=np.float32) / np.float32(256))).astype(np.float32)
    ang = (pos[None, :] * inv[:, None]).astype(np.float32)
    cos, sin = np.cos(ang).astype(np.float32), np.sin(ang).astype(np.float32)
    rotq = np.empty((4, 2, 128, T), np.float32)
    rotk = np.empty((4, 2, 128, T), np.float32)
    pm = (np.arange(T) % 128).astype(np.float64)
    for h in range(4):
        lg = math.log1p(-(2.0 ** (-5.0 - h)))
        dq = np.exp((pm + 1.0) * lg)
        dk = np.exp(-(pm + 1.0) * lg) / 16.0
        rotq[h, 0] = cos * dq[None, :]
        rotq[h, 1] = sin * dq[None, :]
        rotk[h, 0] = cos * dk[None, :]
        rotk[h, 1] = sin * dk[None, :]
    c["rotq"] = rotq
    c["rotk"] = rotk
    c["cmask"] = np.triu(np.ones((128, 128), np.float32))
    c["ident"] = np.eye(128, dtype=np.float32)
    return c


_WMAP = {"wg1": "ffn1_w_gate", "wu1": "ffn1_w_up", "wd1": "ffn1_w_down", "win": "w_in", "wpr": "w_proj_ret",
         "wps": "w_proj_sg", "wpa": "w_proj_att", "wo": "w_out", "wg2": "ffn2_w_gate", "wu2": "ffn2_w_up",
         "wd2": "ffn2_w_down"}


def run(inp, T, L, ncores, nseq, dbg=False, trace=False):
    nc = build(T, L, dbg)
    c = host_consts(T, L, inp)
    base = dict(c)
    for k, v in _WMAP.items():
        base[k] = np.ascontiguousarray(np.asarray(inp[v])[:L], dtype=np.float32)
    in_maps = []
    for i in range(ncores):
        m = dict(base)
        m["x"] = np.ascontiguousarray(np.asarray(inp["x"])[i % nseq, :T], dtype=np.float32)
        in_maps.append(m)
    res = run_bass_kernel_spmd(nc, in_maps, core_ids=list(range(ncores)), **({"trace": True} if trace else {}))
    return res


def kernel(**inputs):
    res = run(inputs, SEQ, LDEPTH, 2, 2)
    out = np.stack([np.asarray(res.results[0]["y"]), np.asarray(res.results[1]["y"])], axis=0)
    return out.astype(np.float32)
```

```python
import math
import numpy as np
from contextlib import ExitStack
import concourse.bass as bass
import concourse.mybir as mybir
from concourse.bass_utils import run_bass_kernel_spmd

F32 = mybir.dt.float32
BF16 = mybir.dt.bfloat16
AF = mybir.ActivationFunctionType
ALU = mybir.AluOpType

D = 2048
DFF = 5632
DIN = 17408
TT = 512
EPS = 1e-6
NDS = 6
LDEPTH = 4
SEQ = 8192


class Sched:
    def __init__(s, nc, es):
        s.nc = nc
        s.E = {'pe': nc.tensor, 'act': nc.scalar, 'dve': nc.vector, 'pool': nc.gpsimd, 'sp': nc.sync}
        s.sem = {}
        s.cnt = {}
        for e in s.E:
            s.sem[e] = es.enter_context(nc.semaphore('s_' + e))
            s.cnt[e] = 0
        s.waited = {e: {} for e in s.E}
        s.hist = {}
        s.dq = {}
        s.dqi = {}
        for q in ('sp', 'act', 'pool'):
            s.dq[q] = [[es.enter_context(nc.semaphore('d_%s%d' % (q, i))), 0] for i in range(NDS)]
            s.dqi[q] = 0

    def _wait(s, e, toks):
        need = {}
        for t in toks:
            if t is None:
                continue
            sem, val = t
            k = id(sem)
            if e == 'pe' and sem is s.sem['pe']:
                continue
            if s.waited[e].get(k, 0) >= val:
                continue
            if k not in need or need[k][1] < val:
                need[k] = (sem, val)
        for k, (sem, val) in need.items():
            s.E[e].wait_ge(sem, val)
            s.waited[e][k] = val

    def _deps(s, reads, writes):
        toks = []
        for r in reads:
            h = s.hist.get(r)
            if h:
                toks.append(h[0])
        for w in writes:
            h = s.hist.get(w)
            if h:
                toks.append(h[0])
                toks.extend(h[1].values())
        return toks

    def _record(s, key, tok, reads, writes):
        for r in reads:
            h = s.hist.setdefault(r, [None, {}])
            h[1][key] = tok
        for w in writes:
            s.hist[w] = [tok, {}]

    def op(s, e, fn, reads=(), writes=(), inc=True):
        s._wait(e, s._deps(reads, writes))
        ins = fn()
        if inc:
            s.cnt[e] += 1
            ins.then_inc(s.sem[e], 1)
            tok = (s.sem[e], s.cnt[e])
        else:
            tok = (s.sem[e], s.cnt[e] + 1)
        s._record(e, tok, reads, writes)
        return tok

    def dma(s, q, out, in_, reads=(), writes=()):
        slot = s.dq[q][s.dqi[q] % NDS]
        s.dqi[q] += 1
        toks = s._deps(reads, writes)
        if slot[1] > 0:
            toks.append((slot[0], slot[1]))
        s._wait(q, toks)
        ins = s.E[q].dma_start(out=out, in_=in_)
        slot[1] += 16
        ins.then_inc(slot[0], 16)
        tok = (slot[0], slot[1])
        s._record(id(slot[0]), tok, reads, writes)
        return tok

    def barrier(s):
        toks = [(s.sem[e], s.cnt[e]) for e in s.E if s.cnt[e] > 0]
        for q in s.dq:
            for sl in s.dq[q]:
                if sl[1] > 0:
                    toks.append((sl[0], sl[1]))
        for e in s.E:
            s._wait(e, toks)
        s.hist.clear()


class Pool:
    def __init__(s, tiles, name):
        s.tiles = tiles
        s.name = name
        s.i = 0

    def get(s):
        k = s.i % len(s.tiles)
        s.i += 1
        return s.tiles[k], (s.name, k)


def build(T, L, dbg=False):
    NT = T // TT
    NCH = T // 128
    nc = bass.Bass("TRN2", target_bir_lowering=False)
    es = ExitStack()
    KIN = "ExternalInput"
    KSC = "ExternalOutput" if dbg else "Internal"

    def din(name, shape, dt=F32):
        return nc.dram_tensor(name, list(shape), dt, kind=KIN).ap()

    def dsc(name, shape, dt):
        return nc.dram_tensor(name, list(shape), dt, kind=KSC).ap()

    x_in = din("x", [T, D])
    W = {}
    wspec = [("wg1", D, DFF), ("wu1", D, DFF), ("wd1", DFF, D), ("win", D, DIN), ("wpr", 2048, D),
             ("wps", 1024, D), ("wpa", 1024, D), ("wo", D, D), ("wg2", D, DFF), ("wu2", D, DFF), ("wd2", DFF, D)]
    WB = {}
    for nm, K, N in wspec:
        W[nm] = din(nm, [L, K, N])
        cb = 128 if nm in ("wd1", "wd2") else 256
        WB[nm] = ([nc.dram_tensor("%sb%d" % (nm, l_), [N // cb, 128, K // 128, cb], BF16, kind="Internal").ap() for l_ in range(L)], K // 128, cb)
    NCOLS = (3 * L + 1) * 16 + L * 48
    cols_in = din("cols", [128, NCOLS])
    lng_in = din("lng", [L, 1024])
    lnb_in = din("lnb", [L, 1024])
    sgwT_in = din("sgwT", [L, 4, 128, 128])
    sgb_in = din("sgb", [L, 4, 128])
    bm_in = din("bm", [3, 8, 2, 128, 128])
    rotq_in = din("rotq", [4, 2, 128, T])
    rotk_in = din("rotk", [4, 2, 128, T])
    cmask_in = din("cmask", [128, 128])
    ident_in = din("ident", [128, 128])
    y_out = nc.dram_tensor("y", [T, D], F32, kind="ExternalOutput").ap()

    xT = dsc("xT", [D, T], F32)
    qrT = dsc("qrT", [1024, T], BF16)
    krT = dsc("krT", [1024, T], BF16)
    rvT = dsc("rvT", [2048, T], BF16)
    rgT = dsc("rgT", [2048, T], BF16)
    suT = dsc("suT", [1024, T], BF16)
    svT = dsc("svT", [1024, T], F32)
    aqT = dsc("aqT", [1024, T], BF16)
    akT = dsc("akT", [1024, T], BF16)
    avT = dsc("avT", [1024, T], BF16)
    gT = dsc("gT", [6144, T], BF16)
    retoT = dsc("retoT", [2048, T], BF16)
    sgoT = dsc("sgoT", [1024, T], BF16)
    attoT = dsc("attoT", [1024, T], BF16)

    S = Sched(nc, es)
    E = S.E

    uid = [0]

    def sb(name, shape, dt, stack=None):
        uid[0] += 1
        return (stack or es).enter_context(nc.sbuf_tensor("sb%d_%s" % (uid[0], name), list(shape), dt))

    PSB = [es.enter_context(nc.psum_tensor("psb%d" % i, [128, 512], F32)) for i in range(8)]
    ps_pool = Pool(PSB, "psb")

    cols = sb("cols", [128, NCOLS], F32)
    ident_f = sb("identf", [128, 128], F32)
    ident_b = sb("identb", [128, 128], BF16)
    ones_b = sb("onesb", [128, 128], BF16)
    cmask = sb("cmaskf", [128, 128], F32)
    epsc = sb("epsc", [128, 1], F32)
    S.dma('sp', cols[:], cols_in, writes=['cols'])
    S.dma('sp', ident_f[:], ident_in, writes=['identf'])
    S.dma('sp', cmask[:], cmask_in, writes=['cmask'])
    S.op('pool', lambda: E['pool'].memset(ones_b[:], 1.0), writes=['onesb'])
    S.op('pool', lambda: E['pool'].memset(epsc[:], EPS), writes=['epsc'])
    S.op('dve', lambda: E['dve'].tensor_copy(out=ident_b[:], in_=ident_f[:]), reads=['identf'], writes=['identb'])

    def cast_jobs(l):
        for nm, K, N in wspec:
            wb, KC, cb = WB[nm]
            for c0 in range(N // cb):
                src = W[nm][l][:, c0 * cb:(c0 + 1) * cb].rearrange("(kc p) c -> p kc c", p=128)
                yield (wb[l][c0], src)

    filler = [iter(())]

    def fill(n=1):
        for _ in range(n):
            j = next(filler[0], None)
            if j is None:
                return
            S.dma('pool', j[0], j[1])

    def drain_fill():
        while True:
            j = next(filler[0], None)
            if j is None:
                return
            S.dma('pool', j[0], j[1])

    for j in cast_jobs(0):
        S.dma('pool', j[0], j[1])
    S.barrier()

    def colv(idx):
        return cols[:, idx:idx + 1]

    def c_norm(kind, l):
        return (kind * L + l) * 16

    C_FINAL = 3 * L * 16
    C_BG = (3 * L + 1) * 16

    xTv = xT.rearrange("(m p) t -> p m t", p=128)

    def gemm_blocks(st, wname, l, nblocks, rhs_fn, KC, epi, wpool, b0=0):
        wb, KCw, cb = WB[wname]
        assert KCw == KC
        nsub = cb // 128
        pend = []
        PF = 2
        blocks = list(range(b0, b0 + nblocks))
        loaded = {}

        def load(bi):
            wt, wk = wpool.get()
            S.dma('sp', wt[:, 0:KC, 0:cb], wb[l][bi], writes=[wk])
            loaded[bi] = (wt, wk)
        for bi in blocks[:PF]:
            load(bi)
        for ii, bi in enumerate(blocks):
            if ii + PF < len(blocks):
                load(blocks[ii + PF])
            wt, wk = loaded.pop(bi)
            fill()
            pss = []
            for sub in range(nsub):
                ps, pk = ps_pool.get()
                for kc in range(KC):
                    rap, rk = rhs_fn(kc)
                    S.op('pe', lambda: E['pe'].matmul(ps[:], lhsT=wt[:, kc, sub * 128:(sub + 1) * 128], rhs=rap,
                                                      start=(kc == 0), stop=(kc == KC - 1)),
                         reads=[wk, rk], writes=[pk], inc=(kc == KC - 1))
                pss.append((ps, pk))
            epi(bi, pss)

    def rms_norm(st, t, gcol0, out_fn):
        ps, pk = ps_pool.get()
        xn = st['xn16']
        for m in range(16):
            S.dma('sp', xn[:, m, :], xTv[:, m, t * TT:(t + 1) * TT], reads=[('xT', m, t)], writes=[('xn16', m)])
        for m in range(16):
            sq, sk = st['sq'].get()
            S.op('act', lambda: E['act'].activation(out=sq[:], in_=xn[:, m, :], func=AF.Square), reads=[('xn16', m)], writes=[sk])
            S.op('pe', lambda: E['pe'].matmul(ps[:], lhsT=ones_b[:], rhs=sq[:], start=(m == 0), stop=(m == 15)),
                 reads=['onesb', sk], writes=[pk], inc=True)
        rs = st['rstd']
        S.op('act', lambda: E['act'].activation(out=rs[:], in_=ps[:], func=AF.Sqrt, bias=epsc[:], scale=1.0 / D),
             reads=[pk, 'epsc'], writes=['rstd'])
        S.op('dve', lambda: E['dve'].reciprocal(out=rs[:], in_=rs[:]), reads=['rstd'], writes=['rstd'])
        for m in range(16):
            out_fn(m, xn[:, m, :], ('xn16', m), rs)

    def norm_to_hT(st, t, gcol0):
        hT = st['hT']

        def o(m, xs, xk, rs):
            S.op('dve', lambda: E['dve'].scalar_tensor_tensor(out=hT[:, m, :], in0=xs, scalar=colv(gcol0 + m),
                                                              in1=rs[:], op0=ALU.mult, op1=ALU.mult),
                 reads=[xk, 'rstd', 'cols'], writes=[('hT', m)])
        rms_norm(st, t, gcol0, o)

    def resid_epi(st, t, scale):
        def epi(bi, pss):
            for sub, (ps, pk) in enumerate(pss):
                m = bi * len(pss) + sub
                xs, xk = st['xs'].get()
                S.dma('sp', xs[:], xTv[:, m, t * TT:(t + 1) * TT], reads=[('xT', m, t)], writes=[xk])
                S.op('dve', lambda: E['dve'].scalar_tensor_tensor(out=xs[:], in0=ps[:], scalar=scale, in1=xs[:],
                                                                  op0=ALU.mult, op1=ALU.add),
                     reads=[pk, xk], writes=[xk])
                S.dma('pool', xTv[:, m, t * TT:(t + 1) * TT], xs[:], reads=[xk], writes=[('xT', m, t)])
        return epi

    def ffn(st, l, t, which):
        wg, wu, wd = ("wg1", "wu1", "wd1") if which == 1 else ("wg2", "wu2", "wd2")
        norm_to_hT(st, t, c_norm(0 if which == 1 else 2, l))
        hT = st['hT']
        aT = st['aT']
        wbg, _, _ = WB[wg]
        wbu, _, _ = WB[wu]
        PF = 1
        nb = DFF // 256

        def loadgu(bi):
            wt, wk = st['wA'].get()
            S.dma('sp', wt[:], wbg[l][bi], writes=[wk])
            wt2, wk2 = st['wA'].get()
            S.dma('sp', wt2[:], wbu[l][bi], writes=[wk2])
            return (wt, wk, wt2, wk2)
        q = [loadgu(bi) for bi in range(min(PF, nb))]
        for bi in range(nb):
            if bi + PF < nb:
                q.append(loadgu(bi + PF))
            wt, wk, wt2, wk2 = q.pop(0)
            fill()
            for sub in range(2):
                m = bi * 2 + sub
                pg, pgk = ps_pool.get()
                pu, puk = ps_pool.get()
                for (ps, pk, w_, wk_) in ((pg, pgk, wt, wk), (pu, puk, wt2, wk2)):
                    for kc in range(16):
                        S.op('pe', lambda: E['pe'].matmul(ps[:], lhsT=w_[:, kc, sub * 128:(sub + 1) * 128],
                                                          rhs=hT[:, kc, :], start=(kc == 0), stop=(kc == 15)),
                             reads=[wk_, ('hT', kc)], writes=[pk], inc=(kc == 15))
                sg, sgk = st['f32'].get()
                S.op('act', lambda: E['act'].activation(out=sg[:], in_=pg[:], func=AF.Silu), reads=[pgk], writes=[sgk])
                S.op('dve', lambda: E['dve'].tensor_tensor(out=aT[:, m, :], in0=sg[:], in1=pu[:], op=ALU.mult),
                     reads=[sgk, puk], writes=[('aT', m)])
        gemm_blocks(st, wd, l, 16, lambda kc: (aT[:, kc, :], ('aT', kc)), 44, resid_epi(st, t, 0.5), st['wB'])

    def gemm_scope():
        stack = ExitStack()
        st = {'stack': stack}
        st['hT'] = sb("hT", [128, 16, TT], BF16, stack)
        st['aT'] = sb("aT", [128, 44, TT], BF16, stack)
        st['wA'] = Pool([sb("wA%d" % i, [128, 16, 256], BF16, stack) for i in range(4)], "wA")
        st['wB'] = Pool([sb("wB%d" % i, [128, 44, 128], BF16, stack) for i in range(3)], "wB")
        st['xs'] = Pool([sb("xs%d" % i, [128, TT], F32, stack) for i in range(3)], "xs")
        st['xn16'] = sb("xn16", [128, 16, TT], F32, stack)
        st['sq'] = Pool([sb("sq%d" % i, [128, TT], BF16, stack) for i in range(3)], "sq")
        st['f32'] = Pool([sb("f32_%d" % i, [128, TT], F32, stack) for i in range(6)], "f32")
        st['ob'] = Pool([sb("ob%d" % i, [128, TT], BF16, stack) for i in range(5)], "ob")
        st['rstd'] = sb("rstd", [128, TT], F32, stack)
        st['rot'] = Pool([sb("rot%d" % i, [128, 2, TT], F32, stack) for i in range(2)], "rot")
        st['g3'] = Pool([sb("g3_%d" % i, [128, 3, TT], BF16, stack) for i in range(2)], "g3")
        return st

    def io_scope():
        stack = ExitStack()
        st = {'stack': stack}
        st['xn16'] = sb("ixn16", [128, 16, TT], F32, stack)
        st['sq'] = Pool([sb("isq%d" % i, [128, TT], BF16, stack) for i in range(2)], "sq")
        st['rstd'] = sb("irstd", [128, TT], F32, stack)
        st['xrow'] = sb("xrow", [128, D], F32, stack)
        st['stg'] = sb("stg", [128, 16, TT], F32, stack)
        return st

    def phase_in(st):
        stg = st['stg']
        xr = st['xrow']
        for t in range(NT):
            for s4 in range(4):
                S.dma('sp', xr[:], x_in[t * TT + s4 * 128:t * TT + (s4 + 1) * 128, :], writes=['xrow'])
                for mg in range(4):
                    ps, pk = ps_pool.get()
                    for j in range(4):
                        m = mg * 4 + j
                        S.op('pe', lambda: E['pe'].transpose(ps[:, j * 128:(j + 1) * 128], xr[:, m * 128:(m + 1) * 128], ident_f[:]),
                             reads=['xrow', 'identf'], writes=[pk], inc=(j == 3))
                    S.op('act', lambda: E['act'].activation(
                        out=stg[:, mg * 4:(mg + 1) * 4, s4 * 128:(s4 + 1) * 128],
                        in_=ps[:].rearrange("p (j c) -> p j c", j=4), func=AF.Copy),
                        reads=[pk], writes=['stg'])
            S.dma('pool', xTv[:, :, t * TT:(t + 1) * TT], stg[:], reads=['stg'])
        S.barrier()

    def phase_inproj(st, l, t):
        norm_to_hT(st, t, c_norm(1, l))
        hT = st['hT']
        tsl = slice(t * TT, (t + 1) * TT)

        def store(dst, m_local, ob, obk):
            S.dma('pool', dst[m_local * 128:(m_local + 1) * 128, tsl], ob[:], reads=[obk])

        def epi(bi, pss):
            (p0, k0), (p1, k1) = pss
            if bi < 8:
                isq = bi < 4
                h = bi if isq else bi - 4
                tab = rotq_in if isq else rotk_in
                dst = qrT if isq else krT
                rt, rtk = st['rot'].get()
                S.dma('sp', rt[:], tab[h, :, :, tsl].rearrange("c p t -> p c t"), writes=[rtk])
                t1, t1k = st['f32'].get()
                t2, t2k = st['f32'].get()
                S.op('act', lambda: E['act'].activation(out=t1[:], in_=p0[:], func=AF.Copy), reads=[k0], writes=[t1k])
                S.op('act', lambda: E['act'].activation(out=t2[:], in_=p1[:], func=AF.Copy), reads=[k1], writes=[t2k])
                a, ak = st['f32'].get()
                b, bk = st['f32'].get()
                o1, o1k = st['ob'].get()
                o2, o2k = st['ob'].get()
                S.op('pool', lambda: E['pool'].tensor_tensor(out=a[:], in0=t1[:], in1=rt[:, 0, :], op=ALU.mult), reads=[t1k, rtk], writes=[ak])
                S.op('pool', lambda: E['pool'].tensor_tensor(out=b[:], in0=t2[:], in1=rt[:, 1, :], op=ALU.mult), reads=[t2k, rtk], writes=[bk])
                S.op('pool', lambda: E['pool'].tensor_tensor(out=o1[:], in0=a[:], in1=b[:], op=ALU.subtract), reads=[ak, bk], writes=[o1k])
                S.op('dve', lambda: E['dve'].tensor_tensor(out=t1[:], in0=t1[:], in1=rt[:, 1, :], op=ALU.mult), reads=[t1k, rtk, ak], writes=[t1k])
                S.op('dve', lambda: E['dve'].tensor_tensor(out=t2[:], in0=t2[:], in1=rt[:, 0, :], op=ALU.mult), reads=[t2k, rtk, bk], writes=[t2k])
                S.op('dve', lambda: E['dve'].tensor_tensor(out=o2[:], in0=t1[:], in1=t2[:], op=ALU.add), reads=[t1k, t2k], writes=[o2k])
                store(dst, 2 * h, o1, o1k)
                store(dst, 2 * h + 1, o2, o2k)
                return
            for sub, (ps, pk) in enumerate(pss):
                m = bi * 2 + sub
                if m < 32:
                    ob, obk = st['ob'].get()
                    if sub == 0:
                        S.op('act', lambda: E['act'].activation(out=ob[:], in_=ps[:], func=AF.Copy), reads=[pk], writes=[obk])
                    else:
                        S.op('dve', lambda: E['dve'].tensor_copy(out=ob[:], in_=ps[:]), reads=[pk], writes=[obk])
                    store(rvT, m - 16, ob, obk)
                elif m < 48:
                    ob, obk = st['ob'].get()
                    S.op('act', lambda: E['act'].activation(out=ob[:], in_=ps[:], func=AF.Silu), reads=[pk], writes=[obk])
                    store(rgT, m - 32, ob, obk)
                elif m < 64:
                    xs_, xk_ = st['f32'].get()
                    u, uk = st['f32'].get()
                    S.op('act', lambda: E['act'].activation(out=xs_[:], in_=ps[:], func=AF.Copy), reads=[pk], writes=[xk_])
                    S.op('act', lambda: E['act'].activation(out=u[:], in_=ps[:], func=AF.Square), reads=[pk], writes=[uk])
                    S.op('dve', lambda: E['dve'].tensor_scalar(out=u[:], in0=u[:], scalar1=0.044715, scalar2=1.0, op0=ALU.mult, op1=ALU.add),
                         reads=[uk], writes=[uk])
                    S.op('dve', lambda: E['dve'].tensor_tensor(out=u[:], in0=u[:], in1=xs_[:], op=ALU.mult), reads=[uk, xk_], writes=[uk])
                    S.op('act', lambda: E['act'].activation(out=u[:], in_=u[:], func=AF.Sigmoid, scale=1.5957691216057308),
                         reads=[uk], writes=[uk])
                    if m < 56:
                        ob, obk = st['ob'].get()
                        S.op('dve', lambda: E['dve'].tensor_tensor(out=ob[:], in0=u[:], in1=xs_[:], op=ALU.mult), reads=[uk, xk_], writes=[obk])
                        store(suT, m - 48, ob, obk)
                    else:
                        S.op('dve', lambda: E['dve'].tensor_tensor(out=u[:], in0=u[:], in1=xs_[:], op=ALU.mult), reads=[uk, xk_], writes=[uk])
                        store(svT, m - 56, u, uk)
                elif m < 88:
                    ob, obk = st['ob'].get()
                    sc = (128 ** -0.5) if m < 72 else 1.0
                    dst = aqT if m < 72 else (akT if m < 80 else avT)
                    mb = 64 if m < 72 else (72 if m < 80 else 80)
                    S.op('act', lambda: E['act'].activation(out=ob[:], in_=ps[:], func=AF.Copy, scale=sc), reads=[pk], writes=[obk])
                    store(dst, m - mb, ob, obk)
                else:
                    ob, obk = st['ob'].get()
                    S.op('act', lambda: E['act'].activation(out=ob[:], in_=ps[:], func=AF.Sigmoid, bias=colv(C_BG + l * 48 + (m - 88)), scale=1.0),
                         reads=[pk, 'cols'], writes=[obk])
                    store(gT, m - 88, ob, obk)
        gemm_blocks(st, "win", l, 68, lambda kc: (hT[:, kc, :], ('hT', kc)), 16, epi, st['wA'])

    def phase_proj(st, l, t):
        tsl = slice(t * TT, (t + 1) * TT)
        rt_ = st['hT']
        aT = st['aT']
        sgt = aT[:, 0:8, :]
        att = aT[:, 8:16, :]
        mg = aT[:, 16:32, :]
        S.dma('sp', rt_[:], retoT.rearrange("(m p) t -> p m t", p=128)[:, :, tsl], writes=[('hT', k) for k in range(16)])
        S.dma('sp', sgt, sgoT.rearrange("(m p) t -> p m t", p=128)[:, :, tsl], writes=[('aT', k) for k in range(8)])
        S.dma('sp', att, attoT.rearrange("(m p) t -> p m t", p=128)[:, :, tsl], writes=[('aT', 8 + k) for k in range(8)])
        gTv = gT.rearrange("(b m p) t -> p b m t", b=3, p=128)
        wbr, _, _ = WB["wpr"]
        wbs, _, _ = WB["wps"]
        wba, _, _ = WB["wpa"]
        def loadp(bi):
            w1, w1k = st['wA'].get()
            S.dma('sp', w1[:], wbr[l][bi], writes=[w1k])
            w2, w2k = st['wA'].get()
            S.dma('sp', w2[:, 0:8, :], wbs[l][bi], writes=[w2k])
            S.dma('sp', w2[:, 8:16, :], wba[l][bi], writes=[w2k])
            return (w1, w1k, w2, w2k)
        pq = [loadp(0)]
        for bi in range(8):
            if bi + 1 < 8:
                pq.append(loadp(bi + 1))
            w1, w1k, w2, w2k = pq.pop(0)
            for sub in range(2):
                m = bi * 2 + sub
                pa, pak = ps_pool.get()
                pb, pbk = ps_pool.get()
                pc, pck = ps_pool.get()
                for kc in range(16):
                    S.op('pe', lambda: E['pe'].matmul(pa[:], lhsT=w1[:, kc, sub * 128:(sub + 1) * 128], rhs=rt_[:, kc, :],
                                                      start=(kc == 0), stop=(kc == 15)),
                         reads=[w1k, ('hT', kc)], writes=[pak], inc=(kc == 15))
                for kc in range(8):
                    S.op('pe', lambda: E['pe'].matmul(pb[:], lhsT=w2[:, kc, sub * 128:(sub + 1) * 128], rhs=sgt[:, kc, :],
                                                      start=(kc == 0), stop=(kc == 7)),
                         reads=[w2k, ('aT', kc)], writes=[pbk], inc=(kc == 7))
                for kc in range(8):
                    S.op('pe', lambda: E['pe'].matmul(pc[:], lhsT=w2[:, 8 + kc, sub * 128:(sub + 1) * 128], rhs=att[:, kc, :],
                                                      start=(kc == 0), stop=(kc == 7)),
                         reads=[w2k, ('aT', 8 + kc)], writes=[pck], inc=(kc == 7))
                g3, g3k = st['g3'].get()
                S.dma('sp', g3[:], gTv[:, :, m, tsl], writes=[g3k])
                t1, t1k = st['f32'].get()
                t2, t2k = st['f32'].get()
                t3, t3k = st['f32'].get()
                S.op('dve', lambda: E['dve'].tensor_tensor(out=t1[:], in0=pa[:], in1=g3[:, 0, :], op=ALU.mult), reads=[pak, g3k], writes=[t1k])
                S.op('dve', lambda: E['dve'].tensor_tensor(out=t2[:], in0=pb[:], in1=g3[:, 1, :], op=ALU.mult), reads=[pbk, g3k], writes=[t2k])
                S.op('dve', lambda: E['dve'].tensor_tensor(out=t3[:], in0=pc[:], in1=g3[:, 2, :], op=ALU.mult), reads=[pck, g3k], writes=[t3k])
                S.op('pool', lambda: E['pool'].tensor_tensor(out=t1[:], in0=t1[:], in1=t2[:], op=ALU.add), reads=[t1k, t2k], writes=[t1k])
                S.op('pool', lambda: E['pool'].tensor_tensor(out=mg[:, m, :], in0=t1[:], in1=t3[:], op=ALU.add), reads=[t1k, t3k], writes=[('aT', 16 + m)])
        gemm_blocks(st, "wo", l, 8, lambda kc: (mg[:, kc, :], ('aT', 16 + kc)), 16, resid_epi(st, t, 1.0), st['wA'])

    def phase_out(st):
        stg = st['stg']
        orow = st['xrow']
        for t in range(NT):
            def o(m, xs, xk, rs):
                S.op('dve', lambda: E['dve'].scalar_tensor_tensor(out=stg[:, m, :], in0=xs, scalar=colv(C_FINAL + m),
                                                                  in1=rs[:], op0=ALU.mult, op1=ALU.mult),
                     reads=[xk, 'rstd', 'cols'], writes=[('stg', m)])
            rms_norm(st, t, C_FINAL, o)
            for s4 in range(4):
                for mg_ in range(4):
                    ps, pk = ps_pool.get()
                    for j in range(4):
                        m = mg_ * 4 + j
                        S.op('pe', lambda: E['pe'].transpose(ps[:, j * 128:(j + 1) * 128], stg[:, m, s4 * 128:(s4 + 1) * 128], ident_f[:]),
                             reads=[('stg', m), 'identf'], writes=[pk], inc=(j == 3))
                    S.op('act', lambda: E['act'].activation(out=orow[:, mg_ * 512:(mg_ + 1) * 512], in_=ps[:], func=AF.Copy),
                         reads=[pk], writes=['xrow'])
                S.dma('pool', y_out[t * TT + s4 * 128:t * TT + (s4 + 1) * 128, :], orow[:], reads=['xrow'])
            S.barrier()

    def phase_ret(l):
        stack = ExitStack()
        SC = 256
        qv = qrT.rearrange("(m p) t -> p m t", p=128)
        kv = krT.rearrange("(m p) t -> p m t", p=128)
        vv = rvT.rearrange("(m p) t -> p m t", p=128)
        gv = rgT.rearrange("(m p) t -> p m t", p=128)
        ov = retoT.rearrange("(m p) t -> p m t", p=128)
        qs = Pool([sb("rq%d" % i, [128, 8, SC], BF16, stack) for i in range(2)], "rq")
        ks = Pool([sb("rk%d" % i, [128, 8, SC], BF16, stack) for i in range(2)], "rk")
        vs = Pool([sb("rv%d" % i, [128, 16, SC], BF16, stack) for i in range(2)], "rv")
        gs = Pool([sb("rg%d" % i, [128, 16, SC], BF16, stack) for i in range(2)], "rg")
        os_ = Pool([sb("ro%d" % i, [128, 16, SC], BF16, stack) for i in range(2)], "ro")
        R32 = [sb("R32_%d" % h, [128, 2, 512], F32, stack) for h in range(4)]
        Rbf = [sb("Rbf_%d" % h, [128, 2, 512], BF16, stack) for h in range(4)]
        kh = Pool([sb("kh%d" % i, [128, 256], BF16, stack) for i in range(4)], "kh")
        vt = Pool([sb("vt%d" % i, [128, 512], BF16, stack) for i in range(4)], "vt")
        At = Pool([sb("At%d" % i, [128, 128], BF16, stack) for i in range(4)], "At")
        osb = Pool([sb("osb%d" % i, [128, 512], F32, stack) for i in range(4)], "osb")
        sqb = Pool([sb("sqb%d" % i, [128, 512], BF16, stack) for i in range(4)], "sqb")
        rin = Pool([sb("rin%d" % i, [128, 128], F32, stack) for i in range(4)], "rin")
        for h in range(4):
            S.op('pool', lambda: E['pool'].memset(R32[h][:], 0.0), writes=[('R32', h)])
            S.op('pool', lambda: E['pool'].memset(Rbf[h][:], 0.0), writes=[('Rbf', h)])
        bA = [(PSB[0], ('psb', 0)), (PSB[1], ('psb', 1))]
        bB = [(PSB[2], ('psb', 2)), (PSB[3], ('psb', 3))]
        bC = [(PSB[4], ('psb', 4)), (PSB[5], ('psb', 5))]
        bU = [(PSB[6], ('psb', 6)), (PSB[7], ('psb', 7))]
        for sc in range(T // SC):
            tsl = slice(sc * SC, (sc + 1) * SC)
            q_, qk = qs.get()
            k_, kk = ks.get()
            v_, vk = vs.get()
            g_, gk = gs.get()
            o_, ok = os_.get()
            S.dma('sp', q_[:], qv[:, :, tsl], writes=[qk])
            S.dma('sp', k_[:], kv[:, :, tsl], writes=[kk])
            S.dma('sp', v_[:], vv[:, :, tsl], writes=[vk])
            S.dma('sp', g_[:], gv[:, :, tsl], writes=[gk])
            for cc in range(SC // 128):
                csl = slice(cc * 128, (cc + 1) * 128)
                for hp in range(2):
                    U_ = []
                    for ui, h in enumerate((2 * hp, 2 * hp + 1)):
                        c = {'h': h}
                        c['gC'] = float(np.float32(np.exp(np.float32(128.0) * np.log1p(np.float32(-(2.0 ** (-5.0 - h)))))))
                        c['pA'], c['pAk'] = bA[ui]
                        c['pB'], c['pBk'] = bB[ui]
                        c['pC'], c['pCk'] = bC[ui]
                        c['pU'], c['pUk'] = bU[ui]
                        c['pAb'] = c['pA'][:].bitcast(BF16)
                        U_.append(c)
                    for c in U_:
                        h = c['h']
                        pAb, pAk, pB, pBk = c['pAb'], c['pAk'], c['pB'], c['pBk']
                        for dch in range(2):
                            S.op('pe', lambda: E['pe'].transpose(pAb[:, dch * 128:(dch + 1) * 128], k_[:, 2 * h + dch, csl], ident_b[:]),
                                 reads=[kk, 'identb'], writes=[pAk], inc=False)
                        for ech in range(4):
                            S.op('pe', lambda: E['pe'].transpose(pAb[:, 256 + ech * 128:256 + (ech + 1) * 128], v_[:, 4 * h + ech, csl], ident_b[:]),
                                 reads=[vk, 'identb'], writes=[pAk], inc=(ech == 3))
                        for dch in range(2):
                            S.op('pe', lambda: E['pe'].matmul(pB[:, 0:128], lhsT=k_[:, 2 * h + dch, csl], rhs=q_[:, 2 * h + dch, csl],
                                                              start=(dch == 0), stop=(dch == 1)),
                                 reads=[kk, qk], writes=[pBk], inc=(dch == 1))
                    for c in U_:
                        pAb, pAk, pB, pBk = c['pAb'], c['pAk'], c['pB'], c['pBk']
                        c['kh'], c['khk'] = kh.get()
                        c['vt'], c['vtk'] = vt.get()
                        c['A'], c['Ak'] = At.get()
                        kh_, vt_, A_ = c['kh'], c['vt'], c['A']
                        gC = c['gC']
                        S.op('act', lambda: E['act'].activation(out=kh_[:], in_=pAb[:, 0:256], func=AF.Copy, scale=gC), reads=[pAk], writes=[c['khk']])
                        S.op('act', lambda: E['act'].activation(out=vt_[:], in_=pAb[:, 256:768], func=AF.Copy), reads=[pAk], writes=[c['vtk']])
                        S.op('dve', lambda: E['dve'].tensor_tensor(out=A_[:], in0=pB[:, 0:128], in1=cmask[:], op=ALU.mult),
                             reads=[pBk, 'cmask'], writes=[c['Ak']])
                    for c in U_:
                        h = c['h']
                        pC, pCk, vt_, A_, kh_ = c['pC'], c['pCk'], c['vt'], c['A'], c['kh']
                        for ech in range(4):
                            oc = pC[:, ech * 128:(ech + 1) * 128]
                            S.op('pe', lambda: E['pe'].matmul(oc, lhsT=vt_[:, ech * 128:(ech + 1) * 128], rhs=A_[:], start=True, stop=False),
                                 reads=[c['vtk'], c['Ak']], writes=[pCk], inc=False)
                            for dch in range(2):
                                S.op('pe', lambda: E['pe'].matmul(oc, lhsT=Rbf[h][:, dch, ech * 128:(ech + 1) * 128], rhs=q_[:, 2 * h + dch, csl],
                                                                  start=False, stop=(dch == 1)),
                                     reads=[('Rbf', h), qk], writes=[pCk], inc=(dch == 1 and ech == 3))
                    for c in U_:
                        pC, pCk = c['pC'], c['pCk']
                        c['ob'], c['obk'] = osb.get()
                        c['sq'], c['sqk'] = sqb.get()
                        ob_, sq_ = c['ob'], c['sq']
                        S.op('act', lambda: E['act'].activation(out=ob_[:], in_=pC[:], func=AF.Copy), reads=[pCk], writes=[c['obk']])
                        S.op('act', lambda: E['act'].activation(out=sq_[:], in_=pC[:], func=AF.Square), reads=[pCk], writes=[c['sqk']])
                    for c in U_:
                        pB, pBk, sq_ = c['pB'], c['pBk'], c['sq']
                        for ech in range(4):
                            S.op('pe', lambda: E['pe'].matmul(pB[:, 128:256], lhsT=ones_b[:], rhs=sq_[:, ech * 128:(ech + 1) * 128],
                                                              start=(ech == 0), stop=(ech == 3)),
                                 reads=['onesb', c['sqk']], writes=[pBk], inc=(ech == 3))
                    for dch in range(2):
                        for c in U_:
                            h = c['h']
                            pU, pUk, kh_, vt_ = c['pU'], c['pUk'], c['kh'], c['vt']
                            gC = c['gC']
                            S.op('pe', lambda: E['pe'].matmul(pU[:], lhsT=kh_[:, dch * 128:(dch + 1) * 128], rhs=vt_[:], start=True, stop=True),
                                 reads=[c['khk'], c['vtk']], writes=[pUk], inc=True)
                            S.op('dve', lambda: E['dve'].scalar_tensor_tensor(out=R32[h][:, dch, :], in0=R32[h][:, dch, :], scalar=gC,
                                                                              in1=pU[:], op0=ALU.mult, op1=ALU.add),
                                 reads=[pUk, ('R32', h)], writes=[('R32', h)])
                    for c in U_:
                        h = c['h']
                        pB, pBk, ob_ = c['pB'], c['pBk'], c['ob']
                        ri, rik = rin.get()
                        S.op('act', lambda: E['act'].activation(out=Rbf[h][:], in_=R32[h][:], func=AF.Copy),
                             reads=[('R32', h)], writes=[('Rbf', h)])
                        S.op('act', lambda: E['act'].activation(out=ri[:], in_=pB[:, 128:256], func=AF.Sqrt, bias=epsc[:], scale=1.0 / 512),
                             reads=[pBk, 'epsc'], writes=[rik])
                        S.op('dve', lambda: E['dve'].reciprocal(out=ri[:], in_=ri[:]), reads=[rik], writes=[rik])
                        S.op('pool', lambda: E['pool'].tensor_tensor(out=ob_[:].rearrange("p (e i) -> p e i", e=4),
                                                                     in0=ob_[:].rearrange("p (e i) -> p e i", e=4),
                                                                     in1=ri[:].unsqueeze(1).broadcast_to([128, 4, 128]), op=ALU.mult),
                             reads=[c['obk'], rik], writes=[c['obk']])
                        S.op('pool', lambda: E['pool'].tensor_tensor(out=o_[:, 4 * h:4 * h + 4, csl],
                                                                     in0=ob_[:].rearrange("p (e i) -> p e i", e=4),
                                                                     in1=g_[:, 4 * h:4 * h + 4, csl], op=ALU.mult),
                             reads=[c['obk'], gk], writes=[ok])
            S.dma('pool', ov[:, :, tsl], o_[:], reads=[ok])
        S.barrier()
        stack.close()

    def phase_sgu(l):
        stack = ExitStack()
        SC = 256
        uv = suT.rearrange("(m p) t -> p m t", p=128)
        vv = svT.rearrange("(m p) t -> p m t", p=128)
        ov = sgoT.rearrange("(m p) t -> p m t", p=128)
        us = Pool([sb("su%d" % i, [128, 8, SC], BF16, stack) for i in range(2)], "su")
        vs = Pool([sb("sv%d" % i, [128, 8, SC], F32, stack) for i in range(2)], "sv")
        os_ = Pool([sb("so%d" % i, [128, 8, SC], BF16, stack) for i in range(2)], "so")
        lng = sb("lng", [128, 1024], F32, stack)
        lnb = sb("lnb", [128, 1024], F32, stack)
        wcf = sb("wcf", [128, 4, 128], F32, stack)
        wcb = sb("wcb", [128, 4, 128], BF16, stack)
        sgb = sb("sgb", [128, 4, 128], F32, stack)
        xn = Pool([sb("xn%d" % i, [128, 1024], F32, stack) for i in range(2)], "xn")
        vb = Pool([sb("vb%d" % i, [128, 1024], BF16, stack) for i in range(2)], "vb")
        stt = Pool([sb("stt%d" % i, [128, 16], F32, stack) for i in range(2)], "stt")
        tm = Pool([sb("tm%d" % i, [128, 8, 128], F32, stack) for i in range(2)], "tm")
        S.dma('sp', lng[:], lng_in[l].partition_broadcast(128), writes=['lng'])
        S.dma('sp', lnb[:], lnb_in[l].partition_broadcast(128), writes=['lnb'])
        S.dma('sp', wcf[:], sgwT_in[l].rearrange("g s t -> s g t"), writes=['wcf'])
        S.dma('sp', sgb[:], sgb_in[l].partition_broadcast(128), writes=['sgb'])
        S.op('dve', lambda: E['dve'].tensor_tensor(out=wcb[:], in0=wcf[:], in1=cmask[:].unsqueeze(1).broadcast_to([128, 4, 128]), op=ALU.mult),
             reads=['wcf', 'cmask'], writes=['wcb'])
        for sc in range(T // SC):
            tsl = slice(sc * SC, (sc + 1) * SC)
            u_, uk = us.get()
            v_, vk = vs.get()
            o_, ok = os_.get()
            S.dma('sp', u_[:], uv[:, :, tsl], writes=[uk])
            S.dma('sp', v_[:], vv[:, :, tsl], writes=[vk])
            U_ = []
            for cc in range(SC // 128):
                c = {'csl': slice(cc * 128, (cc + 1) * 128)}
                b0 = (cc % 2) * 4
                c['p'] = [(PSB[b0 + i], ('psb', b0 + i)) for i in range(4)]
                U_.append(c)
            for c in U_:
                (p0, p0k), (p1, p1k) = c['p'][0], c['p'][1]
                for f in range(8):
                    pp, ppk = (p0, p0k) if f < 4 else (p1, p1k)
                    S.op('pe', lambda: E['pe'].transpose(pp[:, (f % 4) * 128:(f % 4 + 1) * 128], v_[:, f, c['csl']], ident_f[:]),
                         reads=[vk, 'identf'], writes=[ppk], inc=(f % 4 == 3))
            for c in U_:
                (p0, p0k), (p1, p1k) = c['p'][0], c['p'][1]
                c['st'], c['stk'] = stt.get()
                st_, stk = c['st'], c['stk']
                S.op('dve', lambda: E['dve'].bn_stats(out=st_[:, 0:6], in_=p0[:]), reads=[p0k], writes=[stk])
                S.op('dve', lambda: E['dve'].bn_stats(out=st_[:, 6:12], in_=p1[:]), reads=[p1k, stk], writes=[stk])
                S.op('dve', lambda: E['dve'].bn_aggr(out=st_[:, 12:14], in_=st_[:, 0:12]), reads=[stk], writes=[stk])
                S.op('act', lambda: E['act'].activation(out=st_[:, 14:15], in_=st_[:, 13:14], func=AF.Sqrt, bias=epsc[:], scale=1.0),
                     reads=[stk, 'epsc'], writes=[stk])
            for c in U_:
                (p0, p0k), (p1, p1k) = c['p'][0], c['p'][1]
                st_, stk = c['st'], c['stk']
                S.op('dve', lambda: E['dve'].reciprocal(out=st_[:, 15:16], in_=st_[:, 14:15]), reads=[stk], writes=[stk])
                c['xn'], c['xnk'] = xn.get()
                xn_, xnk = c['xn'], c['xnk']
                for half, (pp, ppk) in enumerate(((p0, p0k), (p1, p1k))):
                    S.op('dve', lambda: E['dve'].tensor_scalar(out=xn_[:, half * 512:(half + 1) * 512], in0=pp[:], scalar1=st_[:, 12:13],
                                                               scalar2=st_[:, 15:16], op0=ALU.subtract, op1=ALU.mult),
                         reads=[ppk, stk], writes=[xnk])
            for c in U_:
                xn_, xnk = c['xn'], c['xnk']
                c['vb'], c['vbk'] = vb.get()
                vb_, vbk = c['vb'], c['vbk']
                S.op('pool', lambda: E['pool'].tensor_tensor(out=xn_[:], in0=xn_[:], in1=lng[:], op=ALU.mult), reads=[xnk, 'lng'], writes=[xnk])
                S.op('pool', lambda: E['pool'].tensor_tensor(out=vb_[:], in0=xn_[:], in1=lnb[:], op=ALU.add), reads=[xnk, 'lnb'], writes=[vbk])
            for c in U_:
                (p2, p2k), (p3, p3k) = c['p'][2], c['p'][3]
                vb_, vbk = c['vb'], c['vbk']
                for f in range(8):
                    pp, ppk = (p2, p2k) if f < 4 else (p3, p3k)
                    S.op('pe', lambda: E['pe'].matmul(pp[:, (f % 4) * 128:(f % 4 + 1) * 128], lhsT=vb_[:, f * 128:(f + 1) * 128], rhs=wcb[:, f // 2, :],
                                                      start=True, stop=True),
                         reads=[vbk, 'wcb'], writes=[ppk], inc=(f % 4 == 3))
            for c in U_:
                (p2, p2k), (p3, p3k) = c['p'][2], c['p'][3]
                c['tm'], c['tmk'] = tm.get()
                tm_, tmk = c['tm'], c['tmk']
                for half, (pp, ppk) in enumerate(((p2, p2k), (p3, p3k))):
                    S.op('dve', lambda: E['dve'].tensor_tensor(
                        out=tm_[:, half * 4:(half + 1) * 4, :].rearrange("p (g two) t -> p g two t", two=2),
                        in0=pp[:].rearrange("p (g two t) -> p g two t", g=2, two=2),
                        in1=sgb[:, 2 * half:2 * half + 2, :].unsqueeze(2).broadcast_to([128, 2, 2, 128]), op=ALU.add),
                        reads=[ppk, 'sgb'], writes=[tmk])
            for c in U_:
                tm_, tmk = c['tm'], c['tmk']
                S.op('pool', lambda: E['pool'].tensor_tensor(out=o_[:, :, c['csl']], in0=tm_[:], in1=u_[:, :, c['csl']], op=ALU.mult),
                     reads=[tmk, uk], writes=[ok])
            S.dma('pool', ov[:, :, tsl], o_[:], reads=[ok])
        S.barrier()
        stack.close()

    def phase_att(l):
        stack = ExitStack()
        WIN = 2048
        NW = T // WIN
        qv = aqT.rearrange("(m p) t -> p m t", p=128)
        kv = akT.rearrange("(m p) t -> p m t", p=128)
        vv = avT.rearrange("(m p) t -> p m t", p=128)
        ov = attoT.rearrange("(m p) t -> p m t", p=128)
        Em = sb("Em", [128, 48, 128], F32, stack)
        qb = Pool([sb("aq%d" % i, [128, WIN], BF16, stack) for i in range(2)], "aq")
        kb = Pool([sb("ak%d" % i, [128, 2 * WIN], BF16, stack) for i in range(2)], "ak")
        vbf = Pool([sb("av%d" % i, [128, 2 * WIN], BF16, stack) for i in range(2)], "av")
        acc = Pool([sb("acc%d" % i, [128, 2, WIN], F32, stack) for i in range(2)], "acc")
        oo = Pool([sb("ao%d" % i, [128, WIN], BF16, stack) for i in range(2)], "ao")
        ex = Pool([sb("ex%d" % i, [128, 2, 2, 128], F32, stack) for i in range(4)], "ex")
        Pb = Pool([sb("Pb%d" % i, [128, 2, 2, 128], BF16, stack) for i in range(4)], "Pb")
        vtk_ = Pool([sb("avt%d" % i, [128, 2, 2, 128], BF16, stack) for i in range(4)], "avt")
        S.dma('sp', Em[:], bm_in.rearrange("c h k j i -> j (c h k) i"), writes=['Em'])
        S.op('act', lambda: E['act'].activation(out=Em[:], in_=Em[:], func=AF.Exp), reads=['Em'], writes=['Em'])
        u = 0
        for h in range(8):
            for w in range(NW):
                q_, qk = qb.get()
                k_, kk = kb.get()
                v_, vk = vbf.get()
                a_, ak_ = acc.get()
                o_, ok = oo.get()
                w0 = w * WIN
                S.dma('sp', q_[:], qv[:, h, w0:w0 + WIN], writes=[qk])
                if w > 0:
                    S.dma('sp', k_[:], kv[:, h, w0 - WIN:w0 + WIN], writes=[kk])
                    S.dma('sp', v_[:], vv[:, h, w0 - WIN:w0 + WIN], writes=[vk])
                else:
                    S.dma('sp', k_[:, WIN:], kv[:, h, 0:WIN], writes=[kk])
                    S.dma('sp', v_[:, WIN:], vv[:, h, 0:WIN], writes=[vk])
                batches = []
                for ci, dil in enumerate((1, 4, 16)):
                    span = 128 * dil
                    ulist = [(nb_, rho) for nb_ in range(WIN // span) for rho in range(dil)]
                    for i2 in range(0, len(ulist), 2):
                        pair = ulist[i2:i2 + 2]
                        hp = [(w0 + nb_ * span) > 0 for (nb_, rho) in pair]
                        if hp[0] == hp[1]:
                            batches.append((ci, dil, span, pair, hp[0]))
                        else:
                            batches.append((ci, dil, span, pair[0:1], hp[0]))
                            batches.append((ci, dil, span, pair[1:2], hp[1]))
                for g0 in range(0, len(batches), 2):
                    U_ = []
                    for (ci, dil, span, pair, has_prev) in batches[g0:g0 + 2]:
                        c = {'ci': ci, 'nu': len(pair), 'dil': dil, 'span': span, 'pair': pair}
                        bs = (u % 2) * 3
                        u += 1
                        c['pS'], c['pSk'] = PSB[bs], ('psb', bs)
                        c['pV'], c['pVk'] = PSB[bs + 1], ('psb', bs + 1)
                        c['pO'], c['pOk'] = PSB[bs + 2], ('psb', bs + 2)
                        c['pVb'] = c['pV'][:].bitcast(BF16)
                        c['p_lo'] = 0 if has_prev else 1
                        c['e0'] = (ci * 8 + h) * 2
                        c['units'] = []
                        for (nb_, rho) in pair:
                            base = nb_ * span + rho
                            qsl = slice(base, base + 127 * dil + 1, dil)
                            ksl_c = slice(WIN + base, WIN + base + 127 * dil + 1, dil)
                            ksl_p = slice(WIN + base - span, WIN + base - span + 127 * dil + 1, dil)
                            tiles = ([(0, ksl_p)] if has_prev else []) + [(1, ksl_c)]
                            c['units'].append((qsl, tiles))
                        U_.append(c)
                    for c in U_:
                        pS, pSk, pVb, pVk = c['pS'], c['pSk'], c['pVb'], c['pVk']
                        nun = len(c['units'])
                        for ui, (qsl, tiles) in enumerate(c['units']):
                            for ii, (pc, ksl) in enumerate(tiles):
                                S.op('pe', lambda: E['pe'].matmul(pS[:, ui * 256 + pc * 128:ui * 256 + (pc + 1) * 128], lhsT=k_[:, ksl], rhs=q_[:, qsl], start=True, stop=True),
                                     reads=[kk, qk], writes=[pSk], inc=(ui == nun - 1 and ii == len(tiles) - 1))
                        for ui, (qsl, tiles) in enumerate(c['units']):
                            for ii, (pc, ksl) in enumerate(tiles):
                                S.op('pe', lambda: E['pe'].transpose(pVb[:, ui * 256 + pc * 128:ui * 256 + (pc + 1) * 128], v_[:, ksl], ident_b[:]),
                                     reads=[vk, 'identb'], writes=[pVk], inc=(ui == nun - 1 and ii == len(tiles) - 1))
                    for c in U_:
                        pS, pSk, pVb, pVk, p_lo, nu = c['pS'], c['pSk'], c['pVb'], c['pVk'], c['p_lo'], c['nu']
                        c['ex'], c['exk'] = ex.get()
                        c['vt'], c['vtk'] = vtk_.get()
                        ex_, vt_ = c['ex'], c['vt']
                        S.op('act', lambda: E['act'].activation(out=ex_[:, 0:nu, p_lo:2, :],
                                                                in_=pS[:, 0:nu * 256].rearrange("p (u k i) -> p u k i", u=nu, k=2)[:, :, p_lo:2, :], func=AF.Exp),
                             reads=[pSk], writes=[c['exk']])
                        S.op('act', lambda: E['act'].activation(out=vt_[:, 0:nu, p_lo:2, :],
                                                                in_=pVb[:, 0:nu * 256].rearrange("p (u k i) -> p u k i", u=nu, k=2)[:, :, p_lo:2, :], func=AF.Copy),
                             reads=[pVk], writes=[c['vtk']])
                    for c in U_:
                        p_lo, e0, ex_, nu = c['p_lo'], c['e0'], c['ex'], c['nu']
                        c['P'], c['Pk'] = Pb.get()
                        P_ = c['P']
                        S.op('pool', lambda: E['pool'].tensor_tensor(out=P_[:, 0:nu, p_lo:2, :], in0=ex_[:, 0:nu, p_lo:2, :],
                                                                     in1=Em[:, e0 + p_lo:e0 + 2, :].unsqueeze(1).broadcast_to([128, nu, 2 - p_lo, 128]), op=ALU.mult),
                             reads=[c['exk'], 'Em'], writes=[c['Pk']])
                    for c in U_:
                        pO, pOk, vt_, P_ = c['pO'], c['pOk'], c['vt'], c['P']
                        nun = len(c['units'])
                        for ui, (qsl, tiles) in enumerate(c['units']):
                            for ii, (pc, ksl) in enumerate(tiles):
                                S.op('pe', lambda: E['pe'].matmul(pO[:, ui * 256:ui * 256 + 128], lhsT=vt_[:, ui, pc, :], rhs=P_[:, ui, pc, :], start=(ii == 0), stop=(ii == len(tiles) - 1)),
                                     reads=[c['vtk'], c['Pk']], writes=[pOk], inc=False)
                            for ii, (pc, ksl) in enumerate(tiles):
                                S.op('pe', lambda: E['pe'].matmul(pO[:, ui * 256 + 128:ui * 256 + 256], lhsT=ones_b[:], rhs=P_[:, ui, pc, :], start=(ii == 0), stop=(ii == len(tiles) - 1)),
                                     reads=['onesb', c['Pk']], writes=[pOk], inc=(ui == nun - 1 and ii == len(tiles) - 1))
                    for c in U_:
                        pO, pOk, nu, dil, span = c['pO'], c['pOk'], c['nu'], c['dil'], c['span']
                        nb0, rho0 = c['pair'][0]
                        pOv = pO[:, 0:nu * 256].rearrange("p (u k i) -> p k u i", u=nu, k=2)
                        if dil == 1:
                            dst = a_[:, :, nb0 * 128:(nb0 + nu) * 128].rearrange("p k (u i) -> p k u i", u=nu)
                        else:
                            dst = a_[:, :, nb0 * span:(nb0 + 1) * span].rearrange("p k (i d) -> p k d i", d=dil)[:, :, rho0:rho0 + nu, :]
                        if c['ci'] == 0:
                            S.op('dve', lambda: E['dve'].tensor_copy(out=dst, in_=pOv), reads=[pOk], writes=[ak_])
                        else:
                            S.op('dve', lambda: E['dve'].tensor_tensor(out=dst, in0=dst, in1=pOv, op=ALU.add),
                                 reads=[pOk, ak_], writes=[ak_])
                S.op('dve', lambda: E['dve'].reciprocal(out=a_[:, 1, :], in_=a_[:, 1, :]), reads=[ak_], writes=[ak_])
                S.op('pool', lambda: E['pool'].tensor_tensor(out=o_[:], in0=a_[:, 0, :], in1=a_[:, 1, :], op=ALU.mult), reads=[ak_], writes=[ok])
                S.dma('pool', ov[:, h, w0:w0 + WIN], o_[:], reads=[ok])
        S.barrier()
        stack.close()

    st = io_scope()
    phase_in(st)
    st['stack'].close()
    for l in range(L):
        st = gemm_scope()
        if l + 1 < L:
            filler[0] = cast_jobs(l + 1)
        for t in range(NT):
            ffn(st, l, t, 1)
            phase_inproj(st, l, t)
        drain_fill()
        S.barrier()
        st['stack'].close()
        phase_ret(l)
        phase_sgu(l)
        phase_att(l)
        st = gemm_scope()
        for t in range(NT):
            phase_proj(st, l, t)
            ffn(st, l, t, 2)
        S.barrier()
        st['stack'].close()
    st = io_scope()
    phase_out(st)
    st['stack'].close()
    es.close()
    return nc


def _t5_bucket_np(dist):
    max_exact = 16
    d_f = np.maximum(dist, 1).astype(np.float32)
    large = max_exact + (np.log(d_f / np.float32(max_exact)) / np.float32(math.log(2048 / max_exact))
                         * np.float32(32 - max_exact)).astype(np.int32)
    large = np.minimum(large, 31)
    return np.where(dist < max_exact, dist, large)


def host_consts(T, L, inp):
    c = {}
    colsl = []
    for nm in ("ffn1_norm", "mix_norm", "ffn2_norm"):
        for l in range(L):
            colsl.append(np.asarray(inp[nm][l]).reshape(16, 128).T)
    colsl.append(np.asarray(inp["final_norm"]).reshape(16, 128).T)
    for l in range(L):
        colsl.append(np.asarray(inp["b_gate"][l]).reshape(48, 128).T)
    c["cols"] = np.ascontiguousarray(np.concatenate(colsl, axis=1), dtype=np.float32)
    c["lng"] = np.ascontiguousarray(inp["sg_ln_g"][:L], dtype=np.float32)
    c["lnb"] = np.ascontiguousarray(inp["sg_ln_b"][:L], dtype=np.float32)
    c["sgwT"] = np.ascontiguousarray(np.swapaxes(np.asarray(inp["sg_w"][:L]), 2, 3), dtype=np.float32)
    c["sgb"] = np.ascontiguousarray(inp["sg_b"][:L], dtype=np.float32)
    rb = np.asarray(inp["rel_bias"], dtype=np.float32)
    bm = np.empty((3, 8, 2, 128, 128), np.float32)
    i = np.arange(128)[None, :]
    j = np.arange(128)[:, None]
    for ci, dil in enumerate((1, 4, 16)):
        for pc in range(2):
            steps = (128 + i - j) if pc == 0 else (i - j)
            valid = (steps >= 0) & (steps <= 128)
            bucket = _t5_bucket_np(dil * np.maximum(steps, 0))
            for h in range(8):
                bm[ci, h, pc] = np.where(valid, rb[bucket, h], np.float32(-30000.0))
    c["bm"] = bm
    pos = np.arange(T, dtype=np.float32)
    inv = (np.float32(10000.0) ** (-np.arange(0, 256, 2, dtype=np.float32) / np.float32(256))).astype(np.float32)
    ang = (pos[None, :] * inv[:, None]).astype(np.float32)
    cos, sin = np.cos(ang).astype(np.float32), np.sin(ang).astype(np.float32)
    rotq = np.empty((4, 2, 128, T), np.float32)
    rotk = np.empty((4, 2, 128, T), np.float32)
    pm = (np.arange(T) % 128).astype(np.float64)
    for h in range(4):
        lg = math.log1p(-(2.0 ** (-5.0 - h)))
        dq = np.exp((pm + 1.0) * lg)
        dk = np.exp(-(pm + 1.0) * lg) / 16.0
        rotq[h, 0] = cos * dq[None, :]
        rotq[h, 1] = sin * dq[None, :]
        rotk[h, 0] = cos * dk[None, :]
        rotk[h, 1] = sin * dk[None, :]
    c["rotq"] = rotq
    c["rotk"] = rotk
    c["cmask"] = np.triu(np.ones((128, 128), np.float32))
    c["ident"] = np.eye(128, dtype=np.float32)
    return c


_WMAP = {"wg1": "ffn1_w_gate", "wu1": "ffn1_w_up", "wd1": "ffn1_w_down", "win": "w_in", "wpr": "w_proj_ret",
         "wps": "w_proj_sg", "wpa": "w_proj_att", "wo": "w_out", "wg2": "ffn2_w_gate", "wu2": "ffn2_w_up",
         "wd2": "ffn2_w_down"}


def run(inp, T, L, ncores, nseq, dbg=False, trace=False):
    nc = build(T, L, dbg)
    c = host_consts(T, L, inp)
    base = dict(c)
    for k, v in _WMAP.items():
        base[k] = np.ascontiguousarray(np.asarray(inp[v])[:L], dtype=np.float32)
    in_maps = []
    for i in range(ncores):
        m = dict(base)
        m["x"] = np.ascontiguousarray(np.asarray(inp["x"])[i % nseq, :T], dtype=np.float32)
        in_maps.append(m)
    res = run_bass_kernel_spmd(nc, in_maps, core_ids=list(range(ncores)), **({"trace": True} if trace else {}))
    return res


def kernel(**inputs):
    res = run(inputs, SEQ, LDEPTH, 2, 2)
    out = np.stack([np.asarray(res.results[0]["y"]), np.asarray(res.results[1]["y"])], axis=0)
    return out.astype(np.float32)
```

```python
import math
import numpy as np
from contextlib import ExitStack
import concourse.bass as bass
import concourse.mybir as mybir
from concourse.bass_utils import run_bass_kernel_spmd

F32 = mybir.dt.float32
BF16 = mybir.dt.bfloat16
AF = mybir.ActivationFunctionType
ALU = mybir.AluOpType

D = 2048
DFF = 5632
DIN = 17408
TT = 512
EPS = 1e-6
NDS = 6
LDEPTH = 4
SEQ = 8192


class Sched:
    def __init__(s, nc, es):
        s.nc = nc
        s.E = {'pe': nc.tensor, 'act': nc.scalar, 'dve': nc.vector, 'pool': nc.gpsimd, 'sp': nc.sync}
        s.sem = {}
        s.cnt = {}
        for e in s.E:
            s.sem[e] = es.enter_context(nc.semaphore('s_' + e))
            s.cnt[e] = 0
        s.waited = {e: {} for e in s.E}
        s.hist = {}
        s.dq = {}
        s.dqi = {}
        for q in ('sp', 'act', 'pool'):
            s.dq[q] = [[es.enter_context(nc.semaphore('d_%s%d' % (q, i))), 0] for i in range(NDS)]
            s.dqi[q] = 0

    def _wait(s, e, toks):
        need = {}
        for t in toks:
            if t is None:
                continue
            sem, val = t
            k = id(sem)
            if e == 'pe' and sem is s.sem['pe']:
                continue
            if s.waited[e].get(k, 0) >= val:
                continue
            if k not in need or need[k][1] < val:
                need[k] = (sem, val)
        for k, (sem, val) in need.items():
            s.E[e].wait_ge(sem, val)
            s.waited[e][k] = val

    def _deps(s, reads, writes):
        toks = []
        for r in reads:
            h = s.hist.get(r)
            if h:
                toks.append(h[0])
        for w in writes:
            h = s.hist.get(w)
            if h:
                toks.append(h[0])
                toks.extend(h[1].values())
        return toks

    def _record(s, key, tok, reads, writes):
        for r in reads:
            h = s.hist.setdefault(r, [None, {}])
            h[1][key] = tok
        for w in writes:
            s.hist[w] = [tok, {}]

    def op(s, e, fn, reads=(), writes=(), inc=True):
        s._wait(e, s._deps(reads, writes))
        ins = fn()
        if inc:
            s.cnt[e] += 1
            ins.then_inc(s.sem[e], 1)
            tok = (s.sem[e], s.cnt[e])
        else:
            tok = (s.sem[e], s.cnt[e] + 1)
        s._record(e, tok, reads, writes)
        return tok

    def dma(s, q, out, in_, reads=(), writes=()):
        slot = s.dq[q][s.dqi[q] % NDS]
        s.dqi[q] += 1
        toks = s._deps(reads, writes)
        if slot[1] > 0:
            toks.append((slot[0], slot[1]))
        s._wait(q, toks)
        ins = s.E[q].dma_start(out=out, in_=in_)
        slot[1] += 16
        ins.then_inc(slot[0], 16)
        tok = (slot[0], slot[1])
        s._record(id(slot[0]), tok, reads, writes)
        return tok

    def barrier(s):
        toks = [(s.sem[e], s.cnt[e]) for e in s.E if s.cnt[e] > 0]
        for q in s.dq:
            for sl in s.dq[q]:
                if sl[1] > 0:
                    toks.append((sl[0], sl[1]))
        for e in s.E:
            s._wait(e, toks)
        s.hist.clear()


class Pool:
    def __init__(s, tiles, name):
        s.tiles = tiles
        s.name = name
        s.i = 0

    def get(s):
        k = s.i % len(s.tiles)
        s.i += 1
        return s.tiles[k], (s.name, k)


def build(T, L, dbg=False):
    NT = T // TT
    NCH = T // 128
    nc = bass.Bass("TRN2", target_bir_lowering=False)
    es = ExitStack()
    KIN = "ExternalInput"
    KSC = "ExternalOutput" if dbg else "Internal"

    def din(name, shape, dt=F32):
        return nc.dram_tensor(name, list(shape), dt, kind=KIN).ap()

    def dsc(name, shape, dt):
        return nc.dram_tensor(name, list(shape), dt, kind=KSC).ap()

    x_in = din("x", [T, D])
    W = {}
    wspec = [("wg1", D, DFF), ("wu1", D, DFF), ("wd1", DFF, D), ("win", D, DIN), ("wpr", 2048, D),
             ("wps", 1024, D), ("wpa", 1024, D), ("wo", D, D), ("wg2", D, DFF), ("wu2", D, DFF), ("wd2", DFF, D)]
    WB = {}
    for nm, K, N in wspec:
        W[nm] = din(nm, [L, K, N])
        cb = 128 if nm in ("wd1", "wd2") else 256
        WB[nm] = ([nc.dram_tensor("%sb%d" % (nm, l_), [N // cb, 128, K // 128, cb], BF16, kind="Internal").ap() for l_ in range(L)], K // 128, cb)
    NCOLS = (3 * L + 1) * 16 + L * 48
    cols_in = din("cols", [128, NCOLS])
    lng_in = din("lng", [L, 1024])
    lnb_in = din("lnb", [L, 1024])
    sgwT_in = din("sgwT", [L, 4, 128, 128])
    sgb_in = din("sgb", [L, 4, 128])
    bm_in = din("bm", [3, 8, 2, 128, 128])
    rotq_in = din("rotq", [4, 2, 128, T])
    rotk_in = din("rotk", [4, 2, 128, T])
    cmask_in = din("cmask", [128, 128])
    ident_in = din("ident", [128, 128])
    y_out = nc.dram_tensor("y", [T, D], F32, kind="ExternalOutput").ap()

    xT = dsc("xT", [D, T], F32)
    qrT = dsc("qrT", [1024, T], BF16)
    krT = dsc("krT", [1024, T], BF16)
    rvT = dsc("rvT", [2048, T], BF16)
    rgT = dsc("rgT", [2048, T], BF16)
    suT = dsc("suT", [1024, T], BF16)
    svT = dsc("svT", [1024, T], F32)
    aqT = dsc("aqT", [1024, T], BF16)
    akT = dsc("akT", [1024, T], BF16)
    avT = dsc("avT", [1024, T], BF16)
    gT = dsc("gT", [6144, T], BF16)
    retoT = dsc("retoT", [2048, T], BF16)
    sgoT = dsc("sgoT", [1024, T], BF16)
    attoT = dsc("attoT", [1024, T], BF16)

    S = Sched(nc, es)
    E = S.E

    uid = [0]

    def sb(name, shape, dt, stack=None):
        uid[0] += 1
        return (stack or es).enter_context(nc.sbuf_tensor("sb%d_%s" % (uid[0], name), list(shape), dt))

    PSB = [es.enter_context(nc.psum_tensor("psb%d" % i, [128, 512], F32)) for i in range(8)]
    ps_pool = Pool(PSB, "psb")

    cols = sb("cols", [128, NCOLS], F32)
    ident_f = sb("identf", [128, 128], F32)
    ident_b = sb("identb", [128, 128], BF16)
    ones_b = sb("onesb", [128, 128], BF16)
    cmask = sb("cmaskf", [128, 128], F32)
    epsc = sb("epsc", [128, 1], F32)
    S.dma('sp', cols[:], cols_in, writes=['cols'])
    S.dma('sp', ident_f[:], ident_in, writes=['identf'])
    S.dma('sp', cmask[:], cmask_in, writes=['cmask'])
    S.op('pool', lambda: E['pool'].memset(ones_b[:], 1.0), writes=['onesb'])
    S.op('pool', lambda: E['pool'].memset(epsc[:], EPS), writes=['epsc'])
    S.op('dve', lambda: E['dve'].tensor_copy(out=ident_b[:], in_=ident_f[:]), reads=['identf'], writes=['identb'])

    def cast_jobs(l):
        for nm, K, N in wspec:
            wb, KC, cb = WB[nm]
            for c0 in range(N // cb):
                src = W[nm][l][:, c0 * cb:(c0 + 1) * cb].rearrange("(kc p) c -> p kc c", p=128)
                yield (wb[l][c0], src)

    filler = [iter(())]

    def fill(n=1):
        for _ in range(n):
            j = next(filler[0], None)
            if j is None:
                return
            S.dma('pool', j[0], j[1])

    def drain_fill():
        while True:
            j = next(filler[0], None)
            if j is None:
                return
            S.dma('pool', j[0], j[1])

    for j in cast_jobs(0):
        S.dma('pool', j[0], j[1])
    S.barrier()

    def colv(idx):
        return cols[:, idx:idx + 1]

    def c_norm(kind, l):
        return (kind * L + l) * 16

    C_FINAL = 3 * L * 16
    C_BG = (3 * L + 1) * 16

    xTv = xT.rearrange("(m p) t -> p m t", p=128)

    def gemm_blocks(st, wname, l, nblocks, rhs_fn, KC, epi, wpool, b0=0):
        wb, KCw, cb = WB[wname]
        assert KCw == KC
        nsub = cb // 128
        pend = []
        PF = len(wpool.tiles) - 1
        blocks = list(range(b0, b0 + nblocks))
        loaded = {}

        def load(bi):
            wt, wk = wpool.get()
            S.dma('sp', wt[:, 0:KC, 0:cb], wb[l][bi], writes=[wk])
            loaded[bi] = (wt, wk)
        for bi in blocks[:PF]:
            load(bi)
        for ii, bi in enumerate(blocks):
            if ii + PF < len(blocks):
                load(blocks[ii + PF])
            wt, wk = loaded.pop(bi)
            fill()
            pss = []
            for sub in range(nsub):
                ps, pk = ps_pool.get()
                for kc in range(KC):
                    rap, rk = rhs_fn(kc)
                    S.op('pe', lambda: E['pe'].matmul(ps[:], lhsT=wt[:, kc, sub * 128:(sub + 1) * 128], rhs=rap,
                                                      start=(kc == 0), stop=(kc == KC - 1)),
                         reads=[wk, rk], writes=[pk], inc=(kc == KC - 1))
                pss.append((ps, pk))
            epi(bi, pss)

    def rms_norm(st, t, gcol0, out_fn):
        ps, pk = ps_pool.get()
        xn = st['xn16']
        for m in range(16):
            S.dma('sp', xn[:, m, :], xTv[:, m, t * TT:(t + 1) * TT], reads=[('xT', m, t)], writes=[('xn16', m)])
        for m in range(16):
            sq, sk = st['sq'].get()
            if m % 2 == 0:
                S.op('act', lambda: E['act'].activation(out=sq[:], in_=xn[:, m, :], func=AF.Square), reads=[('xn16', m)], writes=[sk])
            else:
                S.op('dve', lambda: E['dve'].tensor_tensor(out=sq[:], in0=xn[:, m, :], in1=xn[:, m, :], op=ALU.mult), reads=[('xn16', m)], writes=[sk])
            S.op('pe', lambda: E['pe'].matmul(ps[:], lhsT=ones_b[:], rhs=sq[:], start=(m == 0), stop=(m == 15)),
                 reads=['onesb', sk], writes=[pk], inc=True)
        rs = st['rstd']
        S.op('act', lambda: E['act'].activation(out=rs[:], in_=ps[:], func=AF.Sqrt, bias=epsc[:], scale=1.0 / D),
             reads=[pk, 'epsc'], writes=['rstd'])
        S.op('dve', lambda: E['dve'].reciprocal(out=rs[:], in_=rs[:]), reads=['rstd'], writes=['rstd'])
        for m in range(16):
            out_fn(m, xn[:, m, :], ('xn16', m), rs)

    def norm_to_hT(st, t, gcol0):
        hT = st['hT']

        def o(m, xs, xk, rs):
            S.op('dve', lambda: E['dve'].scalar_tensor_tensor(out=hT[:, m, :], in0=xs, scalar=colv(gcol0 + m),
                                                              in1=rs[:], op0=ALU.mult, op1=ALU.mult),
                 reads=[xk, 'rstd', 'cols'], writes=[('hT', m)])
        rms_norm(st, t, gcol0, o)

    def resid_epi(st, t, scale):
        def epi(bi, pss):
            for sub, (ps, pk) in enumerate(pss):
                m = bi * len(pss) + sub
                xs, xk = st['xs'].get()
                S.dma('sp', xs[:], xTv[:, m, t * TT:(t + 1) * TT], reads=[('xT', m, t)], writes=[xk])
                S.op('dve', lambda: E['dve'].scalar_tensor_tensor(out=xs[:], in0=ps[:], scalar=scale, in1=xs[:],
                                                                  op0=ALU.mult, op1=ALU.add),
                     reads=[pk, xk], writes=[xk])
                S.dma('pool', xTv[:, m, t * TT:(t + 1) * TT], xs[:], reads=[xk], writes=[('xT', m, t)])
        return epi

    def ffn(st, l, t, which):
        wg, wu, wd = ("wg1", "wu1", "wd1") if which == 1 else ("wg2", "wu2", "wd2")
        norm_to_hT(st, t, c_norm(0 if which == 1 else 2, l))
        hT = st['hT']
        aT = st['aT']
        wbg, _, _ = WB[wg]
        wbu, _, _ = WB[wu]
        PF = 1
        nb = DFF // 256

        def loadgu(bi):
            wt, wk = st['wA'].get()
            S.dma('sp', wt[:], wbg[l][bi], writes=[wk])
            wt2, wk2 = st['wA'].get()
            S.dma('sp', wt2[:], wbu[l][bi], writes=[wk2])
            return (wt, wk, wt2, wk2)
        q = [loadgu(bi) for bi in range(min(PF, nb))]
        for bi in range(nb):
            if bi + PF < nb:
                q.append(loadgu(bi + PF))
            wt, wk, wt2, wk2 = q.pop(0)
            fill()
            for sub in range(2):
                m = bi * 2 + sub
                pg, pgk = ps_pool.get()
                pu, puk = ps_pool.get()
                for (ps, pk, w_, wk_) in ((pg, pgk, wt, wk), (pu, puk, wt2, wk2)):
                    for kc in range(16):
                        S.op('pe', lambda: E['pe'].matmul(ps[:], lhsT=w_[:, kc, sub * 128:(sub + 1) * 128],
                                                          rhs=hT[:, kc, :], start=(kc == 0), stop=(kc == 15)),
                             reads=[wk_, ('hT', kc)], writes=[pk], inc=(kc == 15))
                sg, sgk = st['f32'].get()
                S.op('act', lambda: E['act'].activation(out=sg[:], in_=pg[:], func=AF.Silu), reads=[pgk], writes=[sgk])
                S.op('dve', lambda: E['dve'].tensor_tensor(out=aT[:, m, :], in0=sg[:], in1=pu[:], op=ALU.mult),
                     reads=[sgk, puk], writes=[('aT', m)])
        gemm_blocks(st, wd, l, 16, lambda kc: (aT[:, kc, :], ('aT', kc)), 44, resid_epi(st, t, 0.5), st['wB'])

    def gemm_scope():
        stack = ExitStack()
        st = {'stack': stack}
        st['hT'] = sb("hT", [128, 16, TT], BF16, stack)
        st['aT'] = sb("aT", [128, 44, TT], BF16, stack)
        st['wA'] = Pool([sb("wA%d" % i, [128, 16, 256], BF16, stack) for i in range(4)], "wA")
        st['wB'] = Pool([sb("wB%d" % i, [128, 44, 128], BF16, stack) for i in range(3)], "wB")
        st['xs'] = Pool([sb("xs%d" % i, [128, TT], F32, stack) for i in range(3)], "xs")
        st['xn16'] = sb("xn16", [128, 16, TT], F32, stack)
        st['sq'] = Pool([sb("sq%d" % i, [128, TT], BF16, stack) for i in range(3)], "sq")
        st['f32'] = Pool([sb("f32_%d" % i, [128, TT], F32, stack) for i in range(6)], "f32")
        st['ob'] = Pool([sb("ob%d" % i, [128, TT], BF16, stack) for i in range(5)], "ob")
        st['rstd'] = sb("rstd", [128, TT], F32, stack)
        st['rot'] = Pool([sb("rot%d" % i, [128, 2, TT], F32, stack) for i in range(2)], "rot")
        st['g3'] = Pool([sb("g3_%d" % i, [128, 3, TT], BF16, stack) for i in range(2)], "g3")
        return st

    def io_scope():
        stack = ExitStack()
        st = {'stack': stack}
        st['xn16'] = sb("ixn16", [128, 16, TT], F32, stack)
        st['sq'] = Pool([sb("isq%d" % i, [128, TT], BF16, stack) for i in range(2)], "sq")
        st['rstd'] = sb("irstd", [128, TT], F32, stack)
        st['xrow'] = sb("xrow", [128, D], F32, stack)
        st['stg'] = sb("stg", [128, 16, TT], F32, stack)
        return st

    def phase_in(st):
        stg = st['stg']
        xr = st['xrow']
        for t in range(NT):
            for s4 in range(4):
                S.dma('sp', xr[:], x_in[t * TT + s4 * 128:t * TT + (s4 + 1) * 128, :], writes=['xrow'])
                for mg in range(4):
                    ps, pk = ps_pool.get()
                    for j in range(4):
                        m = mg * 4 + j
                        S.op('pe', lambda: E['pe'].transpose(ps[:, j * 128:(j + 1) * 128], xr[:, m * 128:(m + 1) * 128], ident_f[:]),
                             reads=['xrow', 'identf'], writes=[pk], inc=(j == 3))
                    S.op('act', lambda: E['act'].activation(
                        out=stg[:, mg * 4:(mg + 1) * 4, s4 * 128:(s4 + 1) * 128],
                        in_=ps[:].rearrange("p (j c) -> p j c", j=4), func=AF.Copy),
                        reads=[pk], writes=['stg'])
            S.dma('pool', xTv[:, :, t * TT:(t + 1) * TT], stg[:], reads=['stg'])
        S.barrier()

    def phase_inproj(st, l, t):
        norm_to_hT(st, t, c_norm(1, l))
        hT = st['hT']
        tsl = slice(t * TT, (t + 1) * TT)

        def store(dst, m_local, ob, obk):
            S.dma('pool', dst[m_local * 128:(m_local + 1) * 128, tsl], ob[:], reads=[obk])

        def epi(bi, pss):
            (p0, k0), (p1, k1) = pss
            if bi < 8:
                isq = bi < 4
                h = bi if isq else bi - 4
                tab = rotq_in if isq else rotk_in
                dst = qrT if isq else krT
                rt, rtk = st['rot'].get()
                S.dma('sp', rt[:], tab[h, :, :, tsl].rearrange("c p t -> p c t"), writes=[rtk])
                t1, t1k = st['f32'].get()
                t2, t2k = st['f32'].get()
                S.op('act', lambda: E['act'].activation(out=t1[:], in_=p0[:], func=AF.Copy), reads=[k0], writes=[t1k])
                S.op('act', lambda: E['act'].activation(out=t2[:], in_=p1[:], func=AF.Copy), reads=[k1], writes=[t2k])
                a, ak = st['f32'].get()
                b, bk = st['f32'].get()
                o1, o1k = st['ob'].get()
                o2, o2k = st['ob'].get()
                S.op('pool', lambda: E['pool'].tensor_tensor(out=a[:], in0=t1[:], in1=rt[:, 0, :], op=ALU.mult), reads=[t1k, rtk], writes=[ak])
                S.op('pool', lambda: E['pool'].tensor_tensor(out=b[:], in0=t2[:], in1=rt[:, 1, :], op=ALU.mult), reads=[t2k, rtk], writes=[bk])
                S.op('pool', lambda: E['pool'].tensor_tensor(out=o1[:], in0=a[:], in1=b[:], op=ALU.subtract), reads=[ak, bk], writes=[o1k])
                S.op('dve', lambda: E['dve'].tensor_tensor(out=t1[:], in0=t1[:], in1=rt[:, 1, :], op=ALU.mult), reads=[t1k, rtk, ak], writes=[t1k])
                S.op('dve', lambda: E['dve'].tensor_tensor(out=t2[:], in0=t2[:], in1=rt[:, 0, :], op=ALU.mult), reads=[t2k, rtk, bk], writes=[t2k])
                S.op('dve', lambda: E['dve'].tensor_tensor(out=o2[:], in0=t1[:], in1=t2[:], op=ALU.add), reads=[t1k, t2k], writes=[o2k])
                store(dst, 2 * h, o1, o1k)
                store(dst, 2 * h + 1, o2, o2k)
                return
            for sub, (ps, pk) in enumerate(pss):
                m = bi * 2 + sub
                if m < 32:
                    ob, obk = st['ob'].get()
                    if sub == 0:
                        S.op('act', lambda: E['act'].activation(out=ob[:], in_=ps[:], func=AF.Copy), reads=[pk], writes=[obk])
                    else:
                        S.op('dve', lambda: E['dve'].tensor_copy(out=ob[:], in_=ps[:]), reads=[pk], writes=[obk])
                    store(rvT, m - 16, ob, obk)
                elif m < 48:
                    ob, obk = st['ob'].get()
                    S.op('act', lambda: E['act'].activation(out=ob[:], in_=ps[:], func=AF.Silu), reads=[pk], writes=[obk])
                    store(rgT, m - 32, ob, obk)
                elif m < 64:
                    xs_, xk_ = st['f32'].get()
                    u, uk = st['f32'].get()
                    S.op('act', lambda: E['act'].activation(out=xs_[:], in_=ps[:], func=AF.Copy), reads=[pk], writes=[xk_])
                    S.op('act', lambda: E['act'].activation(out=u[:], in_=ps[:], func=AF.Square), reads=[pk], writes=[uk])
                    S.op('dve', lambda: E['dve'].tensor_scalar(out=u[:], in0=u[:], scalar1=0.044715, scalar2=1.0, op0=ALU.mult, op1=ALU.add),
                         reads=[uk], writes=[uk])
                    S.op('dve', lambda: E['dve'].tensor_tensor(out=u[:], in0=u[:], in1=xs_[:], op=ALU.mult), reads=[uk, xk_], writes=[uk])
                    S.op('act', lambda: E['act'].activation(out=u[:], in_=u[:], func=AF.Sigmoid, scale=1.5957691216057308),
                         reads=[uk], writes=[uk])
                    if m < 56:
                        ob, obk = st['ob'].get()
                        S.op('dve', lambda: E['dve'].tensor_tensor(out=ob[:], in0=u[:], in1=xs_[:], op=ALU.mult), reads=[uk, xk_], writes=[obk])
                        store(suT, m - 48, ob, obk)
                    else:
                        S.op('dve', lambda: E['dve'].tensor_tensor(out=u[:], in0=u[:], in1=xs_[:], op=ALU.mult), reads=[uk, xk_], writes=[uk])
                        store(svT, m - 56, u, uk)
                elif m < 88:
                    ob, obk = st['ob'].get()
                    sc = (128 ** -0.5) if m < 72 else 1.0
                    dst = aqT if m < 72 else (akT if m < 80 else avT)
                    mb = 64 if m < 72 else (72 if m < 80 else 80)
                    S.op('act', lambda: E['act'].activation(out=ob[:], in_=ps[:], func=AF.Copy, scale=sc), reads=[pk], writes=[obk])
                    store(dst, m - mb, ob, obk)
                else:
                    ob, obk = st['ob'].get()
                    S.op('act', lambda: E['act'].activation(out=ob[:], in_=ps[:], func=AF.Sigmoid, bias=colv(C_BG + l * 48 + (m - 88)), scale=1.0),
                         reads=[pk, 'cols'], writes=[obk])
                    store(gT, m - 88, ob, obk)
        gemm_blocks(st, "win", l, 68, lambda kc: (hT[:, kc, :], ('hT', kc)), 16, epi, st['wA'])

    def phase_proj(st, l, t):
        tsl = slice(t * TT, (t + 1) * TT)
        rt_ = st['hT']
        aT = st['aT']
        sgt = aT[:, 0:8, :]
        att = aT[:, 8:16, :]
        mg = aT[:, 16:32, :]
        S.dma('sp', rt_[:], retoT.rearrange("(m p) t -> p m t", p=128)[:, :, tsl], writes=[('hT', k) for k in range(16)])
        S.dma('sp', sgt, sgoT.rearrange("(m p) t -> p m t", p=128)[:, :, tsl], writes=[('aT', k) for k in range(8)])
        S.dma('sp', att, attoT.rearrange("(m p) t -> p m t", p=128)[:, :, tsl], writes=[('aT', 8 + k) for k in range(8)])
        gTv = gT.rearrange("(b m p) t -> p b m t", b=3, p=128)
        wbr, _, _ = WB["wpr"]
        wbs, _, _ = WB["wps"]
        wba, _, _ = WB["wpa"]
        def loadp(bi):
            w1, w1k = st['wA'].get()
            S.dma('sp', w1[:], wbr[l][bi], writes=[w1k])
            w2, w2k = st['wA'].get()
            S.dma('sp', w2[:, 0:8, :], wbs[l][bi], writes=[w2k])
            S.dma('sp', w2[:, 8:16, :], wba[l][bi], writes=[w2k])
            return (w1, w1k, w2, w2k)
        pq = [loadp(0)]
        for bi in range(8):
            if bi + 1 < 8:
                pq.append(loadp(bi + 1))
            w1, w1k, w2, w2k = pq.pop(0)
            for sub in range(2):
                m = bi * 2 + sub
                pa, pak = ps_pool.get()
                pb, pbk = ps_pool.get()
                pc, pck = ps_pool.get()
                for kc in range(16):
                    S.op('pe', lambda: E['pe'].matmul(pa[:], lhsT=w1[:, kc, sub * 128:(sub + 1) * 128], rhs=rt_[:, kc, :],
                                                      start=(kc == 0), stop=(kc == 15)),
                         reads=[w1k, ('hT', kc)], writes=[pak], inc=(kc == 15))
                for kc in range(8):
                    S.op('pe', lambda: E['pe'].matmul(pb[:], lhsT=w2[:, kc, sub * 128:(sub + 1) * 128], rhs=sgt[:, kc, :],
                                                      start=(kc == 0), stop=(kc == 7)),
                         reads=[w2k, ('aT', kc)], writes=[pbk], inc=(kc == 7))
                for kc in range(8):
                    S.op('pe', lambda: E['pe'].matmul(pc[:], lhsT=w2[:, 8 + kc, sub * 128:(sub + 1) * 128], rhs=att[:, kc, :],
                                                      start=(kc == 0), stop=(kc == 7)),
                         reads=[w2k, ('aT', 8 + kc)], writes=[pck], inc=(kc == 7))
                g3, g3k = st['g3'].get()
                S.dma('sp', g3[:], gTv[:, :, m, tsl], writes=[g3k])
                t1, t1k = st['f32'].get()
                t2, t2k = st['f32'].get()
                t3, t3k = st['f32'].get()
                S.op('dve', lambda: E['dve'].tensor_tensor(out=t1[:], in0=pa[:], in1=g3[:, 0, :], op=ALU.mult), reads=[pak, g3k], writes=[t1k])
                S.op('dve', lambda: E['dve'].tensor_tensor(out=t2[:], in0=pb[:], in1=g3[:, 1, :], op=ALU.mult), reads=[pbk, g3k], writes=[t2k])
                S.op('dve', lambda: E['dve'].tensor_tensor(out=t3[:], in0=pc[:], in1=g3[:, 2, :], op=ALU.mult), reads=[pck, g3k], writes=[t3k])
                S.op('pool', lambda: E['pool'].tensor_tensor(out=t1[:], in0=t1[:], in1=t2[:], op=ALU.add), reads=[t1k, t2k], writes=[t1k])
                S.op('pool', lambda: E['pool'].tensor_tensor(out=mg[:, m, :], in0=t1[:], in1=t3[:], op=ALU.add), reads=[t1k, t3k], writes=[('aT', 16 + m)])
        gemm_blocks(st, "wo", l, 8, lambda kc: (mg[:, kc, :], ('aT', 16 + kc)), 16, resid_epi(st, t, 1.0), st['wA'])

    def phase_out(st):
        stg = st['stg']
        orow = st['xrow']
        for t in range(NT):
            def o(m, xs, xk, rs):
                S.op('dve', lambda: E['dve'].scalar_tensor_tensor(out=stg[:, m, :], in0=xs, scalar=colv(C_FINAL + m),
                                                                  in1=rs[:], op0=ALU.mult, op1=ALU.mult),
                     reads=[xk, 'rstd', 'cols'], writes=[('stg', m)])
            rms_norm(st, t, C_FINAL, o)
            for s4 in range(4):
                for mg_ in range(4):
                    ps, pk = ps_pool.get()
                    for j in range(4):
                        m = mg_ * 4 + j
                        S.op('pe', lambda: E['pe'].transpose(ps[:, j * 128:(j + 1) * 128], stg[:, m, s4 * 128:(s4 + 1) * 128], ident_f[:]),
                             reads=[('stg', m), 'identf'], writes=[pk], inc=(j == 3))
                    S.op('act', lambda: E['act'].activation(out=orow[:, mg_ * 512:(mg_ + 1) * 512], in_=ps[:], func=AF.Copy),
                         reads=[pk], writes=['xrow'])
                S.dma('pool', y_out[t * TT + s4 * 128:t * TT + (s4 + 1) * 128, :], orow[:], reads=['xrow'])
            S.barrier()

    def phase_ret(l):
        stack = ExitStack()
        SC = 256
        qv = qrT.rearrange("(m p) t -> p m t", p=128)
        kv = krT.rearrange("(m p) t -> p m t", p=128)
        vv = rvT.rearrange("(m p) t -> p m t", p=128)
        gv = rgT.rearrange("(m p) t -> p m t", p=128)
        ov = retoT.rearrange("(m p) t -> p m t", p=128)
        qs = Pool([sb("rq%d" % i, [128, 8, SC], BF16, stack) for i in range(2)], "rq")
        ks = Pool([sb("rk%d" % i, [128, 8, SC], BF16, stack) for i in range(2)], "rk")
        vs = Pool([sb("rv%d" % i, [128, 16, SC], BF16, stack) for i in range(2)], "rv")
        gs = Pool([sb("rg%d" % i, [128, 16, SC], BF16, stack) for i in range(2)], "rg")
        os_ = Pool([sb("ro%d" % i, [128, 16, SC], BF16, stack) for i in range(2)], "ro")
        R32 = [sb("R32_%d" % h, [128, 2, 512], F32, stack) for h in range(4)]
        Rbf = [sb("Rbf_%d" % h, [128, 2, 512], BF16, stack) for h in range(4)]
        kh = Pool([sb("kh%d" % i, [128, 256], BF16, stack) for i in range(4)], "kh")
        vt = Pool([sb("vt%d" % i, [128, 512], BF16, stack) for i in range(4)], "vt")
        At = Pool([sb("At%d" % i, [128, 128], BF16, stack) for i in range(4)], "At")
        osb = Pool([sb("osb%d" % i, [128, 512], F32, stack) for i in range(4)], "osb")
        sqb = Pool([sb("sqb%d" % i, [128, 512], BF16, stack) for i in range(4)], "sqb")
        rin = Pool([sb("rin%d" % i, [128, 128], F32, stack) for i in range(4)], "rin")
        for h in range(4):
            S.op('pool', lambda: E['pool'].memset(R32[h][:], 0.0), writes=[('R32', h)])
            S.op('pool', lambda: E['pool'].memset(Rbf[h][:], 0.0), writes=[('Rbf', h)])
        bA = [(PSB[0], ('psb', 0)), (PSB[1], ('psb', 1))]
        bB = [(PSB[2], ('psb', 2)), (PSB[3], ('psb', 3))]
        bC = [(PSB[4], ('psb', 4)), (PSB[5], ('psb', 5))]
        bU = [(PSB[6], ('psb', 6)), (PSB[7], ('psb', 7))]
        for sc in range(T // SC):
            tsl = slice(sc * SC, (sc + 1) * SC)
            q_, qk = qs.get()
            k_, kk = ks.get()
            v_, vk = vs.get()
            g_, gk = gs.get()
            o_, ok = os_.get()
            S.dma('sp', q_[:], qv[:, :, tsl], writes=[qk])
            S.dma('sp', k_[:], kv[:, :, tsl], writes=[kk])
            S.dma('sp', v_[:], vv[:, :, tsl], writes=[vk])
            S.dma('sp', g_[:], gv[:, :, tsl], writes=[gk])
            for cc in range(SC // 128):
                csl = slice(cc * 128, (cc + 1) * 128)
                for hp in range(2):
                    U_ = []
                    for ui, h in enumerate((2 * hp, 2 * hp + 1)):
                        c = {'h': h}
                        c['gC'] = float(np.float32(np.exp(np.float32(128.0) * np.log1p(np.float32(-(2.0 ** (-5.0 - h)))))))
                        c['pA'], c['pAk'] = bA[ui]
                        c['pB'], c['pBk'] = bB[ui]
                        c['pC'], c['pCk'] = bC[ui]
                        c['pU'], c['pUk'] = bU[ui]
                        c['pAb'] = c['pA'][:].bitcast(BF16)
                        U_.append(c)
                    for c in U_:
                        h = c['h']
                        pAb, pAk, pB, pBk = c['pAb'], c['pAk'], c['pB'], c['pBk']
                        for dch in range(2):
                            S.op('pe', lambda: E['pe'].transpose(pAb[:, dch * 128:(dch + 1) * 128], k_[:, 2 * h + dch, csl], ident_b[:]),
                                 reads=[kk, 'identb'], writes=[pAk], inc=False)
                        for ech in range(4):
                            S.op('pe', lambda: E['pe'].transpose(pAb[:, 256 + ech * 128:256 + (ech + 1) * 128], v_[:, 4 * h + ech, csl], ident_b[:]),
                                 reads=[vk, 'identb'], writes=[pAk], inc=(ech == 3))
                        for dch in range(2):
                            S.op('pe', lambda: E['pe'].matmul(pB[:, 0:128], lhsT=k_[:, 2 * h + dch, csl], rhs=q_[:, 2 * h + dch, csl],
                                                              start=(dch == 0), stop=(dch == 1)),
                                 reads=[kk, qk], writes=[pBk], inc=(dch == 1))
                    for c in U_:
                        pAb, pAk, pB, pBk = c['pAb'], c['pAk'], c['pB'], c['pBk']
                        c['kh'], c['khk'] = kh.get()
                        c['vt'], c['vtk'] = vt.get()
                        c['A'], c['Ak'] = At.get()
                        kh_, vt_, A_ = c['kh'], c['vt'], c['A']
                        gC = c['gC']
                        S.op('act', lambda: E['act'].activation(out=kh_[:], in_=pAb[:, 0:256], func=AF.Copy, scale=gC), reads=[pAk], writes=[c['khk']])
                        S.op('act', lambda: E['act'].activation(out=vt_[:], in_=pAb[:, 256:768], func=AF.Copy), reads=[pAk], writes=[c['vtk']])
                        S.op('dve', lambda: E['dve'].tensor_tensor(out=A_[:], in0=pB[:, 0:128], in1=cmask[:], op=ALU.mult),
                             reads=[pBk, 'cmask'], writes=[c['Ak']])
                    for c in U_:
                        h = c['h']
                        pC, pCk, vt_, A_, kh_ = c['pC'], c['pCk'], c['vt'], c['A'], c['kh']
                        for ech in range(4):
                            oc = pC[:, ech * 128:(ech + 1) * 128]
                            S.op('pe', lambda: E['pe'].matmul(oc, lhsT=vt_[:, ech * 128:(ech + 1) * 128], rhs=A_[:], start=True, stop=False),
                                 reads=[c['vtk'], c['Ak']], writes=[pCk], inc=False)
                            for dch in range(2):
                                S.op('pe', lambda: E['pe'].matmul(oc, lhsT=Rbf[h][:, dch, ech * 128:(ech + 1) * 128], rhs=q_[:, 2 * h + dch, csl],
                                                                  start=False, stop=(dch == 1)),
                                     reads=[('Rbf', h), qk], writes=[pCk], inc=(dch == 1 and ech == 3))
                    for c in U_:
                        pC, pCk = c['pC'], c['pCk']
                        c['ob'], c['obk'] = osb.get()
                        c['sq'], c['sqk'] = sqb.get()
                        ob_, sq_ = c['ob'], c['sq']
                        S.op('act', lambda: E['act'].activation(out=ob_[:], in_=pC[:], func=AF.Copy), reads=[pCk], writes=[c['obk']])
                        S.op('act', lambda: E['act'].activation(out=sq_[:], in_=pC[:], func=AF.Square), reads=[pCk], writes=[c['sqk']])
                    for c in U_:
                        pB, pBk, sq_ = c['pB'], c['pBk'], c['sq']
                        for ech in range(4):
                            S.op('pe', lambda: E['pe'].matmul(pB[:, 128:256], lhsT=ones_b[:], rhs=sq_[:, ech * 128:(ech + 1) * 128],
                                                              start=(ech == 0), stop=(ech == 3)),
                                 reads=['onesb', c['sqk']], writes=[pBk], inc=(ech == 3))
                    for dch in range(2):
                        for c in U_:
                            h = c['h']
                            pU, pUk, kh_, vt_ = c['pU'], c['pUk'], c['kh'], c['vt']
                            gC = c['gC']
                            S.op('pe', lambda: E['pe'].matmul(pU[:], lhsT=kh_[:, dch * 128:(dch + 1) * 128], rhs=vt_[:], start=True, stop=True),
                                 reads=[c['khk'], c['vtk']], writes=[pUk], inc=True)
                            S.op('dve', lambda: E['dve'].scalar_tensor_tensor(out=R32[h][:, dch, :], in0=R32[h][:, dch, :], scalar=gC,
                                                                              in1=pU[:], op0=ALU.mult, op1=ALU.add),
                                 reads=[pUk, ('R32', h)], writes=[('R32', h)])
                    for c in U_:
                        h = c['h']
                        pB, pBk, ob_ = c['pB'], c['pBk'], c['ob']
                        ri, rik = rin.get()
                        S.op('act', lambda: E['act'].activation(out=Rbf[h][:], in_=R32[h][:], func=AF.Copy),
                             reads=[('R32', h)], writes=[('Rbf', h)])
                        S.op('act', lambda: E['act'].activation(out=ri[:], in_=pB[:, 128:256], func=AF.Sqrt, bias=epsc[:], scale=1.0 / 512),
                             reads=[pBk, 'epsc'], writes=[rik])
                        S.op('dve', lambda: E['dve'].reciprocal(out=ri[:], in_=ri[:]), reads=[rik], writes=[rik])
                        S.op('pool', lambda: E['pool'].tensor_tensor(out=ob_[:].rearrange("p (e i) -> p e i", e=4),
                                                                     in0=ob_[:].rearrange("p (e i) -> p e i", e=4),
                                                                     in1=ri[:].unsqueeze(1).broadcast_to([128, 4, 128]), op=ALU.mult),
                             reads=[c['obk'], rik], writes=[c['obk']])
                        S.op('pool', lambda: E['pool'].tensor_tensor(out=o_[:, 4 * h:4 * h + 4, csl],
                                                                     in0=ob_[:].rearrange("p (e i) -> p e i", e=4),
                                                                     in1=g_[:, 4 * h:4 * h + 4, csl], op=ALU.mult),
                             reads=[c['obk'], gk], writes=[ok])
            S.dma('pool', ov[:, :, tsl], o_[:], reads=[ok])
        S.barrier()
        stack.close()

    def phase_sgu(l):
        stack = ExitStack()
        SC = 256
        uv = suT.rearrange("(m p) t -> p m t", p=128)
        vv = svT.rearrange("(m p) t -> p m t", p=128)
        ov = sgoT.rearrange("(m p) t -> p m t", p=128)
        us = Pool([sb("su%d" % i, [128, 8, SC], BF16, stack) for i in range(2)], "su")
        vs = Pool([sb("sv%d" % i, [128, 8, SC], F32, stack) for i in range(2)], "sv")
        os_ = Pool([sb("so%d" % i, [128, 8, SC], BF16, stack) for i in range(2)], "so")
        lng = sb("lng", [128, 1024], F32, stack)
        lnb = sb("lnb", [128, 1024], F32, stack)
        wcf = sb("wcf", [128, 4, 128], F32, stack)
        wcb = sb("wcb", [128, 4, 128], BF16, stack)
        sgb = sb("sgb", [128, 4, 128], F32, stack)
        xn = Pool([sb("xn%d" % i, [128, 1024], F32, stack) for i in range(2)], "xn")
        vb = Pool([sb("vb%d" % i, [128, 1024], BF16, stack) for i in range(2)], "vb")
        stt = Pool([sb("stt%d" % i, [128, 16], F32, stack) for i in range(2)], "stt")
        tm = Pool([sb("tm%d" % i, [128, 8, 128], F32, stack) for i in range(2)], "tm")
        S.dma('sp', lng[:], lng_in[l].partition_broadcast(128), writes=['lng'])
        S.dma('sp', lnb[:], lnb_in[l].partition_broadcast(128), writes=['lnb'])
        S.dma('sp', wcf[:], sgwT_in[l].rearrange("g s t -> s g t"), writes=['wcf'])
        S.dma('sp', sgb[:], sgb_in[l].partition_broadcast(128), writes=['sgb'])
        S.op('dve', lambda: E['dve'].tensor_tensor(out=wcb[:], in0=wcf[:], in1=cmask[:].unsqueeze(1).broadcast_to([128, 4, 128]), op=ALU.mult),
             reads=['wcf', 'cmask'], writes=['wcb'])
        for sc in range(T // SC):
            tsl = slice(sc * SC, (sc + 1) * SC)
            u_, uk = us.get()
            v_, vk = vs.get()
            o_, ok = os_.get()
            S.dma('sp', u_[:], uv[:, :, tsl], writes=[uk])
            S.dma('sp', v_[:], vv[:, :, tsl], writes=[vk])
            U_ = []
            for cc in range(SC // 128):
                c = {'csl': slice(cc * 128, (cc + 1) * 128)}
                b0 = (cc % 2) * 4
                c['p'] = [(PSB[b0 + i], ('psb', b0 + i)) for i in range(4)]
                U_.append(c)
            for c in U_:
                (p0, p0k), (p1, p1k) = c['p'][0], c['p'][1]
                for f in range(8):
                    pp, ppk = (p0, p0k) if f < 4 else (p1, p1k)
                    S.op('pe', lambda: E['pe'].transpose(pp[:, (f % 4) * 128:(f % 4 + 1) * 128], v_[:, f, c['csl']], ident_f[:]),
                         reads=[vk, 'identf'], writes=[ppk], inc=(f % 4 == 3))
            for c in U_:
                (p0, p0k), (p1, p1k) = c['p'][0], c['p'][1]
                c['st'], c['stk'] = stt.get()
                st_, stk = c['st'], c['stk']
                S.op('dve', lambda: E['dve'].bn_stats(out=st_[:, 0:6], in_=p0[:]), reads=[p0k], writes=[stk])
                S.op('dve', lambda: E['dve'].bn_stats(out=st_[:, 6:12], in_=p1[:]), reads=[p1k, stk], writes=[stk])
                S.op('dve', lambda: E['dve'].bn_aggr(out=st_[:, 12:14], in_=st_[:, 0:12]), reads=[stk], writes=[stk])
                S.op('act', lambda: E['act'].activation(out=st_[:, 14:15], in_=st_[:, 13:14], func=AF.Sqrt, bias=epsc[:], scale=1.0),
                     reads=[stk, 'epsc'], writes=[stk])
            for c in U_:
                (p0, p0k), (p1, p1k) = c['p'][0], c['p'][1]
                st_, stk = c['st'], c['stk']
                S.op('dve', lambda: E['dve'].reciprocal(out=st_[:, 15:16], in_=st_[:, 14:15]), reads=[stk], writes=[stk])
                c['xn'], c['xnk'] = xn.get()
                xn_, xnk = c['xn'], c['xnk']
                for half, (pp, ppk) in enumerate(((p0, p0k), (p1, p1k))):
                    S.op('dve', lambda: E['dve'].tensor_scalar(out=xn_[:, half * 512:(half + 1) * 512], in0=pp[:], scalar1=st_[:, 12:13],
                                                               scalar2=st_[:, 15:16], op0=ALU.subtract, op1=ALU.mult),
                         reads=[ppk, stk], writes=[xnk])
            for c in U_:
                xn_, xnk = c['xn'], c['xnk']
                c['vb'], c['vbk'] = vb.get()
                vb_, vbk = c['vb'], c['vbk']
                S.op('pool', lambda: E['pool'].tensor_tensor(out=xn_[:], in0=xn_[:], in1=lng[:], op=ALU.mult), reads=[xnk, 'lng'], writes=[xnk])
                S.op('pool', lambda: E['pool'].tensor_tensor(out=vb_[:], in0=xn_[:], in1=lnb[:], op=ALU.add), reads=[xnk, 'lnb'], writes=[vbk])
            for c in U_:
                (p2, p2k), (p3, p3k) = c['p'][2], c['p'][3]
                vb_, vbk = c['vb'], c['vbk']
                for f in range(8):
                    pp, ppk = (p2, p2k) if f < 4 else (p3, p3k)
                    S.op('pe', lambda: E['pe'].matmul(pp[:, (f % 4) * 128:(f % 4 + 1) * 128], lhsT=vb_[:, f * 128:(f + 1) * 128], rhs=wcb[:, f // 2, :],
                                                      start=True, stop=True),
                         reads=[vbk, 'wcb'], writes=[ppk], inc=(f % 4 == 3))
            for c in U_:
                (p2, p2k), (p3, p3k) = c['p'][2], c['p'][3]
                c['tm'], c['tmk'] = tm.get()
                tm_, tmk = c['tm'], c['tmk']
                for half, (pp, ppk) in enumerate(((p2, p2k), (p3, p3k))):
                    S.op('dve', lambda: E['dve'].tensor_tensor(
                        out=tm_[:, half * 4:(half + 1) * 4, :].rearrange("p (g two) t -> p g two t", two=2),
                        in0=pp[:].rearrange("p (g two t) -> p g two t", g=2, two=2),
                        in1=sgb[:, 2 * half:2 * half + 2, :].unsqueeze(2).broadcast_to([128, 2, 2, 128]), op=ALU.add),
                        reads=[ppk, 'sgb'], writes=[tmk])
            for c in U_:
                tm_, tmk = c['tm'], c['tmk']
                S.op('pool', lambda: E['pool'].tensor_tensor(out=o_[:, :, c['csl']], in0=tm_[:], in1=u_[:, :, c['csl']], op=ALU.mult),
                     reads=[tmk, uk], writes=[ok])
            S.dma('pool', ov[:, :, tsl], o_[:], reads=[ok])
        S.barrier()
        stack.close()

    def phase_att(l):
        stack = ExitStack()
        WIN = 2048
        NW = T // WIN
        qv = aqT.rearrange("(m p) t -> p m t", p=128)
        kv = akT.rearrange("(m p) t -> p m t", p=128)
        vv = avT.rearrange("(m p) t -> p m t", p=128)
        ov = attoT.rearrange("(m p) t -> p m t", p=128)
        Em = sb("Em", [128, 48, 128], F32, stack)
        qb = Pool([sb("aq%d" % i, [128, WIN], BF16, stack) for i in range(2)], "aq")
        kb = Pool([sb("ak%d" % i, [128, 2 * WIN], BF16, stack) for i in range(2)], "ak")
        vbf = Pool([sb("av%d" % i, [128, 2 * WIN], BF16, stack) for i in range(2)], "av")
        acc = Pool([sb("acc%d" % i, [128, 2, WIN], F32, stack) for i in range(2)], "acc")
        oo = Pool([sb("ao%d" % i, [128, WIN], BF16, stack) for i in range(2)], "ao")
        ex = Pool([sb("ex%d" % i, [128, 2, 2, 128], F32, stack) for i in range(4)], "ex")
        Pb = Pool([sb("Pb%d" % i, [128, 2, 2, 128], BF16, stack) for i in range(4)], "Pb")
        vtk_ = Pool([sb("avt%d" % i, [128, 2, 2, 128], BF16, stack) for i in range(4)], "avt")
        S.dma('sp', Em[:], bm_in.rearrange("c h k j i -> j (c h k) i"), writes=['Em'])
        S.op('act', lambda: E['act'].activation(out=Em[:], in_=Em[:], func=AF.Exp), reads=['Em'], writes=['Em'])
        u = 0
        for h in range(8):
            for w in range(NW):
                q_, qk = qb.get()
                k_, kk = kb.get()
                v_, vk = vbf.get()
                a_, ak_ = acc.get()
                o_, ok = oo.get()
                w0 = w * WIN
                S.dma('sp', q_[:], qv[:, h, w0:w0 + WIN], writes=[qk])
                if w > 0:
                    S.dma('sp', k_[:], kv[:, h, w0 - WIN:w0 + WIN], writes=[kk])
                    S.dma('sp', v_[:], vv[:, h, w0 - WIN:w0 + WIN], writes=[vk])
                else:
                    S.dma('sp', k_[:, WIN:], kv[:, h, 0:WIN], writes=[kk])
                    S.dma('sp', v_[:, WIN:], vv[:, h, 0:WIN], writes=[vk])
                batches = []
                for ci, dil in enumerate((1, 4, 16)):
                    span = 128 * dil
                    ulist = [(nb_, rho) for nb_ in range(WIN // span) for rho in range(dil)]
                    for i2 in range(0, len(ulist), 2):
                        pair = ulist[i2:i2 + 2]
                        hp = [(w0 + nb_ * span) > 0 for (nb_, rho) in pair]
                        if hp[0] == hp[1]:
                            batches.append((ci, dil, span, pair, hp[0]))
                        else:
                            batches.append((ci, dil, span, pair[0:1], hp[0]))
                            batches.append((ci, dil, span, pair[1:2], hp[1]))
                for g0 in range(0, len(batches), 2):
                    U_ = []
                    for (ci, dil, span, pair, has_prev) in batches[g0:g0 + 2]:
                        c = {'ci': ci, 'nu': len(pair), 'dil': dil, 'span': span, 'pair': pair}
                        bs = (u % 2) * 3
                        u += 1
                        c['pS'], c['pSk'] = PSB[bs], ('psb', bs)
                        c['pV'], c['pVk'] = PSB[bs + 1], ('psb', bs + 1)
                        c['pO'], c['pOk'] = PSB[bs + 2], ('psb', bs + 2)
                        c['pVb'] = c['pV'][:].bitcast(BF16)
                        c['p_lo'] = 0 if has_prev else 1
                        c['e0'] = (ci * 8 + h) * 2
                        c['units'] = []
                        for (nb_, rho) in pair:
                            base = nb_ * span + rho
                            qsl = slice(base, base + 127 * dil + 1, dil)
                            ksl_c = slice(WIN + base, WIN + base + 127 * dil + 1, dil)
                            ksl_p = slice(WIN + base - span, WIN + base - span + 127 * dil + 1, dil)
                            tiles = ([(0, ksl_p)] if has_prev else []) + [(1, ksl_c)]
                            c['units'].append((qsl, tiles))
                        U_.append(c)
                    for c in U_:
                        pS, pSk, pVb, pVk = c['pS'], c['pSk'], c['pVb'], c['pVk']
                        nun = len(c['units'])
                        for ui, (qsl, tiles) in enumerate(c['units']):
                            for ii, (pc, ksl) in enumerate(tiles):
                                S.op('pe', lambda: E['pe'].matmul(pS[:, ui * 256 + pc * 128:ui * 256 + (pc + 1) * 128], lhsT=k_[:, ksl], rhs=q_[:, qsl], start=True, stop=True),
                                     reads=[kk, qk], writes=[pSk], inc=(ui == nun - 1 and ii == len(tiles) - 1))
                        for ui, (qsl, tiles) in enumerate(c['units']):
                            for ii, (pc, ksl) in enumerate(tiles):
                                S.op('pe', lambda: E['pe'].transpose(pVb[:, ui * 256 + pc * 128:ui * 256 + (pc + 1) * 128], v_[:, ksl], ident_b[:]),
                                     reads=[vk, 'identb'], writes=[pVk], inc=(ui == nun - 1 and ii == len(tiles) - 1))
                    for c in U_:
                        pS, pSk, pVb, pVk, p_lo, nu = c['pS'], c['pSk'], c['pVb'], c['pVk'], c['p_lo'], c['nu']
                        c['ex'], c['exk'] = ex.get()
                        c['vt'], c['vtk'] = vtk_.get()
                        ex_, vt_ = c['ex'], c['vt']
                        S.op('act', lambda: E['act'].activation(out=ex_[:, 0:nu, p_lo:2, :],
                                                                in_=pS[:, 0:nu * 256].rearrange("p (u k i) -> p u k i", u=nu, k=2)[:, :, p_lo:2, :], func=AF.Exp),
                             reads=[pSk], writes=[c['exk']])
                        S.op('act', lambda: E['act'].activation(out=vt_[:, 0:nu, p_lo:2, :],
                                                                in_=pVb[:, 0:nu * 256].rearrange("p (u k i) -> p u k i", u=nu, k=2)[:, :, p_lo:2, :], func=AF.Copy),
                             reads=[pVk], writes=[c['vtk']])
                    for c in U_:
                        p_lo, e0, ex_, nu = c['p_lo'], c['e0'], c['ex'], c['nu']
                        c['P'], c['Pk'] = Pb.get()
                        P_ = c['P']
                        S.op('pool', lambda: E['pool'].tensor_tensor(out=P_[:, 0:nu, p_lo:2, :], in0=ex_[:, 0:nu, p_lo:2, :],
                                                                     in1=Em[:, e0 + p_lo:e0 + 2, :].unsqueeze(1).broadcast_to([128, nu, 2 - p_lo, 128]), op=ALU.mult),
                             reads=[c['exk'], 'Em'], writes=[c['Pk']])
                    for c in U_:
                        pO, pOk, vt_, P_ = c['pO'], c['pOk'], c['vt'], c['P']
                        nun = len(c['units'])
                        for ui, (qsl, tiles) in enumerate(c['units']):
                            for ii, (pc, ksl) in enumerate(tiles):
                                S.op('pe', lambda: E['pe'].matmul(pO[:, ui * 256:ui * 256 + 128], lhsT=vt_[:, ui, pc, :], rhs=P_[:, ui, pc, :], start=(ii == 0), stop=(ii == len(tiles) - 1)),
                                     reads=[c['vtk'], c['Pk']], writes=[pOk], inc=False)
                            for ii, (pc, ksl) in enumerate(tiles):
                                S.op('pe', lambda: E['pe'].matmul(pO[:, ui * 256 + 128:ui * 256 + 256], lhsT=ones_b[:], rhs=P_[:, ui, pc, :], start=(ii == 0), stop=(ii == len(tiles) - 1)),
                                     reads=['onesb', c['Pk']], writes=[pOk], inc=(ui == nun - 1 and ii == len(tiles) - 1))
                    for c in U_:
                        pO, pOk, nu, dil, span = c['pO'], c['pOk'], c['nu'], c['dil'], c['span']
                        nb0, rho0 = c['pair'][0]
                        pOv = pO[:, 0:nu * 256].rearrange("p (u k i) -> p k u i", u=nu, k=2)
                        if dil == 1:
                            dst = a_[:, :, nb0 * 128:(nb0 + nu) * 128].rearrange("p k (u i) -> p k u i", u=nu)
                        else:
                            dst = a_[:, :, nb0 * span:(nb0 + 1) * span].rearrange("p k (i d) -> p k d i", d=dil)[:, :, rho0:rho0 + nu, :]
                        if c['ci'] == 0:
                            S.op('dve', lambda: E['dve'].tensor_copy(out=dst, in_=pOv), reads=[pOk], writes=[ak_])
                        else:
                            S.op('dve', lambda: E['dve'].tensor_tensor(out=dst, in0=dst, in1=pOv, op=ALU.add),
                                 reads=[pOk, ak_], writes=[ak_])
                S.op('dve', lambda: E['dve'].reciprocal(out=a_[:, 1, :], in_=a_[:, 1, :]), reads=[ak_], writes=[ak_])
                S.op('pool', lambda: E['pool'].tensor_tensor(out=o_[:], in0=a_[:, 0, :], in1=a_[:, 1, :], op=ALU.mult), reads=[ak_], writes=[ok])
                S.dma('pool', ov[:, h, w0:w0 + WIN], o_[:], reads=[ok])
        S.barrier()
        stack.close()

    st = io_scope()
    phase_in(st)
    st['stack'].close()
    for l in range(L):
        st = gemm_scope()
        if l + 1 < L:
            filler[0] = cast_jobs(l + 1)
        for t in range(NT):
            ffn(st, l, t, 1)
            phase_inproj(st, l, t)
        drain_fill()
        S.barrier()
        st['stack'].close()
        phase_ret(l)
        phase_sgu(l)
        phase_att(l)
        st = gemm_scope()
        for t in range(NT):
            phase_proj(st, l, t)
            ffn(st, l, t, 2)
        S.barrier()
        st['stack'].close()
    st = io_scope()
    phase_out(st)
    st['stack'].close()
    es.close()
    return nc


def _t5_bucket_np(dist):
    max_exact = 16
    d_f = np.maximum(dist, 1).astype(np.float32)
    large = max_exact + (np.log(d_f / np.float32(max_exact)) / np.float32(math.log(2048 / max_exact))
                         * np.float32(32 - max_exact)).astype(np.int32)
    large = np.minimum(large, 31)
    return np.where(dist < max_exact, dist, large)


def host_consts(T, L, inp):
    c = {}
    colsl = []
    for nm in ("ffn1_norm", "mix_norm", "ffn2_norm"):
        for l in range(L):
            colsl.append(np.asarray(inp[nm][l]).reshape(16, 128).T)
    colsl.append(np.asarray(inp["final_norm"]).reshape(16, 128).T)
    for l in range(L):
        colsl.append(np.asarray(inp["b_gate"][l]).reshape(48, 128).T)
    c["cols"] = np.ascontiguousarray(np.concatenate(colsl, axis=1), dtype=np.float32)
    c["lng"] = np.ascontiguousarray(inp["sg_ln_g"][:L], dtype=np.float32)
    c["lnb"] = np.ascontiguousarray(inp["sg_ln_b"][:L], dtype=np.float32)
    c["sgwT"] = np.ascontiguousarray(np.swapaxes(np.asarray(inp["sg_w"][:L]), 2, 3), dtype=np.float32)
    c["sgb"] = np.ascontiguousarray(inp["sg_b"][:L], dtype=np.float32)
    rb = np.asarray(inp["rel_bias"], dtype=np.float32)
    bm = np.empty((3, 8, 2, 128, 128), np.float32)
    i = np.arange(128)[None, :]
    j = np.arange(128)[:, None]
    for ci, dil in enumerate((1, 4, 16)):
        for pc in range(2):
            steps = (128 + i - j) if pc == 0 else (i - j)
            valid = (steps >= 0) & (steps <= 128)
            bucket = _t5_bucket_np(dil * np.maximum(steps, 0))
            for h in range(8):
                bm[ci, h, pc] = np.where(valid, rb[bucket, h], np.float32(-30000.0))
    c["bm"] = bm
    pos = np.arange(T, dtype=np.float32)
    inv = (np.float32(10000.0) ** (-np.arange(0, 256, 2, dtype=np.float32) / np.float32(256))).astype(np.float32)
    ang = (pos[None, :] * inv[:, None]).astype(np.float32)
    cos, sin = np.cos(ang).astype(np.float32), np.sin(ang).astype(np.float32)
    rotq = np.empty((4, 2, 128, T), np.float32)
    rotk = np.empty((4, 2, 128, T), np.float32)
    pm = (np.arange(T) % 128).astype(np.float64)
    for h in range(4):
        lg = math.log1p(-(2.0 ** (-5.0 - h)))
        dq = np.exp((pm + 1.0) * lg)
        dk = np.exp(-(pm + 1.0) * lg) / 16.0
        rotq[h, 0] = cos * dq[None, :]
        rotq[h, 1] = sin * dq[None, :]
        rotk[h, 0] = cos * dk[None, :]
        rotk[h, 1] = sin * dk[None, :]
    c["rotq"] = rotq
    c["rotk"] = rotk
    c["cmask"] = np.triu(np.ones((128, 128), np.float32))
    c["ident"] = np.eye(128, dtype=np.float32)
    return c


_WMAP = {"wg1": "ffn1_w_gate", "wu1": "ffn1_w_up", "wd1": "ffn1_w_down", "win": "w_in", "wpr": "w_proj_ret",
         "wps": "w_proj_sg", "wpa": "w_proj_att", "wo": "w_out", "wg2": "ffn2_w_gate", "wu2": "ffn2_w_up",
         "wd2": "ffn2_w_down"}


def run(inp, T, L, ncores, nseq, dbg=False, trace=False):
    nc = build(T, L, dbg)
    c = host_consts(T, L, inp)
    base = dict(c)
    for k, v in _WMAP.items():
        base[k] = np.ascontiguousarray(np.asarray(inp[v])[:L], dtype=np.float32)
    in_maps = []
    for i in range(ncores):
        m = dict(base)
        m["x"] = np.ascontiguousarray(np.asarray(inp["x"])[i % nseq, :T], dtype=np.float32)
        in_maps.append(m)
    res = run_bass_kernel_spmd(nc, in_maps, core_ids=list(range(ncores)), **({"trace": True} if trace else {}))
    return res


def kernel(**inputs):
    res = run(inputs, SEQ, LDEPTH, 2, 2)
    out = np.stack([np.asarray(res.results[0]["y"]), np.asarray(res.results[1]["y"])], axis=0)
    return out.astype(np.float32)
```
